# Optimizing a Trainium2 kernel written in Bass

```python
import jax, jax.numpy as jnp
from jax import lax
import numpy as np

D_MODEL = 1024
BATCH = 8
SEQ = 4096
DEPTH = 2

CTX_LEN = 256
GRID_W = 64
Q_BLOCK = 128
ROPE_THETA = 10000.0
EPS = 1e-6

MLA_HEADS = 6
MLA_Q_RANK = 256
MLA_KV_RANK = 128
MLA_NOPE = 64
MLA_ROPE = 32
MLA_V = 64
MLA_WIDTH = MLA_HEADS * MLA_V
GQA_HEADS = 6
GQA_KV_HEADS = 2
GQA_HEAD_DIM = 64
GQA_WIDTH = GQA_HEADS * GQA_HEAD_DIM
CONV_CH = 256
CONV_K = 31
MIX_WIDTH = MLA_WIDTH + GQA_WIDTH + CONV_CH

L_MLA_KV = 0
L_MLA_KR = L_MLA_KV + MLA_KV_RANK
L_GQA_K = L_MLA_KR + MLA_ROPE
L_GQA_V = L_GQA_K + GQA_KV_HEADS * GQA_HEAD_DIM
KV_COLS = L_GQA_V + GQA_KV_HEADS * GQA_HEAD_DIM
OFF_MLA_Q = 0
OFF_KV = OFF_MLA_Q + MLA_Q_RANK
OFF_GQA_Q = OFF_KV + KV_COLS
OFF_CONV = OFF_GQA_Q + GQA_WIDTH
OFF_GATE = OFF_CONV + 2 * CONV_CH
IN_COLS = OFF_GATE + MIX_WIDTH

kernel_name = "hybrid_mla_gqa_conformer_dit"


def rms_norm(x, g):
    xf = x.astype(jnp.float32)
    y = xf * lax.rsqrt(jnp.mean(xf * xf, axis=-1, keepdims=True) + EPS)
    return (y * g.astype(jnp.float32)).astype(x.dtype)


def layer_norm(x, g, b):
    xf = x.astype(jnp.float32)
    mu = jnp.mean(xf, axis=-1, keepdims=True)
    var = jnp.mean(jnp.square(xf - mu), axis=-1, keepdims=True)
    y = (xf - mu) * lax.rsqrt(var + EPS) * g.astype(jnp.float32) + b.astype(jnp.float32)
    return y.astype(x.dtype)


def rope_1d(x, pos):
    half = x.shape[-1] // 2
    freqs = ROPE_THETA ** (-jnp.arange(half, dtype=jnp.float32) / half)
    ang = pos.astype(jnp.float32)[:, None] * freqs[None, :]
    cos = jnp.cos(ang)[None, :, None, :]
    sin = jnp.sin(ang)[None, :, None, :]
    xf = x.astype(jnp.float32)
    x1, x2 = xf[..., :half], xf[..., half:]
    return jnp.concatenate([x1 * cos - x2 * sin, x1 * sin + x2 * cos], axis=-1).astype(x.dtype)


def axial_rope(x, row, col):
    r = x.shape[-1] // 2
    return jnp.concatenate([rope_1d(x[..., :r], row), rope_1d(x[..., r:], col)], axis=-1)


def block_attention(q, k, v):
    B, S, H, D = q.shape
    G = k.shape[2]
    R = H // G
    Dv = v.shape[-1]
    scale = D ** -0.5
    nblk = S // Q_BLOCK
    qb = q.reshape(B, nblk, Q_BLOCK, G, R, D).transpose(1, 0, 2, 3, 4, 5)

    def one_block(qi):
        s = jnp.einsum('bqgrd,bkgd->bgrqk', qi, k).astype(jnp.float32) * scale
        p = jax.nn.softmax(s, axis=-1).astype(v.dtype)
        return jnp.einsum('bgrqk,bkgv->bqgrv', p, v)

    o = lax.map(one_block, qb)
    return o.transpose(1, 0, 2, 3, 4, 5).reshape(B, S, H * Dv)


def mla_query(p, q_norm, w_uq, row, col):
    B, T = p.shape[:2]
    cq = rms_norm(p[..., OFF_MLA_Q:OFF_MLA_Q + MLA_Q_RANK], q_norm)
    q = (cq @ w_uq).reshape(B, T, MLA_HEADS, MLA_NOPE + MLA_ROPE)
    if row is not None:
        q = jnp.concatenate([q[..., :MLA_NOPE], axial_rope(q[..., MLA_NOPE:], row, col)], axis=-1)
    return q


def mla_keys_values(pkv, kv_norm, w_ukv, row, col):
    B, T = pkv.shape[:2]
    ckv = rms_norm(pkv[..., L_MLA_KV:L_MLA_KR], kv_norm)
    kv = (ckv @ w_ukv).reshape(B, T, MLA_HEADS, MLA_NOPE + MLA_V)
    k_nope, v = kv[..., :MLA_NOPE], kv[..., MLA_NOPE:]
    k_rope = pkv[..., L_MLA_KR:L_GQA_K][:, :, None, :]
    if row is not None:
        k_rope = axial_rope(k_rope, row, col)
    k_rope = jnp.broadcast_to(k_rope, (B, T, MLA_HEADS, MLA_ROPE))
    return jnp.concatenate([k_nope, k_rope], axis=-1), v


def gqa_query(p, q_norm, row, col):
    B, T = p.shape[:2]
    q = rms_norm(p[..., OFF_GQA_Q:OFF_CONV].reshape(B, T, GQA_HEADS, GQA_HEAD_DIM), q_norm)
    if row is not None:
        q = axial_rope(q, row, col)
    return q


def gqa_keys_values(pkv, k_norm, row, col):
    B, T = pkv.shape[:2]
    k = rms_norm(pkv[..., L_GQA_K:L_GQA_V].reshape(B, T, GQA_KV_HEADS, GQA_HEAD_DIM), k_norm)
    if row is not None:
        k = axial_rope(k, row, col)
    v = pkv[..., L_GQA_V:KV_COLS].reshape(B, T, GQA_KV_HEADS, GQA_HEAD_DIM)
    return k, v


def conformer_conv(u, dw_w, dw_b, ln_w, ln_b, pw_w, pw_b):
    a, g = jnp.split(u, 2, axis=-1)
    y = a * jax.nn.sigmoid(g)
    y = lax.conv_general_dilated(
        y, dw_w[:, None, :], window_strides=(1,),
        padding=((CONV_K // 2, CONV_K // 2),),
        dimension_numbers=('NWC', 'WIO', 'NWC'),
        feature_group_count=CONV_CH) + dw_b
    y = jax.nn.silu(layer_norm(y, ln_w, ln_b))
    return y @ pw_w + pw_b


def hybrid_layer(x, ctx, c, c_ctx, row, col, last,
                 norm_w, w_mod, b_mod, w_in, mla_q_norm, mla_w_uq, mla_kv_norm, mla_w_ukv,
                 gqa_q_norm, gqa_k_norm, conv_dw_w, conv_dw_b, conv_ln_w, conv_ln_b,
                 conv_pw_w, conv_pw_b, w_out):
    shift, scale, gate = jnp.split(jax.nn.silu(c) @ w_mod + b_mod, 3, axis=-1)
    shift_c, scale_c, gate_c = jnp.split(jax.nn.silu(c_ctx) @ w_mod + b_mod, 3, axis=-1)
    h = rms_norm(x, norm_w) * (1.0 + scale[:, None, :]) + shift[:, None, :]
    hc = rms_norm(ctx, norm_w) * (1.0 + scale_c) + shift_c

    p = h @ w_in
    if last:
        pc_kv = hc @ w_in[:, OFF_KV:OFF_GQA_Q]
    else:
        pc = hc @ w_in
        pc_kv = pc[..., OFF_KV:OFF_GQA_Q]
    p_kv = p[..., OFF_KV:OFF_GQA_Q]

    k_mla_c, v_mla_c = mla_keys_values(pc_kv, mla_kv_norm, mla_w_ukv, None, None)
    k_gqa_c, v_gqa_c = gqa_keys_values(pc_kv, gqa_k_norm, None, None)

    k_mla, v_mla = mla_keys_values(p_kv, mla_kv_norm, mla_w_ukv, row, col)
    q_mla = mla_query(p, mla_q_norm, mla_w_uq, row, col)
    o_mla = block_attention(q_mla, jnp.concatenate([k_mla, k_mla_c], axis=1),
                            jnp.concatenate([v_mla, v_mla_c], axis=1))
    k_gqa, v_gqa = gqa_keys_values(p_kv, gqa_k_norm, row, col)
    q_gqa = gqa_query(p, gqa_q_norm, row, col)
    o_gqa = block_attention(q_gqa, jnp.concatenate([k_gqa, k_gqa_c], axis=1),
                            jnp.concatenate([v_gqa, v_gqa_c], axis=1))
    o_conv = conformer_conv(p[..., OFF_CONV:OFF_GATE], conv_dw_w, conv_dw_b,
                            conv_ln_w, conv_ln_b, conv_pw_w, conv_pw_b)
    y = jnp.concatenate([o_mla, o_gqa, o_conv], axis=-1) * jax.nn.silu(p[..., OFF_GATE:])
    x_new = x + gate[:, None, :] * (y @ w_out)

    if last:
        return x_new, ctx

    oc_mla = block_attention(mla_query(pc, mla_q_norm, mla_w_uq, None, None), k_mla_c, v_mla_c)
    oc_gqa = block_attention(gqa_query(pc, gqa_q_norm, None, None), k_gqa_c, v_gqa_c)
    oc_conv = conformer_conv(pc[..., OFF_CONV:OFF_GATE], conv_dw_w, conv_dw_b,
                             conv_ln_w, conv_ln_b, conv_pw_w, conv_pw_b)
    yc = jnp.concatenate([oc_mla, oc_gqa, oc_conv], axis=-1) * jax.nn.silu(pc[..., OFF_GATE:])
    ctx_new = ctx + gate_c * (yc @ w_out)
    return x_new, ctx_new


def setup_inputs(seed: int = 0) -> dict:
    key = jax.random.key(seed)
    ks = jax.random.split(key, 24)
    f32 = jnp.float32
    nrm = lambda k, shape, s: jax.random.normal(k, shape, f32) * s
    L, D = DEPTH, D_MODEL
    return {
        'x': nrm(ks[0], (BATCH, SEQ, D), 1.0),
        'c': nrm(ks[1], (BATCH, D), 1.0),
        'ctx': nrm(ks[2], (BATCH, CTX_LEN, D), 1.0),
        'c_ctx': nrm(ks[3], (D,), 1.0),
        'norm_w': 1.0 + nrm(ks[4], (L, D), 0.05),
        'w_mod': nrm(ks[5], (L, D, 3 * D), 0.5 * D ** -0.5),
        'b_mod': nrm(ks[6], (L, 3 * D), 0.02),
        'w_in': nrm(ks[7], (L, D, IN_COLS), D ** -0.5),
        'mla_q_norm': 1.0 + nrm(ks[8], (L, MLA_Q_RANK), 0.05),
        'mla_w_uq': nrm(ks[9], (L, MLA_Q_RANK, MLA_HEADS * (MLA_NOPE + MLA_ROPE)), MLA_Q_RANK ** -0.5),
        'mla_kv_norm': 1.0 + nrm(ks[10], (L, MLA_KV_RANK), 0.05),
        'mla_w_ukv': nrm(ks[11], (L, MLA_KV_RANK, MLA_HEADS * (MLA_NOPE + MLA_V)), MLA_KV_RANK ** -0.5),
        'gqa_q_norm': 1.0 + nrm(ks[12], (L, GQA_HEAD_DIM), 0.05),
        'gqa_k_norm': 1.0 + nrm(ks[13], (L, GQA_HEAD_DIM), 0.05),
        'conv_dw_w': nrm(ks[14], (L, CONV_K, CONV_CH), CONV_K ** -0.5),
        'conv_dw_b': nrm(ks[15], (L, CONV_CH), 0.02),
        'conv_ln_w': 1.0 + nrm(ks[16], (L, CONV_CH), 0.05),
        'conv_ln_b': nrm(ks[17], (L, CONV_CH), 0.02),
        'conv_pw_w': nrm(ks[18], (L, CONV_CH, CONV_CH), CONV_CH ** -0.5),
        'conv_pw_b': nrm(ks[19], (L, CONV_CH), 0.02),
        'w_out': nrm(ks[20], (L, MIX_WIDTH, D), MIX_WIDTH ** -0.5),
        'final_norm_w': 1.0 + nrm(ks[21], (D,), 0.05),
    }


def reference(x, c, ctx, c_ctx, norm_w, w_mod, b_mod, w_in, mla_q_norm, mla_w_uq, mla_kv_norm,
              mla_w_ukv, gqa_q_norm, gqa_k_norm, conv_dw_w, conv_dw_b, conv_ln_w, conv_ln_b,
              conv_pw_w, conv_pw_b, w_out, final_norm_w):
    S = x.shape[1]
    rows = S // GRID_W
    row = jnp.repeat(jnp.arange(rows, dtype=jnp.int32), GRID_W)
    col = jnp.tile(jnp.arange(GRID_W, dtype=jnp.int32), rows)
    for l in range(DEPTH):
        x, ctx = hybrid_layer(
            x, ctx, c, c_ctx, row, col, l == DEPTH - 1,
            norm_w[l], w_mod[l], b_mod[l], w_in[l], mla_q_norm[l], mla_w_uq[l],
            mla_kv_norm[l], mla_w_ukv[l], gqa_q_norm[l], gqa_k_norm[l],
            conv_dw_w[l], conv_dw_b[l], conv_ln_w[l], conv_ln_b[l],
            conv_pw_w[l], conv_pw_b[l], w_out[l])
    return rms_norm(x, final_norm_w)
```

```python
import numpy as np
from contextlib import ExitStack
import concourse.bass as bass
import concourse.mybir as mybir
from concourse.bass_utils import run_bass_kernel_spmd

F32 = mybir.dt.float32
BF16 = mybir.dt.bfloat16
AF = mybir.ActivationFunctionType
ALU = mybir.AluOpType

D = 1024
SEQ = 4096
CTX = 256
NKEY = SEQ + CTX
NKT = NKEY // 128
EPS = 1e-6
NA = 1216
NB = 2048
NCV = 77
A_CKV, A_KR3, A_KR3P, A_GK, A_GKP, A_GV, A_CA, A_CG = 0, 128, 224, 320, 448, 576, 704, 960
B_CQ, B_GQ, B_GQP, B_GATE = 0, 256, 640, 1024


class _Op:
    __slots__ = ("eng", "fn", "deps", "sig", "val", "dkey", "is_dma", "sem")


class Sched:
    ENGS = ("pe", "act", "dve", "pool", "sp")

    def __init__(self, nc, stack):
        self.nc = nc
        self.stack = stack
        self.streams = {e: [] for e in self.ENGS}
        self.writers = {}
        self.readers = {}
        self.dma_cnt = {}
        self.dma_sems = {}
        self.eng_sems = {}
        self.psum_res = set()

    def op(self, eng, fn, r=(), w=(), dma=None):
        o = _Op()
        o.eng = eng; o.fn = fn; o.sig = False; o.val = None
        o.is_dma = dma is not None; o.dkey = dma; o.sem = None
        deps = []

        def add(d, kind):
            if d is o:
                return
            same = (d.eng == eng) and (not d.is_dma) and (not o.is_dma)
            if same:
                if eng == "pe" or kind == "RR":
                    return
            deps.append(d)

        for res in r:
            ws = self.writers.get(res)
            if ws:
                for d in ws.values():
                    add(d, "RAW")
            if res in self.psum_res:
                rs = self.readers.get(res)
                if rs:
                    for d in rs.values():
                        if d.eng != eng:
                            add(d, "RR")
        for res in w:
            ws = self.writers.get(res)
            if ws:
                for d in ws.values():
                    add(d, "WAW")
            rs = self.readers.get(res)
            if rs:
                for d in rs.values():
                    add(d, "WAR")
        k = ("dma", dma) if o.is_dma else eng
        for res in r:
            self.readers.setdefault(res, {})[k] = o
        for res in w:
            self.writers.setdefault(res, {})[k] = o
        if o.is_dma:
            n = self.dma_cnt.get(dma, 0) + 1
            self.dma_cnt[dma] = n
            o.val = 16 * n
            o.sig = True
        for d in deps:
            d.sig = True
        o.deps = deps
        self.streams[eng].append(o)
        return o

    def emit(self):
        nc = self.nc
        for e in self.ENGS:
            self.eng_sems[e] = self.stack.enter_context(nc.semaphore("es_" + e))
        for dk in self.dma_cnt:
            self.dma_sems[dk] = self.stack.enter_context(nc.semaphore("ds_%d" % len(self.dma_sems)))
        for e in self.ENGS:
            c = 0
            for o in self.streams[e]:
                if o.is_dma:
                    o.sem = self.dma_sems[o.dkey]
                else:
                    o.sem = self.eng_sems[e]
                    if o.sig:
                        c += 1
                        o.val = c
        block = self.stack.enter_context(nc.Block())

        def run(ename):
            def body(eng):
                waited = {}
                for o in self.streams[ename]:
                    need = {}
                    for d in o.deps:
                        if waited.get(d.sem, 0) < d.val and need.get(d.sem, 0) < d.val:
                            need[d.sem] = d.val
                    for sem, val in need.items():
                        eng.wait_ge(sem, val)
                        waited[sem] = val
                    if o.fn is None:
                        assert not o.sig
                        continue
                    ins = o.fn(eng)
                    if o.is_dma:
                        ins.then_inc(o.sem, 16)
                    elif o.sig:
                        ins.then_inc(o.sem, 1)
            return body

        block.tensor(run("pe"))
        block.scalar(run("act"))
        block.vector(run("dve"))
        block.gpsimd(run("pool"))
        block.sync(run("sp"))


def build_program(debug=False, n_layers=2, nblk=SEQ // 512, dump=False):
    nc = bass.Bass("TRN2", target_bir_lowering=False)

    def din(name, shape, dt=F32):
        return nc.dram_tensor(name, list(shape), dt, kind="ExternalInput").ap()

    x_d = din("x", [SEQ, D])
    ctx_d = din("ctx", [CTX, D])
    cvT_d = din("cvT", [128, 8, 2])
    normw_d = din("norm_w", [2, D])
    fnw_d = din("fnw", [D])
    wmod_d = din("w_mod", [2, D, 3 * D])
    bmod_d = din("b_mod", [2, 3 * D])
    winA_d = din("w_inA", [2, D, NA])
    winB_d = din("w_inB", [2, D, NB])
    wout_d = din("w_out", [2, D, D])
    wuqnT_d = din("wuqnT", [2, 6, 64, 256])
    wukT_d = din("wukT", [2, 6, 64, 128])
    wqr_d = din("wqr", [2, 256, 4, 96])
    wuv_d = din("wuv", [2, 128, 384])
    wpw_d = din("wpw", [2, 256, 256])
    colvec_d = din("colvec", [2, 128, NCV])
    tabG_d = din("tabG", [2, 128, SEQ])
    tabM_d = din("tabM", [2, 96, SEQ])
    ident_d = din("ident", [128, 128])
    out_d = nc.dram_tensor("out", [SEQ, D], F32, kind="ExternalOutput").ap()
    ikind = "ExternalOutput" if debug else "Internal"
    x1_d = nc.dram_tensor("x1", [SEQ, D], F32, kind=ikind).ap()
    ctx1_d = nc.dram_tensor("ctx1", [CTX, D], F32, kind=ikind).ap()
    glu_d = nc.dram_tensor("gluD", [2, 128, NKEY], BF16, kind="Internal").ap()
    mod_d = nc.dram_tensor("modD", [2, 2, 3 * D], F32, kind="Internal").ap()

    with ExitStack() as st:
        S = Sched(nc, st)

        def sb(name, shape, dt):
            return st.enter_context(nc.sbuf_tensor("sb_" + name, list(shape), dt))

        def ps(name, shape, dt=F32):
            return st.enter_context(nc.psum_tensor("ps_" + name, list(shape), dt))

        Wbuf = sb("Wbuf", [128, 8, NB], BF16)
        Wout = sb("Wout", [128, 8, D], BF16)
        Wc = sb("Wc", [128, 2, 6, 128], BF16)
        Wqr = sb("Wqr", [128, 2, 4, 96], BF16)
        Wuv = sb("Wuv", [128, 384], BF16)
        Wpw = sb("Wpw", [128, 2, 256], BF16)
        KC = sb("KC", [128, NKEY], BF16)
        KR = sb("KR", [128, NKEY], BF16)
        KG = sb("KG", [128, NKEY], BF16)
        VM = sb("VM", [128, NKT, 6, 65], BF16)
        VG = sb("VG", [128, NKT, 2, 65], BF16)
        gmod_bc = sb("gmod_bc", [128, D], F32)
        shift_bc = sb("shift_bc", [128, D], F32)
        gate_bc = sb("gate_bc", [128, D], F32)
        fnw_bc = sb("fnw_bc", [128, D], F32)
        XT = [sb("xt%d" % i, [128, D], F32) for i in range(2)]
        hb = sb("hb", [128, D], BF16)
        hT = sb("hT", [128, 8, 512], BF16)
        yT = hT
        QA = sb("QA", [128, 6, 512], BF16)
        QRp = sb("QRp", [128, 6, 512], BF16)
        QG = sb("QG", [128, 3, 512], BF16)
        GT = sb("GT", [128, 8, 512], BF16)
        cqn = sb("cqn", [128, 2, 512], BF16)
        PT = [sb("pt%d" % i, [128, 2, 512], BF16) for i in range(2)]
        gluw = sb("gluw", [128, 2, 542], BF16)
        cacc = sb("cacc", [128, 2, 512], F32)
        cbf = sb("cbf", [128, 2, 512], BF16)
        sq = sb("sq", [128, 2, 512], BF16)
        tabGs = sb("tabGs", [128, 2, 512], F32)
        tabMs = sb("tabMs", [128, 2, 512], F32)
        T1 = sb("T1", [128, 512], F32)
        T2 = sb("T2", [128, 512], F32)
        T3 = sb("T3", [128, 512], F32)
        T4 = sb("T4", [128, 512], F32)
        ident = sb("ident", [128, 128], BF16)
        M128 = sb("M128", [128, 128], BF16)
        M256 = sb("M256", [128, 128], BF16)
        M64 = sb("M64", [128, 128], BF16)
        ones32 = sb("ones32", [128, 128], F32)
        colv = sb("colv", [128, NCV], F32)
        ss = sb("ss", [128, 8], F32)
        cvs = sb("cvs", [128, 8, 2], F32)
        cv32 = sb("cv32", [128, 8, 2], F32)

        S0 = ps("S0", [128, 1024]); S1 = ps("S1", [128, 1024])
        O0 = ps("O0", [128, 512]); O1 = ps("O1", [128, 512])
        G0 = ps("G0", [128, 512]); G1 = ps("G1", [128, 512])
        banks = [(G0[:, :], "G0"), (G1[:, :], "G1"), (O0[:, :], "O0"), (O1[:, :], "O1"),
                 (S0[:, 0:512], "S0a"), (S0[:, 512:1024], "S0b"), (S1[:, 0:512], "S1a"), (S1[:, 512:1024], "S1b")]
        S.psum_res.update(k for _, k in banks)
        bank_ctr = [0]

        def nbank():
            b = banks[bank_ctr[0] % 8]
            bank_ctr[0] += 1
            return b

        def MM(out, lhsT, rhs, start, stop, r, w):
            S.op("pe", lambda e: e.matmul(out, lhsT=lhsT, rhs=rhs, start=start, stop=stop), r=r, w=w)

        def MMG(out, pairs, r, w):
            n = len(pairs)
            for i, (l, rr) in enumerate(pairs):
                MM(out, l, rr, i == 0, i == n - 1, r, w)

        def ACT(out, in_, func, r, w, bias=None, scale=None, accum=None):
            kw = {}
            if bias is not None:
                kw["bias"] = bias
            if scale is not None:
                kw["scale"] = scale
            if accum is not None:
                kw["accum_out"] = accum
            S.op("act", lambda e: e.activation(out=out, in_=in_, func=func, **kw), r=r, w=w)

        def TT(eng, out, in0, in1, op, r, w):
            S.op(eng, lambda e: e.tensor_tensor(out=out, in0=in0, in1=in1, op=op), r=r, w=w)

        def STT(eng, out, in0, scalar, in1, op0, op1, r, w):
            S.op(eng, lambda e: e.scalar_tensor_tensor(out=out, in0=in0, scalar=scalar, in1=in1, op0=op0, op1=op1), r=r, w=w)

        def TS(eng, out, in0, s1, s2, op0, op1, r, w):
            if s2 is None:
                S.op(eng, lambda e: e.tensor_scalar(out=out, in0=in0, scalar1=s1, scalar2=None, op0=op0), r=r, w=w)
            else:
                S.op(eng, lambda e: e.tensor_scalar(out=out, in0=in0, scalar1=s1, scalar2=s2, op0=op0, op1=op1), r=r, w=w)

        def CP(eng, out, in_, r, w):
            if eng == "act":
                S.op("act", lambda e: e.copy(out=out, in_=in_), r=r, w=w)
            else:
                S.op(eng, lambda e: e.tensor_copy(out=out, in_=in_), r=r, w=w)

        def TR(out, in_, r, w):
            S.op("pe", lambda e: e.transpose(out, in_, ident[:]), r=r, w=w)

        def RECIP(out, in_, r, w):
            S.op("dve", lambda e: e.reciprocal(out=out, in_=in_), r=r, w=w)

        def _l(x):
            return x if isinstance(x, list) else [x]

        def MSET(eng, ap, val, w):
            S.op(eng, lambda e: e.memset(ap, val), w=w)

        def DMA(q, out, in_, r, w, key):
            S.op(q, lambda e: e.dma_start(out=out, in_=in_), r=r, w=w, dma=key)

        def rstd_from_mean(out, in_, r, w, scale=1.0):
            ACT(out, in_, AF.Ln, r=r, w=w, bias=EPS, scale=scale)
            ACT(out, out, AF.Exp, r=w, w=w, scale=-0.5)

        DMA("pool", ident[:], ident_d, r=[], w=["ident"], key="cst")
        MSET("dve", M128[:], 1.0 / 128, ["M128"])
        MSET("dve", M256[:], 1.0 / 256, ["M256"])
        MSET("dve", M64[:], 0.0, ["M64"])
        MSET("dve", M64[0:64, 0:64], 1.0 / 64, ["M64"])
        MSET("dve", M64[64:128, 64:128], 1.0 / 64, ["M64"])
        MSET("dve", ones32[:], 1.0, ["ones32"])
        MSET("pool", QRp[:], 0.0, ["QR"])
        MSET("pool", VM[:], 1.0, [("VM", i) for i in range(NKT)])
        MSET("pool", VG[:], 1.0, [("VG", i) for i in range(NKT)])
        DMA("sp", fnw_bc[:], fnw_d.partition_broadcast(128), r=[], w=["fnw_bc"], key="cst2")
        DMA("sp", cv32[:], cvT_d, r=[], w=["cv32"], key="cst3")
        ACT(cvs[:], cv32[:], AF.Silu, r=["cv32"], w=["cvs"])

        WQ = ["Wq0", "Wq1", "Wq2", "Wq3"]
        Wf = Wbuf[:].bitcast(F32)

        hoist_l1 = (n_layers == 2 and nblk == SEQ // 512)

        def mod_chunk(l, j, hoisted, part="all"):
            wm = wmod_d[l].rearrange("(c p) n -> p c n", p=128)
            hf = j % 2
            cs = slice(j * 512, (j + 1) * 512)
            wkeys = [WQ[2 * hf], WQ[2 * hf + 1]]
            if hoisted:
                a1, k1 = XT[0][0:2, 0:512], ("xt", 0)
                a2, k2 = XT[0][0:2, 512:1024], ("xt", 0)
                a3, k3 = XT[1][0:2, 0:512], ("xt", 1)
                bk, bkey = G1[:, :], "G1"
                wdma = wkeys + ["WA", "WB"]
            else:
                a1, k1 = T1[0:2, :], "T1"
                a2, k2 = T2[0:2, :], "T2"
                a3, k3 = T3[0:2, :], "T3"
                bk, bkey = nbank()
                wdma = wkeys
            if part in ("all", "dma"):
                DMA("sp", Wf[:, :, hf * 512:(hf + 1) * 512], wm[:, :, cs], r=[], w=wdma, key="Wm%d" % hf)
                if part == "dma":
                    return
            DMA("sp", a2, bmod_d[l, cs].partition_broadcast(2), r=[], w=[k2], key="bm")
            MMG(bk[0:2, :], [(cvs[:, k, :], Wf[:, k, hf * 512:(hf + 1) * 512]) for k in range(8)],
                r=["cvs"] + wkeys, w=[bkey])
            TT("dve", a1, bk[0:2, :], a2, ALU.add, r=[bkey, k2], w=[k1])
            if j in (2, 3):
                DMA("sp", a3, normw_d[l, (j - 2) * 512:(j - 1) * 512].partition_broadcast(2), r=[], w=[k3], key="nwc")
                TS("dve", a1, a1, 1.0, None, ALU.add, None, r=[k1], w=[k1])
                TT("dve", a1, a1, a3, ALU.mult, r=[k1, k3], w=[k1])
            DMA("sp", mod_d[l, :, cs], a1, r=[k1], w=["modD"], key="modst")

        for l in range(2):
            if l == 1 and hoist_l1:
                continue
            for j in range(6):
                mod_chunk(l, j, False)

        def load_bc(l, v):
            DMA("sp", shift_bc[:], mod_d[l, v, 0:D].partition_broadcast(128), r=["modD"], w=["shift_bc"], key="bc0")
            DMA("sp", gmod_bc[:], mod_d[l, v, D:2 * D].partition_broadcast(128), r=["modD"], w=["gmod_bc"], key="bc1")
            DMA("sp", gate_bc[:], mod_d[l, v, 2 * D:3 * D].partition_broadcast(128), r=["modD"], w=["gate_bc"], key="bc2")

        xt_ctr = [0]

        HB = {"buf": None, "res": None}

        stat_ctr = [0]
        hb_ctr = [0]
        PT0f = PT[0][:, :, :].rearrange("p a n -> p (a n)")
        PT1f = PT[1][:, :, :].rearrange("p a n -> p (a n)")

        def rms_stats(xt, xr):
            k = stat_ctr[0] % 2
            stat_ctr[0] += 1
            c0 = 4 * k
            ACT(PT0f, xt[:], AF.Square, r=[xr], w=[("pt", 0), ("ss", k, 0)], accum=ss[:, c0:c0 + 1])
            ACT(ss[:, c0 + 1:c0 + 2], ss[:, c0:c0 + 1], AF.Ln, r=[("ss", k, 0)], w=[("ss", k, 1)], bias=EPS, scale=1.0 / D)
            ACT(ss[:, c0 + 2:c0 + 3], ss[:, c0 + 1:c0 + 2], AF.Exp, r=[("ss", k, 1)], w=[("ss", k, 2)], scale=-0.5)
            return ss[:, c0 + 2:c0 + 3], ("ss", k, 2)

        def front_sub(src_d, srcres, row0, s, bank=None):
            i = xt_ctr[0] % 2
            xt_ctr[0] += 1
            xt = XT[i]
            xr = ("xt", i)
            DMA("sp", xt[:], src_d[row0 + s * 128: row0 + (s + 1) * 128, :], r=[srcres], w=[xr], key="xt%d" % i)
            rs, rk = rms_stats(xt, xr)
            STT("dve", xt[:], xt[:], rs, gmod_bc[:], ALU.mult, ALU.mult, r=[xr, rk, "gmod_bc"], w=[xr])
            k = hb_ctr[0] % 2
            hb_ctr[0] += 1
            hbx, hk = (hb[:, :], "hb") if k == 0 else (PT1f, ("pt", 1))
            TT("dve", hbx, xt[:], shift_bc[:], ALU.add, r=[xr, "shift_bc"], w=[hk])
            bk, bkey = bank if bank is not None else nbank()
            bkb = bk.bitcast(BF16)
            for c in range(8):
                TR(bkb[:, c * 128:(c + 1) * 128], hbx[:, c * 128:(c + 1) * 128], r=[hk, "ident"], w=[bkey])
            return bkb, bkey

        def front_chain(src_d, srcres, row0, s):
            i = xt_ctr[0] % 2
            xt_ctr[0] += 1
            xt = XT[i]
            xr = ("xt", i)
            DMA("sp", xt[:], src_d[row0 + s * 128: row0 + (s + 1) * 128, :], r=[srcres], w=[xr], key="xt%d" % i)
            rs, rk = rms_stats(xt, xr)
            STT("dve", xt[:], xt[:], rs, gmod_bc[:], ALU.mult, ALU.mult, r=[xr, rk, "gmod_bc"], w=[xr])
            k = hb_ctr[0] % 2
            hb_ctr[0] += 1
            hbx, hk = (hb[:, :], "hb") if k == 0 else (PT1f, ("pt", 1))
            TT("dve", hbx, xt[:], shift_bc[:], ALU.add, r=[xr, "shift_bc"], w=[hk])
            return hbx, hk

        def front_tr_evac(hbx, hk, hbuf, hres, s):
            bk, bkey = nbank()
            bkb = bk.bitcast(BF16)
            for c in range(8):
                TR(bkb[:, c * 128:(c + 1) * 128], hbx[:, c * 128:(c + 1) * 128], r=[hk, "ident"], w=[bkey])
            evac_sub(hbuf, hres, s, bkb, bkey)

        def evac_sub(hbuf, hres, s, bkb, bkey):
            CP("dve", hbuf[:, :, s * 128:(s + 1) * 128], bkb[:, 0:1024].rearrange("p (c t) -> p c t", c=8), r=[bkey], w=hres)

        def make_hT(src_d, srcres, row0, NT, hbuf=None, hres=None):
            hbuf = hT if hbuf is None else hbuf
            hres = ["hT"] if hres is None else hres
            for s in range(NT // 128):
                bkb, bkey = front_sub(src_d, srcres, row0, s)
                evac_sub(hbuf, hres, s, bkb, bkey)

        def proj(cols, m, NT, wres, hbuf=None, hres=None):
            hbuf = hT if hbuf is None else hbuf
            hres = ["hT"] if hres is None else hres
            bk, bkey = nbank()
            MMG(bk[0:m, 0:NT], [(Wbuf[:, k, cols:cols + m], hbuf[:, k, 0:NT]) for k in range(8)], r=[wres] + hres, w=[bkey])
            return bk, bkey

        def load_tables(t0, NT):
            DMA("sp", tabGs[:, :, 0:NT], tabG_d[:, :, t0:t0 + NT].rearrange("a p n -> p a n"), r=[], w=["tabGs"], key="tabG")
            DMA("sp", tabMs[0:96, :, 0:NT], tabM_d[:, :, t0:t0 + NT].rearrange("a p n -> p a n"), r=[], w=["tabMs", "tabMs1"], key="tabM")

        def head_norm_rope(o_bk, o_key, p_bk, p_key, g_col, gp_col, NT, rope, dst, dst_res, pre_squared=False):
            if not pre_squared:
                ACT(sq[:, 0, 0:NT], o_bk[:, 0:NT], AF.Square, r=[o_key], w=["sq"])
            mb, mkey = nbank()
            MM(mb[:, 0:NT], M64[:], sq[:, 0, 0:NT], True, True, r=["M64", "sq"], w=[mkey])
            rstd_from_mean(T3[:, 0:NT], mb[:, 0:NT], r=[mkey], w=["T3"])
            if rope:
                STT("dve", T1[:, 0:NT], o_bk[:, 0:NT], colv[:, g_col:g_col + 1], tabGs[:, 0, 0:NT], ALU.mult, ALU.mult,
                    r=[o_key, "colv", "tabGs"], w=["T1"])
                STT("dve", T2[:, 0:NT], p_bk[:, 0:NT], colv[:, gp_col:gp_col + 1], tabGs[:, 1, 0:NT], ALU.mult, ALU.mult,
                    r=[p_key, "colv", "tabGs"], w=["T2"])
                TT("dve", T1[:, 0:NT], T1[:, 0:NT], T2[:, 0:NT], ALU.add, r=["T1", "T2"], w=["T1"])
                TT("dve", dst, T1[:, 0:NT], T3[:, 0:NT], ALU.mult, r=["T1", "T3"], w=_l(dst_res))
            else:
                STT("dve", dst, o_bk[:, 0:NT], colv[:, g_col:g_col + 1], T3[:, 0:NT], ALU.mult, ALU.mult,
                    r=[o_key, "colv", "T3"], w=_l(dst_res))

        def rope96(o_bk, o_key, p_bk, p_key, NT, rope, dst, dst_res):
            if rope:
                TT("dve", T1[0:96, 0:NT], o_bk[0:96, 0:NT], tabMs[0:96, 0, 0:NT], ALU.mult, r=[o_key, "tabMs"], w=["T1"])
                TT("dve", T2[0:96, 0:NT], p_bk[0:96, 0:NT], tabMs[0:96, 1, 0:NT], ALU.mult, r=[p_key, "tabMs", "tabMs1"], w=["T2"])
                if isinstance(dst, list):
                    for jj in range(3):
                        TT("dve", dst[jj], T1[32 * jj:32 * jj + 32, 0:NT], T2[32 * jj:32 * jj + 32, 0:NT], ALU.add, r=["T1", "T2"], w=_l(dst_res))
                else:
                    TT("dve", dst, T1[0:96, 0:NT], T2[0:96, 0:NT], ALU.add, r=["T1", "T2"], w=_l(dst_res))
            else:
                if isinstance(dst, list):
                    for jj in range(3):
                        CP("dve", dst[jj], o_bk[32 * jj:32 * jj + 32, 0:NT], r=[o_key], w=_l(dst_res))
                else:
                    CP("dve", dst, o_bk[0:96, 0:NT], r=[o_key], w=_l(dst_res))

        def phase_A(l, src_d, srcres, row0, NT, key0, rope, hbuf, hres):
            kt0 = key0 // 128
            nsub = NT // 128

            def pj(cols, m, NT, wres):
                return proj(cols, m, NT, wres, hbuf, hres)

            st = {}

            def pieceA():
                if rope:
                    load_tables(row0, NT)
                cb, ckey = pj(A_CKV, 128, NT, "WA")
                ACT(sq[:, 0, 0:NT], cb[:, 0:NT], AF.Square, r=[ckey], w=["sq"])
                st["ckv"] = (cb, ckey)

            def pieceB():
                cb, ckey = st["ckv"]
                mb, mkey = nbank()
                MM(mb[:, 0:NT], M128[:], sq[:, 0, 0:NT], True, True, r=["M128", "sq"], w=[mkey])
                rstd_from_mean(T3[:, 0:NT], mb[:, 0:NT], r=[mkey], w=["T3"])
                STT("dve", KC[:, key0:key0 + NT], cb[:, 0:NT], colv[:, 2:3], T3[:, 0:NT], ALU.mult, ALU.mult,
                    r=[ckey, "colv", "T3"], w=[("KC", kt0 + i) for i in range(nsub)])
                kb, kkey = pj(A_KR3, 96, NT, "WA")
                if rope:
                    kpb, kpkey = pj(A_KR3P, 96, NT, "WA")
                else:
                    kpb, kpkey = None, None
                rope96(kb, kkey, kpb, kpkey, NT, rope, KR[0:96, key0:key0 + NT], [("KR", kt0 + i) for i in range(nsub)])
                for s in range(nsub):
                    vb, vkey = nbank()
                    MM(vb[:, 0:384], KC[:, key0 + s * 128: key0 + (s + 1) * 128], Wuv[:], True, True, r=[("KC", kt0 + s), "Wuv"], w=[vkey])
                    CP("act", VM[:, kt0 + s, :, 0:64], vb[:, 0:384].rearrange("p (h d) -> p h d", h=6), r=[vkey], w=[("VM", kt0 + s)])

            def pieceC():
                gb, gkey = pj(A_GK, 128, NT, "WA")
                if rope:
                    gpb, gpkey = pj(A_GKP, 128, NT, "WA")
                else:
                    gpb, gpkey = None, None
                ACT(sq[:, 0, 0:NT], gb[:, 0:NT], AF.Square, r=[gkey], w=["sq"])
                st["gk"] = (gb, gkey, gpb, gpkey)

            def pieceD():
                gb, gkey, gpb, gpkey = st["gk"]
                head_norm_rope(gb, gkey, gpb, gpkey, 5, 6, NT, rope, KG[:, key0:key0 + NT], [("KG", kt0 + i) for i in range(nsub)],
                               pre_squared=True)
                for s in range(nsub):
                    vb, vkey = nbank()
                    MMG(vb[:, 0:128], [(hbuf[:, k, s * 128:(s + 1) * 128], Wbuf[:, k, A_GV:A_GV + 128]) for k in range(8)],
                        r=hres + ["WA"], w=[vkey])
                    CP("act", VG[:, kt0 + s, :, 0:64], vb[:, 0:128].rearrange("p (h d) -> p h d", h=2), r=[vkey], w=[("VG", kt0 + s)])
                for c in range(2):
                    ab, akey = pj(A_CA + c * 128, 128, NT, "WA")
                    gb2, gkey2 = pj(A_CG + c * 128, 128, NT, "WA")
                    ACT(T1[:, 0:NT], gb2[:, 0:NT], AF.Sigmoid, r=[gkey2], w=["T1"])
                    TT("dve", cbf[:, c, 0:NT], ab[:, 0:NT], T1[:, 0:NT], ALU.mult, r=[akey, "T1"], w=["cbf"])
                DMA("pool", glu_d[:, :, key0:key0 + NT].rearrange("c p n -> p c n"), cbf[:, :, 0:NT], r=["cbf"], w=["gluD"], key="glust")

            return [pieceA, pieceB, pieceC, pieceD]

        def run_interleaved(pieces, front):
            for s in range(4):
                h = None
                if front is not None:
                    h = front_chain(front[0], front[1], front[2], s)
                pieces[s]()
                if front is not None:
                    front_tr_evac(h[0], h[1], front[3], front[4], s)

        def phase_B(l, src_d, srcres, dst_d, dstres, row0, NT, key0, rope, nkt, last, have_hT=False, next_row0=None, hoist=None):
            nsub = NT // 128
            if not have_hT:
                make_hT(src_d, srcres, row0, NT)
            if rope:
                load_tables(row0, NT)
            seq0 = 0 if not rope else CTX
            seqn = CTX if not rope else SEQ
            lo = key0 - 15
            hi = key0 + NT + 15
            clo = max(lo, seq0)
            chi = min(hi, seq0 + seqn)
            if clo > lo:
                MSET("pool", gluw[:, :, 0:clo - lo], 0.0, ["gluw"])
            if chi < hi:
                MSET("pool", gluw[:, :, NT + 30 - (hi - chi):NT + 30], 0.0, ["gluw"])
            DMA("sp", gluw[:, :, clo - lo:chi - lo], glu_d[:, :, clo:chi].rearrange("c p n -> p c n"), r=["gluD"], w=["gluw"], key="gluw")
            cqb = [proj(B_CQ + c * 128, 128, NT, "WB") for c in range(2)]
            for c in range(2):
                ACT(sq[:, c, 0:NT], cqb[c][0][:, 0:NT], AF.Square, r=[cqb[c][1]], w=["sq"])

            def cq_tail():
                mb, mkey = nbank()
                MMG(mb[:, 0:NT], [(M256[:], sq[:, c, 0:NT]) for c in range(2)], r=["M256", "sq"], w=[mkey])
                rstd_from_mean(T3[:, 0:NT], mb[:, 0:NT], r=[mkey], w=["T3"])
                for c in range(2):
                    STT("dve", cqn[:, c, 0:NT], cqb[c][0][:, 0:NT], colv[:, c:c + 1], T3[:, 0:NT], ALU.mult, ALU.mult,
                        r=[cqb[c][1], "colv", "T3"], w=["cqn"])

            def qa_qr():
                for h in range(6):
                    qb, qkey = nbank()
                    MMG(qb[:, 0:NT], [(Wc[:, c, h, :], cqn[:, c, 0:NT]) for c in range(2)], r=["Wc", "cqn"], w=[qkey])
                    CP("act" if h % 2 else "dve", QA[:, h, 0:NT], qb[:, 0:NT], r=[qkey], w=["QA"])
                for g in range(2):
                    ob, okey = nbank()
                    MMG(ob[0:96, 0:NT], [(Wqr[:, c, g, :], cqn[:, c, 0:NT]) for c in range(2)], r=["Wqr", "cqn"], w=[okey])
                    if rope:
                        pb, pkey = nbank()
                        MMG(pb[0:96, 0:NT], [(Wqr[:, c, 2 + g, :], cqn[:, c, 0:NT]) for c in range(2)], r=["Wqr", "cqn"], w=[pkey])
                    else:
                        pb, pkey = None, None
                    rope96(ob, okey, pb, pkey, NT, rope, [QRp[32 * jj:32 * jj + 32, 3 * g + jj, 0:NT] for jj in range(3)], "QR")

            if rope:
                qslots = [(cbf[:, 0, 0:NT], "cbf"), (cbf[:, 1, 0:NT], "cbf"), (sq[:, 0, 0:NT], "sq")]

                def qg_proj(c):
                    ob, okey = proj(B_GQ + c * 128, 128, NT, "WB")
                    pb, pkey = proj(B_GQP + c * 128, 128, NT, "WB")
                    ACT(qslots[c][0], ob[:, 0:NT], AF.Square, r=[okey], w=[qslots[c][1]])
                    return (ob, okey, pb, pkey)

                def qg_tail(c, pr):
                    ob, okey, pb, pkey = pr
                    mb, mkey = nbank()
                    MM(mb[:, 0:NT], M64[:], qslots[c][0], True, True, r=["M64", qslots[c][1]], w=[mkey])
                    rstd_from_mean(T3[:, 0:NT], mb[:, 0:NT], r=[mkey], w=["T3"])
                    STT("dve", T1[:, 0:NT], ob[:, 0:NT], colv[:, 3:4], tabGs[:, 0, 0:NT], ALU.mult, ALU.mult, r=[okey, "colv", "tabGs"], w=["T1"])
                    STT("dve", T2[:, 0:NT], pb[:, 0:NT], colv[:, 4:5], tabGs[:, 1, 0:NT], ALU.mult, ALU.mult, r=[pkey, "colv", "tabGs"], w=["T2"])
                    TT("dve", T1[:, 0:NT], T1[:, 0:NT], T2[:, 0:NT], ALU.add, r=["T1", "T2"], w=["T1"])
                    TT("dve", QG[:, c, 0:NT], T1[:, 0:NT], T3[:, 0:NT], ALU.mult, r=["T1", "T3"], w=["QG"])

                pr0 = qg_proj(0)
                pr1 = qg_proj(1)
                cq_tail()
                qg_tail(0, pr0)
                qg_tail(1, pr1)
                qa_qr()
                pr2 = qg_proj(2)
                qg_tail(2, pr2)
            else:
                cq_tail()
                qa_qr()
                for c in range(3):
                    ob, okey = proj(B_GQ + c * 128, 128, NT, "WB")
                    head_norm_rope(ob, okey, None, None, 3, 4, NT, rope, QG[:, c, 0:NT], "QG")
            for c in range(8):
                gb, gkey = proj(B_GATE + c * 128, 128, NT, "WB")
                ACT(GT[:, c, 0:NT], gb[:, 0:NT], AF.Silu, r=[gkey], w=[("GT", c)])
            interleave = (NT == 512)
            cchunks = []

            def ch_conv_tap(c, k):
                if k == 0:
                    TS("dve", cacc[:, c, 0:NT], gluw[:, c, 0:NT], colv[:, 15 + c * 31: 16 + c * 31], colv[:, 7 + c:8 + c], ALU.mult, ALU.add,
                       r=["gluw", "colv"], w=[("cacc", c)])
                else:
                    STT("dve", cacc[:, c, 0:NT], gluw[:, c, k:k + NT], colv[:, 15 + c * 31 + k: 16 + c * 31 + k], cacc[:, c, 0:NT],
                        ALU.mult, ALU.add, r=["gluw", "colv", ("cacc", c)], w=[("cacc", c)])

            X1 = tabGs[:, 0, 0:NT]; X2 = tabGs[:, 1, 0:NT]; X3 = tabMs[:, 0, 0:NT]; X4 = tabMs[:, 1, 0:NT]

            def cbank():
                return (G1[:, :], "G1") if interleave else nbank()

            def ch_cs_cast(c):
                CP("dve", cbf[:, c, 0:NT], cacc[:, c, 0:NT], r=[("cacc", c)], w=["cbf"])
                TT("dve", sq[:, c, 0:NT], cacc[:, c, 0:NT], cacc[:, c, 0:NT], ALU.mult, r=[("cacc", c)], w=["sq"])

            def ch_cs_mean():
                bk, bkey = cbank()
                MMG(bk[:, 0:NT], [(M256[:], cbf[:, c, 0:NT]) for c in range(2)], r=["M256", "cbf"], w=[bkey])
                CP("dve", X1, bk[:, 0:NT], r=[bkey], w=["tabGs"])
                TT("dve", X2, X1, X1, ALU.mult, r=["tabGs"], w=["tabGs"])

            def ch_cs_var():
                bk, bkey = cbank()
                MMG(bk[:, 0:NT], [(M256[:], sq[:, c, 0:NT]) for c in range(2)], r=["M256", "sq"], w=[bkey])
                TT("dve", X2, bk[:, 0:NT], X2, ALU.subtract, r=[bkey, "tabGs"], w=["tabGs"])
                TS("dve", X2, X2, 0.0, None, ALU.max, None, r=["tabGs"], w=["tabGs"])

            def ch_cs_rstd1():
                ACT(X3, X2, AF.Ln, r=["tabGs"], w=["tabMs"], bias=EPS)

            def ch_cs_rstd2():
                ACT(X3, X3, AF.Exp, r=["tabMs"], w=["tabMs"], scale=-0.5)

            def ch_cs_norm(c):
                TT("dve", cacc[:, c, 0:NT], cacc[:, c, 0:NT], X1, ALU.subtract, r=[("cacc", c), "tabGs"], w=[("cacc", c)])
                TT("dve", cacc[:, c, 0:NT], cacc[:, c, 0:NT], X3, ALU.mult, r=[("cacc", c), "tabMs"], w=[("cacc", c)])
                TS("dve", cacc[:, c, 0:NT], cacc[:, c, 0:NT], colv[:, 9 + c:10 + c], colv[:, 11 + c:12 + c], ALU.mult, ALU.add,
                   r=[("cacc", c), "colv"], w=[("cacc", c)])

            def ch_cs_silu_a(c):
                ACT(X4, cacc[:, c, 0:NT], AF.Exp, r=[("cacc", c)], w=["tabMs1"], scale=-1.0)

            def ch_cs_silu_b(c):
                TS("dve", X4, X4, 1.0, None, ALU.add, None, r=["tabMs1"], w=["tabMs1"])
                RECIP(X4, X4, r=["tabMs1"], w=["tabMs1"])
                TT("dve", cbf[:, c, 0:NT], cacc[:, c, 0:NT], X4, ALU.mult, r=[("cacc", c), "tabMs1"], w=["cbf"])

            cs_chunks = [lambda: ch_cs_cast(0), lambda: ch_cs_cast(1), ch_cs_mean, ch_cs_var, ch_cs_rstd1, ch_cs_rstd2,
                         lambda: ch_cs_norm(0), lambda: ch_cs_norm(1), lambda: ch_cs_silu_a(0), lambda: ch_cs_silu_b(0),
                         lambda: ch_cs_silu_a(1), lambda: ch_cs_silu_b(1)]

            def ch_conv_pw(co):
                bk, bkey = (G1[:, :], "G1") if interleave else nbank()
                MMG(bk[:, 0:NT], [(Wpw[:, ci, co * 128:(co + 1) * 128], cbf[:, ci, 0:NT]) for ci in range(2)], r=["Wpw", "cbf"], w=[bkey])
                STT("dve", yT[:, 6 + co, 0:NT], bk[:, 0:NT], colv[:, 13 + co:14 + co], GT[:, 6 + co, 0:NT], ALU.add, ALU.mult,
                    r=[bkey, "colv", ("GT", 6 + co)], w=["hT"])

            for k in range(31):
                for c in range(2):
                    cchunks.append(lambda c=c, k=k: ch_conv_tap(c, k))
            n_tap_chunks = len(cchunks)
            cchunks.extend(cs_chunks)
            cchunks.append(lambda: ch_conv_pw(0))
            cchunks.append(lambda: ch_conv_pw(1))
            n_late = len(cchunks) - n_tap_chunks
            if not interleave:
                for f in cchunks:
                    f()
                cchunks = []
            Sbufs = [(S0, ["S0a", "S0b"]), (S1, ["S1a", "S1b"])]
            OB = {"O0": (O0, "O0"), "O1": (O1, "O1"), "G0": (G0, "G0")}
            segs = [("gqa", 0, ("O0", "O1")), ("mla", 0, ("G0",)), ("mla", 1, ("O0",)), ("gqa", 1, ("O1", "G0")),
                    ("mla", 2, ("O0",)), ("mla", 3, ("O1",)), ("gqa", 2, ("G0", "O0")), ("mla", 4, ("O1",)), ("mla", 5, ("G0",))]
            items = []
            for kind, idx, obs in segs:
                n = nkt // 2 if kind == "mla" else nkt
                for j in range(n):
                    items.append((kind, idx, obs, j, n))
            nit = len(items)

            def emit_QK(i):
                kind, idx, obs, j, n = items[i]
                Sb, Skeys = Sbufs[i % 2]
                if kind == "mla":
                    h = idx
                    for half in range(2):
                        kt = 2 * j + half
                        ks = slice(kt * 128, (kt + 1) * 128)
                        so = Sb[:, half * 512: half * 512 + NT]
                        MM(so, KC[:, ks], QA[:, h, 0:NT], True, False, r=[("KC", kt), "QA"], w=[Skeys[half]])
                        MM(so, KR[0:96, ks], QRp[0:96, h, 0:NT], False, True, r=[("KR", kt), "QR"], w=[Skeys[half]])
                else:
                    c = idx
                    kt = j
                    ks = slice(kt * 128, (kt + 1) * 128)
                    for g in range(2):
                        so = Sb[:, g * 512: g * 512 + NT]
                        MM(so, KG[64 * g:64 * g + 64, ks], QG[64 * g:64 * g + 64, c, 0:NT], True, True, r=[("KG", kt), "QG"], w=[Skeys[g]])

            def emit_EXP(i):
                kind = items[i][0]
                sc = 96.0 ** -0.5 if kind == "mla" else 0.125
                Sb, Skeys = Sbufs[i % 2]
                Pt = PT[i % 2]
                Pkey = ("pt", i % 2)
                if NT == 512:
                    ACT(Pt[:, :, :].rearrange("p a n -> p (a n)"), Sb[:, :], AF.Exp, r=Skeys, w=[Pkey], scale=sc)
                else:
                    ACT(Pt[:, :, 0:NT], Sb[:, :].rearrange("p (a n) -> p a n", a=2)[:, :, 0:NT], AF.Exp, r=Skeys, w=[Pkey], scale=sc)

            def emit_PV(i):
                kind, idx, obs, j, n = items[i]
                Pt = PT[i % 2]
                Pkey = ("pt", i % 2)
                if kind == "mla":
                    Ob, Okey = OB[obs[0]]
                    for half in range(2):
                        kt = 2 * j + half
                        MM(Ob[0:65, 0:NT], VM[:, kt, idx, :], Pt[:, half, 0:NT], j == 0 and half == 0, j == n - 1 and half == 1,
                           r=[("VM", kt), Pkey], w=[Okey])
                else:
                    kt = j
                    for g in range(2):
                        Ob, Okey = OB[obs[g]]
                        MM(Ob[0:65, 0:NT], VG[:, kt, g, :], Pt[:, g, 0:NT], j == 0, j == n - 1, r=[("VG", kt), Pkey], w=[Okey])

            post_ctr = [0]

            def post1(obname, chunk, R):
                Ob, Okey = OB[obname]
                pc = post_ctr[0] % 2
                post_ctr[0] += 1
                Tr, trk = (T4, "T4r") if pc == 0 else (T3, "T3")
                Tx, txk = (T2, "T2") if pc == 0 else (T1, "T1")
                RECIP(Tr[64:65, 0:NT], Ob[64:65, 0:NT], r=[Okey], w=[trk])
                CP("dve", Tx[R:R + 64, 0:NT], Ob[0:64, 0:NT], r=[Okey], w=[txk])
                TT("dve", Tx[R:R + 64, 0:NT], Tx[R:R + 64, 0:NT], GT[R:R + 64, chunk, 0:NT], ALU.mult, r=[txk, ("GT", chunk)], w=[txk])
                return (Tr, trk, Tx, txk, chunk, R)

            def post2(st):
                Tr, trk, Tx, txk, chunk, R = st
                MM(G1[:, 0:NT], ones32[64:65, :], Tr[64:65, 0:NT], True, True, r=["ones32", trk], w=["G1"])
                TT("dve", yT[R:R + 64, chunk, 0:NT], Tx[R:R + 64, 0:NT], G1[R:R + 64, 0:NT], ALU.mult, r=[txk, "G1"], w=["hT"])

            pending = {}
            emit_QK(0)
            for i in range(nit):
                if hoist:
                    for hf_ in hoist.pop(i, []):
                        hf_()
                if cchunks and i >= 4:
                    k_late = n_late - len(cchunks)
                    if k_late < 0 or i >= 100 + 3 * k_late:
                        cchunks.pop(0)()
                if i + 1 < nit:
                    emit_QK(i + 1)
                emit_EXP(i)
                emit_PV(i)
                kind, idx, obs, j, n = items[i]
                if j == n - 1:
                    if kind == "mla":
                        heads = [(obs[0], idx // 2, 64 * (idx % 2))]
                    else:
                        heads = [(obs[0], 3 + idx // 2, 64 * (idx % 2)), (obs[1], 3 + (idx + 3) // 2, 64 * ((idx + 3) % 2))]
                    for k2, (obn, chunk, R) in enumerate(heads):
                        pending.setdefault(min(i + 8 + k2, nit - 1), []).append(post1(obn, chunk, R))
                for stt in pending.pop(i, []):
                    post2(stt)
            while cchunks:
                cchunks.pop(0)()
            def post_sub(s, obanks):
                i = xt_ctr[0] % 2
                xt_ctr[0] += 1
                xt = XT[i]
                xr = ("xt", i)
                DMA("sp", xt[:], src_d[row0 + s * 128: row0 + (s + 1) * 128, :], r=[srcres], w=[xr], key="xt%d" % i)
                for f in range(2):
                    ob, okey = obanks[f] if obanks is not None else nbank()
                    Tt, tk = (T1, "T1") if f == 0 else (T2, "T2")
                    MMG(ob[:, :], [(yT[:, c, s * 128:(s + 1) * 128], Wout[:, c, f * 512:(f + 1) * 512]) for c in range(8)],
                        r=["hT", "Wout"], w=[okey])
                    TT("dve", Tt[:, :], ob[:, :], gate_bc[:, f * 512:(f + 1) * 512], ALU.mult, r=[okey, "gate_bc"], w=[tk])
                    TT("dve", xt[:, f * 512:(f + 1) * 512], Tt[:, :], xt[:, f * 512:(f + 1) * 512], ALU.add, r=[tk, xr], w=[xr])
                if last:
                    rs, rk = rms_stats(xt, xr)
                    STT("dve", xt[:], xt[:], rs, fnw_bc[:], ALU.mult, ALU.mult, r=[xr, rk, "fnw_bc"], w=[xr])
                DMA("pool", dst_d[row0 + s * 128: row0 + (s + 1) * 128, :], xt[:], r=[xr], w=[dstres], key="xst%d" % i)

            if next_row0 is None:
                for s in range(nsub):
                    post_sub(s, None)
            else:
                fbanks = [(S0[:, 0:512], "S0a"), (S0[:, 512:1024], "S0b"), (S1[:, 0:512], "S1a"), (S1[:, 512:1024], "S1b")]
                ob4 = [(O0[:, :], "O0"), (O1[:, :], "O1"), (G0[:, :], "G0"), (G1[:, :], "G1")]
                pend = []
                for s in range(nsub):
                    if last:
                        post_sub(s, (ob4[(2 * s) % 4], ob4[(2 * s + 1) % 4]))
                        pend.append(front_sub(src_d, srcres, next_row0, s, bank=fbanks[s]))
                        continue
                    hbx, hk = front_chain(src_d, srcres, next_row0, s)
                    post_sub(s, (ob4[(2 * s) % 4], ob4[(2 * s + 1) % 4]))
                    bk, bkey = fbanks[s]
                    bkb = bk.bitcast(BF16)
                    for c in range(8):
                        TR(bkb[:, c * 128:(c + 1) * 128], hbx[:, c * 128:(c + 1) * 128], r=[hk, "ident"], w=[bkey])
                    pend.append((bkb, bkey))
                for s in range(nsub):
                    evac_sub(hT, ["hT"], s, pend[s][0], pend[s][1])

        for l in range(n_layers):
            last = l == 1
            xs_d, xs_res = (x_d, "xD") if l == 0 else (x1_d, "x1D")
            xo_d, xo_res = (x1_d, "x1D") if l == 0 else (out_d, "outD")
            cs_d, cs_res = (ctx_d, "ctxD") if l == 0 else (ctx1_d, "ctx1D")
            DMA("sp", colv[:], colvec_d[l], r=[], w=["colv"], key="colv")
            DMA("pool", Wout[:], wout_d[l].rearrange("(c p) n -> p c n", p=128), r=[], w=["Wout"], key="Wout")
            DMA("pool", Wqr[:], wqr_d[l].rearrange("(c p) g n -> p c g n", p=128), r=[], w=["Wqr"], key="wsm0")
            DMA("pool", Wuv[:], wuv_d[l], r=[], w=["Wuv"], key="wsm1")
            DMA("pool", Wpw[:], wpw_d[l].rearrange("(c p) n -> p c n", p=128), r=[], w=["Wpw"], key="wsm2")
            for h in range(6):
                DMA("sp", T1[0:64, 0:256], wuqnT_d[l, h], r=[], w=["T1"], key="wcA")
                DMA("sp", T2[0:64, 0:128], wukT_d[l, h], r=[], w=["T2"], key="wcB")
                for c in range(2):
                    bk, bkey = nbank()
                    MM(bk[:, 0:128], T1[0:64, c * 128:(c + 1) * 128], T2[0:64, 0:128], True, True, r=["T1", "T2"], w=[bkey])
                    CP("dve", Wc[:, c, h, :], bk[:, 0:128], r=[bkey], w=["Wc"])
            if not (l == 1 and hoist_l1):
                DMA("pool", Wbuf[:, :, 0:NA], winA_d[l].rearrange("(c p) n -> p c n", p=128), r=[], w=WQ + ["WA", "WB"], key="WbufF")
            load_bc(l, 1)
            hres2 = ["hT2"] + [("GT", c) for c in range(8)]
            hbufs = [(hT, ["hT"]), (GT, hres2)]
            make_hT(cs_d, cs_res, 0, CTX, hbufs[0][0], hbufs[0][1])
            load_bc(l, 0)
            ntile = SEQ // 512
            pcs = phase_A(l, cs_d, cs_res, 0, CTX, 0, False, hbufs[0][0], hbufs[0][1])
            run_interleaved(pcs, (xs_d, xs_res, 0, hbufs[1][0], hbufs[1][1]))
            for j in range(ntile):
                cb_, cr_ = hbufs[(j + 1) % 2]
                pcs = phase_A(l, xs_d, xs_res, j * 512, 512, CTX + j * 512, True, cb_, cr_)
                if j + 1 < ntile:
                    nb_, nr_ = hbufs[j % 2]
                    run_interleaved(pcs, (xs_d, xs_res, (j + 1) * 512, nb_, nr_))
                else:
                    run_interleaved(pcs, None)
            DMA("pool", Wbuf[:, :, :], winB_d[l].rearrange("(c p) n -> p c n", p=128), r=[], w=WQ + ["WA", "WB"], key="WbufF")
            if not last:
                load_bc(l, 1)
                phase_B(l, cs_d, cs_res, ctx1_d, "ctx1D", 0, CTX, 0, False, CTX // 128, False)
                load_bc(l, 0)
            for j in range(nblk):
                hoist = None
                if l == 0 and hoist_l1 and j == nblk - 1:
                    hoist = {}
                    d_at = [2, 8, 30, 40, 74, 90]
                    c_at = [16, 28, 72, 86, 140, 154]
                    for jj in range(6):
                        hoist.setdefault(d_at[jj], []).append(lambda jj=jj: mod_chunk(1, jj, True, "dma"))
                        hoist.setdefault(c_at[jj], []).append(lambda jj=jj: mod_chunk(1, jj, True, "compute"))
                    hoist[165] = [lambda: DMA("pool", Wbuf[:, :, 0:NA], winA_d[1].rearrange("(c p) n -> p c n", p=128),
                                              r=[], w=WQ + ["WA", "WB"], key="WbufF")]
                phase_B(l, xs_d, xs_res, xo_d, xo_res, j * 512, 512, CTX + j * 512, True, NKT, last,
                        have_hT=(j > 0), next_row0=((j + 1) * 512 if j + 1 < nblk else None), hoist=hoist)
                assert not hoist
        if dump:
            dd = {}
            for nm, t, shp, dt in (("d_y", yT, [128, 8, 512], BF16), ("d_QA", QA, [128, 6, 512], BF16), ("d_QR", QRp, [128, 6, 512], BF16),
                                   ("d_QG", QG, [128, 3, 512], BF16), ("d_GT", GT, [128, 8, 512], BF16), ("d_KC", KC, [128, NKEY], BF16),
                                   ("d_KR", KR, [128, NKEY], BF16), ("d_KG", KG, [128, NKEY], BF16), ("d_VM", VM, [128, NKT, 6, 65], BF16),
                                   ("d_VG", VG, [128, NKT, 2, 65], BF16), ("d_Wc", Wc, [128, 2, 6, 128], BF16), ("d_cqn", cqn, [128, 2, 512], BF16)):
                dd[nm] = nc.dram_tensor(nm, shp, dt, kind="ExternalOutput").ap()
                allres = list(S.writers.keys())
                DMA("sp", dd[nm], t[:], r=allres, w=["outD"], key="dump_" + nm)
        S.op("sp", None, r=["outD", "x1D", "ctx1D"])
        S.emit()
    return nc


def _perm_blocks(n, blk):
    idx = np.arange(n).reshape(-1, 2, blk)
    return idx[:, ::-1, :].reshape(-1)


def _rope_tables():
    t = np.arange(SEQ)
    row = (t // 64).astype(np.float64)
    col = (t % 64).astype(np.float64)

    def tab(rdim):
        half = rdim // 2
        freqs = 10000.0 ** (-np.arange(half, dtype=np.float64) / half)
        cos = np.zeros((2 * rdim, SEQ)); sin = np.zeros((2 * rdim, SEQ))
        for a, pos in enumerate((row, col)):
            ang = pos[None, :] * freqs[:, None].astype(np.float32).astype(np.float64)
            ang = (pos.astype(np.float32)[None, :] * freqs.astype(np.float32)[:, None]).astype(np.float32)
            c = np.cos(ang); s = np.sin(ang)
            base = a * rdim
            cos[base:base + half] = c; cos[base + half:base + rdim] = c
            sin[base:base + half] = -s; sin[base + half:base + rdim] = s
        return cos.astype(np.float32), sin.astype(np.float32)

    cg, sg = tab(32)
    cm, sm = tab(16)
    tabG = np.stack([np.tile(cg, (2, 1)), np.tile(sg, (2, 1))]).astype(np.float32)
    tabM = np.stack([np.tile(cm, (3, 1)), np.tile(sm, (3, 1))]).astype(np.float32)
    return np.ascontiguousarray(tabG), np.ascontiguousarray(tabM)


def _prep_shared(norm_w, w_mod, b_mod, w_in, mla_q_norm, mla_w_uq, mla_kv_norm, mla_w_ukv, gqa_q_norm, gqa_k_norm,
                 conv_dw_w, conv_dw_b, conv_ln_w, conv_ln_b, conv_pw_w, conv_pw_b, w_out, final_norm_w):
    f = lambda a: np.ascontiguousarray(np.asarray(a, dtype=np.float32))
    w_in = f(w_in)
    p64 = _perm_blocks(64, 16)
    p32 = _perm_blocks(32, 8)
    kr = 384 + np.arange(32)
    krp = 384 + p32
    gk = 416 + np.arange(128)
    gkp = 416 + np.concatenate([p64, 64 + p64])
    colsA = np.concatenate([256 + np.arange(128), np.tile(kr, 3), np.tile(krp, 3), gk, gkp, 544 + np.arange(128), 1056 + np.arange(512)])
    assert colsA.size == NA
    gq = []
    gqp = []
    for c in range(3):
        for hq in (c, c + 3):
            gq.append(672 + hq * 64 + np.arange(64))
            gqp.append(672 + hq * 64 + p64)
    colsB = np.concatenate([np.arange(256)] + gq + gqp + [1568 + np.arange(1024)])
    assert colsB.size == NB
    w_inA = np.ascontiguousarray(w_in[:, :, colsA])
    w_inB = np.ascontiguousarray(w_in[:, :, colsB])
    wuq = f(mla_w_uq).reshape(2, 256, 6, 96)
    wuqnT = np.ascontiguousarray(wuq[:, :, :, 0:64].transpose(0, 2, 3, 1))
    wukv = f(mla_w_ukv).reshape(2, 128, 6, 128)
    wukT = np.ascontiguousarray(wukv[:, :, :, 0:64].transpose(0, 2, 3, 1))
    wuv = np.ascontiguousarray(wukv[:, :, :, 64:128].reshape(2, 128, 384))
    rope_o = wuq[:, :, :, 64:96]
    rope_p = rope_o[:, :, :, p32]
    wqr = np.stack([rope_o[:, :, 0:3].reshape(2, 256, 96), rope_o[:, :, 3:6].reshape(2, 256, 96),
                    rope_p[:, :, 0:3].reshape(2, 256, 96), rope_p[:, :, 3:6].reshape(2, 256, 96)], axis=2)
    colvec = np.zeros((2, 128, NCV), np.float32)
    qn = f(mla_q_norm)
    colvec[:, :, 0] = qn[:, 0:128]; colvec[:, :, 1] = qn[:, 128:256]
    colvec[:, :, 2] = f(mla_kv_norm)
    gqn = f(gqa_q_norm); gkn = f(gqa_k_norm)
    colvec[:, :, 3] = np.tile(gqn, (1, 2)); colvec[:, :, 4] = np.tile(gqn[:, p64], (1, 2))
    colvec[:, :, 5] = np.tile(gkn, (1, 2)); colvec[:, :, 6] = np.tile(gkn[:, p64], (1, 2))
    for c in range(2):
        sl = slice(c * 128, (c + 1) * 128)
        colvec[:, :, 7 + c] = f(conv_dw_b)[:, sl]
        colvec[:, :, 9 + c] = f(conv_ln_w)[:, sl]
        colvec[:, :, 11 + c] = f(conv_ln_b)[:, sl]
        colvec[:, :, 13 + c] = f(conv_pw_b)[:, sl]
        colvec[:, :, 15 + c * 31: 15 + (c + 1) * 31] = f(conv_dw_w)[:, :, sl].transpose(0, 2, 1)
    tabG, tabM = _rope_tables()
    return {
        "norm_w": f(norm_w), "fnw": f(final_norm_w), "w_mod": f(w_mod), "b_mod": f(b_mod),
        "w_inA": w_inA, "w_inB": w_inB, "w_out": f(w_out), "wuqnT": wuqnT, "wukT": wukT,
        "wqr": np.ascontiguousarray(wqr.astype(np.float32)), "wuv": wuv, "wpw": f(conv_pw_w), "colvec": colvec,
        "tabG": tabG, "tabM": tabM, "ident": np.eye(128, dtype=np.float32),
    }


_NC_CACHE = {}


def kernel(x, c, ctx, c_ctx, norm_w, w_mod, b_mod, w_in, mla_q_norm, mla_w_uq, mla_kv_norm, mla_w_ukv,
           gqa_q_norm, gqa_k_norm, conv_dw_w, conv_dw_b, conv_ln_w, conv_ln_b, conv_pw_w, conv_pw_b, w_out,
           final_norm_w, _debug=None):
    shared = _prep_shared(norm_w, w_mod, b_mod, w_in, mla_q_norm, mla_w_uq, mla_kv_norm, mla_w_ukv, gqa_q_norm,
                          gqa_k_norm, conv_dw_w, conv_dw_b, conv_ln_w, conv_ln_b, conv_pw_w, conv_pw_b, w_out, final_norm_w)
    x = np.asarray(x, dtype=np.float32)
    ctx = np.asarray(ctx, dtype=np.float32)
    c = np.asarray(c, dtype=np.float32)
    c_ctx = np.asarray(c_ctx, dtype=np.float32)
    nb = x.shape[0]
    batches = list(range(nb)) if _debug is None else _debug.get("batches", list(range(nb)))
    in_maps = []
    for b in batches:
        cv = np.stack([c[b], c_ctx], axis=0)
        cvT = np.ascontiguousarray(cv.reshape(2, 8, 128).transpose(2, 1, 0))
        m = dict(shared)
        m["x"] = np.ascontiguousarray(x[b])
        m["ctx"] = np.ascontiguousarray(ctx[b])
        m["cvT"] = cvT
        in_maps.append(m)
    if _debug is None:
        if "main" not in _NC_CACHE:
            _NC_CACHE["main"] = build_program()
        nc = _NC_CACHE["main"]
    else:
        nc = build_program(debug=True, **_debug.get("build", {}))
    res = run_bass_kernel_spmd(nc, in_maps, core_ids=list(range(len(batches))))
    if _debug is not None:
        return [r for r in res.results]
    return np.stack([np.asarray(r["out"]) for r in res.results], axis=0).astype(np.float32)
```

```python
import numpy as np
from contextlib import ExitStack
import concourse.bass as bass
import concourse.mybir as mybir
from concourse.bass_utils import run_bass_kernel_spmd

F32 = mybir.dt.float32
BF16 = mybir.dt.bfloat16
AF = mybir.ActivationFunctionType
ALU = mybir.AluOpType

D = 1024
SEQ = 4096
CTX = 256
NKEY = SEQ + CTX
NKT = NKEY // 128
EPS = 1e-6
NA = 1216
NB = 2048
NCV = 77
A_CKV, A_KR3, A_KR3P, A_GK, A_GKP, A_GV, A_CA, A_CG = 0, 128, 224, 320, 448, 576, 704, 960
B_CQ, B_GQ, B_GQP, B_GATE = 0, 256, 640, 1024


class _Op:
    __slots__ = ("eng", "fn", "deps", "sig", "val", "dkey", "is_dma", "sem")


class Sched:
    ENGS = ("pe", "act", "dve", "pool", "sp")

    def __init__(self, nc, stack):
        self.nc = nc
        self.stack = stack
        self.streams = {e: [] for e in self.ENGS}
        self.writers = {}
        self.readers = {}
        self.dma_cnt = {}
        self.dma_sems = {}
        self.eng_sems = {}
        self.psum_res = set()

    def op(self, eng, fn, r=(), w=(), dma=None):
        o = _Op()
        o.eng = eng; o.fn = fn; o.sig = False; o.val = None
        o.is_dma = dma is not None; o.dkey = dma; o.sem = None
        deps = []

        def add(d, kind):
            if d is o:
                return
            same = (d.eng == eng) and (not d.is_dma) and (not o.is_dma)
            if same:
                if eng == "pe" or kind == "RR":
                    return
            deps.append(d)

        for res in r:
            ws = self.writers.get(res)
            if ws:
                for d in ws.values():
                    add(d, "RAW")
            if res in self.psum_res:
                rs = self.readers.get(res)
                if rs:
                    for d in rs.values():
                        if d.eng != eng:
                            add(d, "RR")
        for res in w:
            ws = self.writers.get(res)
            if ws:
                for d in ws.values():
                    add(d, "WAW")
            rs = self.readers.get(res)
            if rs:
                for d in rs.values():
                    add(d, "WAR")
        k = ("dma", dma) if o.is_dma else eng
        for res in r:
            self.readers.setdefault(res, {})[k] = o
        for res in w:
            self.writers.setdefault(res, {})[k] = o
        if o.is_dma:
            n = self.dma_cnt.get(dma, 0) + 1
            self.dma_cnt[dma] = n
            o.val = 16 * n
            o.sig = True
        for d in deps:
            d.sig = True
        o.deps = deps
        self.streams[eng].append(o)
        return o

    def emit(self):
        nc = self.nc
        for e in self.ENGS:
            self.eng_sems[e] = self.stack.enter_context(nc.semaphore("es_" + e))
        for dk in self.dma_cnt:
            self.dma_sems[dk] = self.stack.enter_context(nc.semaphore("ds_%d" % len(self.dma_sems)))
        for e in self.ENGS:
            c = 0
            for o in self.streams[e]:
                if o.is_dma:
                    o.sem = self.dma_sems[o.dkey]
                else:
                    o.sem = self.eng_sems[e]
                    if o.sig:
                        c += 1
                        o.val = c
        block = self.stack.enter_context(nc.Block())

        def run(ename):
            def body(eng):
                waited = {}
                for o in self.streams[ename]:
                    need = {}
                    for d in o.deps:
                        if waited.get(d.sem, 0) < d.val and need.get(d.sem, 0) < d.val:
                            need[d.sem] = d.val
                    for sem, val in need.items():
                        eng.wait_ge(sem, val)
                        waited[sem] = val
                    if o.fn is None:
                        assert not o.sig
                        continue
                    ins = o.fn(eng)
                    if o.is_dma:
                        ins.then_inc(o.sem, 16)
                    elif o.sig:
                        ins.then_inc(o.sem, 1)
            return body

        block.tensor(run("pe"))
        block.scalar(run("act"))
        block.vector(run("dve"))
        block.gpsimd(run("pool"))
        block.sync(run("sp"))


def build_program(debug=False, n_layers=2, nblk=SEQ // 512, dump=False):
    nc = bass.Bass("TRN2", target_bir_lowering=False)

    def din(name, shape, dt=F32):
        return nc.dram_tensor(name, list(shape), dt, kind="ExternalInput").ap()

    x_d = din("x", [SEQ, D])
    ctx_d = din("ctx", [CTX, D])
    cvT_d = din("cvT", [128, 8, 2])
    normw_d = din("norm_w", [2, D])
    fnw_d = din("fnw", [D])
    wmod_d = din("w_mod", [2, D, 3 * D])
    bmod_d = din("b_mod", [2, 3 * D])
    winA_d = din("w_inA", [2, D, NA])
    winB_d = din("w_inB", [2, D, NB])
    wout_d = din("w_out", [2, D, D])
    wuqnT_d = din("wuqnT", [2, 6, 64, 256])
    wukT_d = din("wukT", [2, 6, 64, 128])
    wqr_d = din("wqr", [2, 256, 4, 96])
    wuv_d = din("wuv", [2, 128, 384])
    wpw_d = din("wpw", [2, 256, 256])
    colvec_d = din("colvec", [2, 128, NCV])
    tabG_d = din("tabG", [2, 128, SEQ])
    tabM_d = din("tabM", [2, 96, SEQ])
    ident_d = din("ident", [128, 128])
    out_d = nc.dram_tensor("out", [SEQ, D], F32, kind="ExternalOutput").ap()
    ikind = "ExternalOutput" if debug else "Internal"
    x1_d = nc.dram_tensor("x1", [SEQ, D], F32, kind=ikind).ap()
    ctx1_d = nc.dram_tensor("ctx1", [CTX, D], F32, kind=ikind).ap()
    glu_d = nc.dram_tensor("gluD", [2, 128, NKEY], BF16, kind="Internal").ap()
    mod_d = nc.dram_tensor("modD", [2, 2, 3 * D], F32, kind="Internal").ap()

    with ExitStack() as st:
        S = Sched(nc, st)

        def sb(name, shape, dt):
            return st.enter_context(nc.sbuf_tensor("sb_" + name, list(shape), dt))

        def ps(name, shape, dt=F32):
            return st.enter_context(nc.psum_tensor("ps_" + name, list(shape), dt))

        Wbuf = sb("Wbuf", [128, 8, NB], BF16)
        Wout = sb("Wout", [128, 8, D], BF16)
        Wc = sb("Wc", [128, 2, 6, 128], BF16)
        Wqr = sb("Wqr", [128, 2, 4, 96], BF16)
        Wuv = sb("Wuv", [128, 384], BF16)
        Wpw = sb("Wpw", [128, 2, 256], BF16)
        KC = sb("KC", [128, NKEY], BF16)
        KR = sb("KR", [128, NKEY], BF16)
        KG = sb("KG", [128, NKEY], BF16)
        VM = sb("VM", [128, NKT, 6, 65], BF16)
        VG = sb("VG", [128, NKT, 2, 65], BF16)
        gmod_bc = sb("gmod_bc", [128, D], F32)
        shift_bc = sb("shift_bc", [128, D], F32)
        gate_bc = sb("gate_bc", [128, D], F32)
        fnw_bc = sb("fnw_bc", [128, D], F32)
        XT = [sb("xt%d" % i, [128, D], F32) for i in range(2)]
        hb = sb("hb", [128, D], BF16)
        hT = sb("hT", [128, 8, 512], BF16)
        yT = hT
        QA = sb("QA", [128, 6, 512], BF16)
        QRp = sb("QRp", [128, 6, 512], BF16)
        QG = sb("QG", [128, 3, 512], BF16)
        GT = sb("GT", [128, 8, 512], BF16)
        cqn = sb("cqn", [128, 2, 512], BF16)
        PT = [sb("pt%d" % i, [128, 2, 512], BF16) for i in range(2)]
        gluw = sb("gluw", [128, 2, 542], BF16)
        cacc = sb("cacc", [128, 2, 512], F32)
        cbf = sb("cbf", [128, 2, 512], BF16)
        sq = sb("sq", [128, 2, 512], BF16)
        tabGs = sb("tabGs", [128, 2, 512], F32)
        tabMs = sb("tabMs", [128, 2, 512], F32)
        T1 = sb("T1", [128, 512], F32)
        T2 = sb("T2", [128, 512], F32)
        T3 = sb("T3", [128, 512], F32)
        T4 = sb("T4", [128, 512], F32)
        ident = sb("ident", [128, 128], BF16)
        M128 = sb("M128", [128, 128], BF16)
        M256 = sb("M256", [128, 128], BF16)
        M64 = sb("M64", [128, 128], BF16)
        ones32 = sb("ones32", [128, 128], F32)
        colv = sb("colv", [128, NCV], F32)
        ss = sb("ss", [128, 8], F32)
        cvs = sb("cvs", [128, 8, 2], F32)
        cv32 = sb("cv32", [128, 8, 2], F32)

        S0 = ps("S0", [128, 1024]); S1 = ps("S1", [128, 1024])
        O0 = ps("O0", [128, 512]); O1 = ps("O1", [128, 512])
        G0 = ps("G0", [128, 512]); G1 = ps("G1", [128, 512])
        banks = [(G0[:, :], "G0"), (G1[:, :], "G1"), (O0[:, :], "O0"), (O1[:, :], "O1"),
                 (S0[:, 0:512], "S0a"), (S0[:, 512:1024], "S0b"), (S1[:, 0:512], "S1a"), (S1[:, 512:1024], "S1b")]
        S.psum_res.update(k for _, k in banks)
        bank_ctr = [0]

        def nbank():
            b = banks[bank_ctr[0] % 8]
            bank_ctr[0] += 1
            return b

        def MM(out, lhsT, rhs, start, stop, r, w):
            S.op("pe", lambda e: e.matmul(out, lhsT=lhsT, rhs=rhs, start=start, stop=stop), r=r, w=w)

        def MMG(out, pairs, r, w):
            n = len(pairs)
            for i, (l, rr) in enumerate(pairs):
                MM(out, l, rr, i == 0, i == n - 1, r, w)

        def ACT(out, in_, func, r, w, bias=None, scale=None, accum=None):
            kw = {}
            if bias is not None:
                kw["bias"] = bias
            if scale is not None:
                kw["scale"] = scale
            if accum is not None:
                kw["accum_out"] = accum
            S.op("act", lambda e: e.activation(out=out, in_=in_, func=func, **kw), r=r, w=w)

        def TT(eng, out, in0, in1, op, r, w):
            S.op(eng, lambda e: e.tensor_tensor(out=out, in0=in0, in1=in1, op=op), r=r, w=w)

        def STT(eng, out, in0, scalar, in1, op0, op1, r, w):
            S.op(eng, lambda e: e.scalar_tensor_tensor(out=out, in0=in0, scalar=scalar, in1=in1, op0=op0, op1=op1), r=r, w=w)

        def TS(eng, out, in0, s1, s2, op0, op1, r, w):
            if s2 is None:
                S.op(eng, lambda e: e.tensor_scalar(out=out, in0=in0, scalar1=s1, scalar2=None, op0=op0), r=r, w=w)
            else:
                S.op(eng, lambda e: e.tensor_scalar(out=out, in0=in0, scalar1=s1, scalar2=s2, op0=op0, op1=op1), r=r, w=w)

        def CP(eng, out, in_, r, w):
            if eng == "act":
                S.op("act", lambda e: e.copy(out=out, in_=in_), r=r, w=w)
            else:
                S.op(eng, lambda e: e.tensor_copy(out=out, in_=in_), r=r, w=w)

        def TR(out, in_, r, w):
            S.op("pe", lambda e: e.transpose(out, in_, ident[:]), r=r, w=w)

        def RECIP(out, in_, r, w):
            S.op("dve", lambda e: e.reciprocal(out=out, in_=in_), r=r, w=w)

        def _l(x):
            return x if isinstance(x, list) else [x]

        def MSET(eng, ap, val, w):
            S.op(eng, lambda e: e.memset(ap, val), w=w)

        def DMA(q, out, in_, r, w, key):
            S.op(q, lambda e: e.dma_start(out=out, in_=in_), r=r, w=w, dma=key)

        def rstd_from_mean(out, in_, r, w, scale=1.0):
            ACT(out, in_, AF.Ln, r=r, w=w, bias=EPS, scale=scale)
            ACT(out, out, AF.Exp, r=w, w=w, scale=-0.5)

        DMA("pool", ident[:], ident_d, r=[], w=["ident"], key="cst")
        MSET("dve", M128[:], 1.0 / 128, ["M128"])
        MSET("dve", M256[:], 1.0 / 256, ["M256"])
        MSET("dve", M64[:], 0.0, ["M64"])
        MSET("dve", M64[0:64, 0:64], 1.0 / 64, ["M64"])
        MSET("dve", M64[64:128, 64:128], 1.0 / 64, ["M64"])
        MSET("dve", ones32[:], 1.0, ["ones32"])
        MSET("pool", QRp[:], 0.0, ["QR"])
        MSET("pool", VM[:], 1.0, [("VM", i) for i in range(NKT)])
        MSET("pool", VG[:], 1.0, [("VG", i) for i in range(NKT)])
        DMA("sp", fnw_bc[:], fnw_d.partition_broadcast(128), r=[], w=["fnw_bc"], key="cst2")
        DMA("sp", cv32[:], cvT_d, r=[], w=["cv32"], key="cst3")
        ACT(cvs[:], cv32[:], AF.Silu, r=["cv32"], w=["cvs"])

        WQ = ["Wq0", "Wq1", "Wq2", "Wq3"]
        Wf = Wbuf[:].bitcast(F32)

        hoist_l1 = (n_layers == 2 and nblk == SEQ // 512)

        def mod_chunk(l, j, hoisted, part="all"):
            wm = wmod_d[l].rearrange("(c p) n -> p c n", p=128)
            hf = j % 2
            cs = slice(j * 512, (j + 1) * 512)
            wkeys = [WQ[2 * hf], WQ[2 * hf + 1]]
            if hoisted:
                a1, k1 = XT[0][0:2, 0:512], ("xt", 0)
                a2, k2 = XT[0][0:2, 512:1024], ("xt", 0)
                a3, k3 = XT[1][0:2, 0:512], ("xt", 1)
                bk, bkey = G1[:, :], "G1"
                wdma = wkeys + ["WA", "WB"]
            else:
                a1, k1 = T1[0:2, :], "T1"
                a2, k2 = T2[0:2, :], "T2"
                a3, k3 = T3[0:2, :], "T3"
                bk, bkey = nbank()
                wdma = wkeys
            if part in ("all", "dma"):
                DMA("sp", Wf[:, :, hf * 512:(hf + 1) * 512], wm[:, :, cs], r=[], w=wdma, key="Wm%d" % hf)
                if part == "dma":
                    return
            DMA("sp", a2, bmod_d[l, cs].partition_broadcast(2), r=[], w=[k2], key="bm")
            MMG(bk[0:2, :], [(cvs[:, k, :], Wf[:, k, hf * 512:(hf + 1) * 512]) for k in range(8)],
                r=["cvs"] + wkeys, w=[bkey])
            TT("dve", a1, bk[0:2, :], a2, ALU.add, r=[bkey, k2], w=[k1])
            if j in (2, 3):
                DMA("sp", a3, normw_d[l, (j - 2) * 512:(j - 1) * 512].partition_broadcast(2), r=[], w=[k3], key="nwc")
                TS("dve", a1, a1, 1.0, None, ALU.add, None, r=[k1], w=[k1])
                TT("dve", a1, a1, a3, ALU.mult, r=[k1, k3], w=[k1])
            DMA("sp", mod_d[l, :, cs], a1, r=[k1], w=["modD"], key="modst")

        for l in range(2):
            if l == 1 and hoist_l1:
                continue
            for j in range(6):
                mod_chunk(l, j, False)

        def load_bc(l, v):
            DMA("sp", shift_bc[:], mod_d[l, v, 0:D].partition_broadcast(128), r=["modD"], w=["shift_bc"], key="bc0")
            DMA("sp", gmod_bc[:], mod_d[l, v, D:2 * D].partition_broadcast(128), r=["modD"], w=["gmod_bc"], key="bc1")
            DMA("sp", gate_bc[:], mod_d[l, v, 2 * D:3 * D].partition_broadcast(128), r=["modD"], w=["gate_bc"], key="bc2")

        xt_ctr = [0]

        HB = {"buf": None, "res": None}

        stat_ctr = [0]
        hb_ctr = [0]
        PT0f = PT[0][:, :, :].rearrange("p a n -> p (a n)")
        PT1f = PT[1][:, :, :].rearrange("p a n -> p (a n)")

        def rms_stats(xt, xr):
            k = stat_ctr[0] % 2
            stat_ctr[0] += 1
            c0 = 4 * k
            ACT(PT0f, xt[:], AF.Square, r=[xr], w=[("pt", 0), ("ss", k, 0)], accum=ss[:, c0:c0 + 1])
            ACT(ss[:, c0 + 1:c0 + 2], ss[:, c0:c0 + 1], AF.Ln, r=[("ss", k, 0)], w=[("ss", k, 1)], bias=EPS, scale=1.0 / D)
            ACT(ss[:, c0 + 2:c0 + 3], ss[:, c0 + 1:c0 + 2], AF.Exp, r=[("ss", k, 1)], w=[("ss", k, 2)], scale=-0.5)
            return ss[:, c0 + 2:c0 + 3], ("ss", k, 2)

        def front_sub(src_d, srcres, row0, s, bank=None):
            i = xt_ctr[0] % 2
            xt_ctr[0] += 1
            xt = XT[i]
            xr = ("xt", i)
            DMA("sp", xt[:], src_d[row0 + s * 128: row0 + (s + 1) * 128, :], r=[srcres], w=[xr], key="xt%d" % i)
            rs, rk = rms_stats(xt, xr)
            STT("dve", xt[:], xt[:], rs, gmod_bc[:], ALU.mult, ALU.mult, r=[xr, rk, "gmod_bc"], w=[xr])
            k = hb_ctr[0] % 2
            hb_ctr[0] += 1
            hbx, hk = (hb[:, :], "hb") if k == 0 else (PT1f, ("pt", 1))
            TT("dve", hbx, xt[:], shift_bc[:], ALU.add, r=[xr, "shift_bc"], w=[hk])
            bk, bkey = bank if bank is not None else nbank()
            bkb = bk.bitcast(BF16)
            for c in range(8):
                TR(bkb[:, c * 128:(c + 1) * 128], hbx[:, c * 128:(c + 1) * 128], r=[hk, "ident"], w=[bkey])
            return bkb, bkey

        def front_chain(src_d, srcres, row0, s):
            i = xt_ctr[0] % 2
            xt_ctr[0] += 1
            xt = XT[i]
            xr = ("xt", i)
            DMA("sp", xt[:], src_d[row0 + s * 128: row0 + (s + 1) * 128, :], r=[srcres], w=[xr], key="xt%d" % i)
            rs, rk = rms_stats(xt, xr)
            STT("dve", xt[:], xt[:], rs, gmod_bc[:], ALU.mult, ALU.mult, r=[xr, rk, "gmod_bc"], w=[xr])
            k = hb_ctr[0] % 2
            hb_ctr[0] += 1
            hbx, hk = (hb[:, :], "hb") if k == 0 else (PT1f, ("pt", 1))
            TT("dve", hbx, xt[:], shift_bc[:], ALU.add, r=[xr, "shift_bc"], w=[hk])
            return hbx, hk

        def front_tr_evac(hbx, hk, hbuf, hres, s):
            bk, bkey = nbank()
            bkb = bk.bitcast(BF16)
            for c in range(8):
                TR(bkb[:, c * 128:(c + 1) * 128], hbx[:, c * 128:(c + 1) * 128], r=[hk, "ident"], w=[bkey])
            evac_sub(hbuf, hres, s, bkb, bkey)

        def evac_sub(hbuf, hres, s, bkb, bkey):
            CP("dve", hbuf[:, :, s * 128:(s + 1) * 128], bkb[:, 0:1024].rearrange("p (c t) -> p c t", c=8), r=[bkey], w=hres)

        def make_hT(src_d, srcres, row0, NT, hbuf=None, hres=None):
            hbuf = hT if hbuf is None else hbuf
            hres = ["hT"] if hres is None else hres
            for s in range(NT // 128):
                bkb, bkey = front_sub(src_d, srcres, row0, s)
                evac_sub(hbuf, hres, s, bkb, bkey)

        def proj(cols, m, NT, wres, hbuf=None, hres=None):
            hbuf = hT if hbuf is None else hbuf
            hres = ["hT"] if hres is None else hres
            bk, bkey = nbank()
            MMG(bk[0:m, 0:NT], [(Wbuf[:, k, cols:cols + m], hbuf[:, k, 0:NT]) for k in range(8)], r=[wres] + hres, w=[bkey])
            return bk, bkey

        def load_tables(t0, NT):
            DMA("sp", tabGs[:, :, 0:NT], tabG_d[:, :, t0:t0 + NT].rearrange("a p n -> p a n"), r=[], w=["tabGs"], key="tabG")
            DMA("sp", tabMs[0:96, :, 0:NT], tabM_d[:, :, t0:t0 + NT].rearrange("a p n -> p a n"), r=[], w=["tabMs", "tabMs1"], key="tabM")

        def head_norm_rope(o_bk, o_key, p_bk, p_key, g_col, gp_col, NT, rope, dst, dst_res, pre_squared=False):
            if not pre_squared:
                ACT(sq[:, 0, 0:NT], o_bk[:, 0:NT], AF.Square, r=[o_key], w=["sq"])
            mb, mkey = nbank()
            MM(mb[:, 0:NT], M64[:], sq[:, 0, 0:NT], True, True, r=["M64", "sq"], w=[mkey])
            rstd_from_mean(T3[:, 0:NT], mb[:, 0:NT], r=[mkey], w=["T3"])
            if rope:
                STT("dve", T1[:, 0:NT], o_bk[:, 0:NT], colv[:, g_col:g_col + 1], tabGs[:, 0, 0:NT], ALU.mult, ALU.mult,
                    r=[o_key, "colv", "tabGs"], w=["T1"])
                STT("dve", T2[:, 0:NT], p_bk[:, 0:NT], colv[:, gp_col:gp_col + 1], tabGs[:, 1, 0:NT], ALU.mult, ALU.mult,
                    r=[p_key, "colv", "tabGs"], w=["T2"])
                TT("dve", T1[:, 0:NT], T1[:, 0:NT], T2[:, 0:NT], ALU.add, r=["T1", "T2"], w=["T1"])
                TT("dve", dst, T1[:, 0:NT], T3[:, 0:NT], ALU.mult, r=["T1", "T3"], w=_l(dst_res))
            else:
                STT("dve", dst, o_bk[:, 0:NT], colv[:, g_col:g_col + 1], T3[:, 0:NT], ALU.mult, ALU.mult,
                    r=[o_key, "colv", "T3"], w=_l(dst_res))

        def rope96(o_bk, o_key, p_bk, p_key, NT, rope, dst, dst_res):
            if rope:
                TT("dve", T1[0:96, 0:NT], o_bk[0:96, 0:NT], tabMs[0:96, 0, 0:NT], ALU.mult, r=[o_key, "tabMs"], w=["T1"])
                TT("dve", T2[0:96, 0:NT], p_bk[0:96, 0:NT], tabMs[0:96, 1, 0:NT], ALU.mult, r=[p_key, "tabMs", "tabMs1"], w=["T2"])
                if isinstance(dst, list):
                    for jj in range(3):
                        TT("dve", dst[jj], T1[32 * jj:32 * jj + 32, 0:NT], T2[32 * jj:32 * jj + 32, 0:NT], ALU.add, r=["T1", "T2"], w=_l(dst_res))
                else:
                    TT("dve", dst, T1[0:96, 0:NT], T2[0:96, 0:NT], ALU.add, r=["T1", "T2"], w=_l(dst_res))
            else:
                if isinstance(dst, list):
                    for jj in range(3):
                        CP("dve", dst[jj], o_bk[32 * jj:32 * jj + 32, 0:NT], r=[o_key], w=_l(dst_res))
                else:
                    CP("dve", dst, o_bk[0:96, 0:NT], r=[o_key], w=_l(dst_res))

        def phase_A(l, src_d, srcres, row0, NT, key0, rope, hbuf, hres):
            kt0 = key0 // 128
            nsub = NT // 128

            def pj(cols, m, NT, wres):
                return proj(cols, m, NT, wres, hbuf, hres)

            st = {}

            def pieceA():
                if rope:
                    load_tables(row0, NT)
                cb, ckey = pj(A_CKV, 128, NT, "WA")
                ACT(sq[:, 0, 0:NT], cb[:, 0:NT], AF.Square, r=[ckey], w=["sq"])
                st["ckv"] = (cb, ckey)

            def pieceB():
                cb, ckey = st["ckv"]
                mb, mkey = nbank()
                MM(mb[:, 0:NT], M128[:], sq[:, 0, 0:NT], True, True, r=["M128", "sq"], w=[mkey])
                rstd_from_mean(T3[:, 0:NT], mb[:, 0:NT], r=[mkey], w=["T3"])
                STT("dve", KC[:, key0:key0 + NT], cb[:, 0:NT], colv[:, 2:3], T3[:, 0:NT], ALU.mult, ALU.mult,
                    r=[ckey, "colv", "T3"], w=[("KC", kt0 + i) for i in range(nsub)])
                kb, kkey = pj(A_KR3, 96, NT, "WA")
                if rope:
                    kpb, kpkey = pj(A_KR3P, 96, NT, "WA")
                else:
                    kpb, kpkey = None, None
                rope96(kb, kkey, kpb, kpkey, NT, rope, KR[0:96, key0:key0 + NT], [("KR", kt0 + i) for i in range(nsub)])
                for s in range(nsub):
                    vb, vkey = nbank()
                    MM(vb[:, 0:384], KC[:, key0 + s * 128: key0 + (s + 1) * 128], Wuv[:], True, True, r=[("KC", kt0 + s), "Wuv"], w=[vkey])
                    CP("act", VM[:, kt0 + s, :, 0:64], vb[:, 0:384].rearrange("p (h d) -> p h d", h=6), r=[vkey], w=[("VM", kt0 + s)])

            def pieceC():
                gb, gkey = pj(A_GK, 128, NT, "WA")
                if rope:
                    gpb, gpkey = pj(A_GKP, 128, NT, "WA")
                else:
                    gpb, gpkey = None, None
                ACT(sq[:, 0, 0:NT], gb[:, 0:NT], AF.Square, r=[gkey], w=["sq"])
                st["gk"] = (gb, gkey, gpb, gpkey)

            def pieceD():
                gb, gkey, gpb, gpkey = st["gk"]
                head_norm_rope(gb, gkey, gpb, gpkey, 5, 6, NT, rope, KG[:, key0:key0 + NT], [("KG", kt0 + i) for i in range(nsub)],
                               pre_squared=True)
                for s in range(nsub):
                    vb, vkey = nbank()
                    MMG(vb[:, 0:128], [(hbuf[:, k, s * 128:(s + 1) * 128], Wbuf[:, k, A_GV:A_GV + 128]) for k in range(8)],
                        r=hres + ["WA"], w=[vkey])
                    CP("act", VG[:, kt0 + s, :, 0:64], vb[:, 0:128].rearrange("p (h d) -> p h d", h=2), r=[vkey], w=[("VG", kt0 + s)])
                for c in range(2):
                    ab, akey = pj(A_CA + c * 128, 128, NT, "WA")
                    gb2, gkey2 = pj(A_CG + c * 128, 128, NT, "WA")
                    ACT(T1[:, 0:NT], gb2[:, 0:NT], AF.Sigmoid, r=[gkey2], w=["T1"])
                    TT("dve", cbf[:, c, 0:NT], ab[:, 0:NT], T1[:, 0:NT], ALU.mult, r=[akey, "T1"], w=["cbf"])
                DMA("pool", glu_d[:, :, key0:key0 + NT].rearrange("c p n -> p c n"), cbf[:, :, 0:NT], r=["cbf"], w=["gluD"], key="glust")

            return [pieceA, pieceB, pieceC, pieceD]

        def run_interleaved(pieces, front):
            for s in range(4):
                h = None
                if front is not None:
                    h = front_chain(front[0], front[1], front[2], s)
                pieces[s]()
                if front is not None:
                    front_tr_evac(h[0], h[1], front[3], front[4], s)

        def phase_B(l, src_d, srcres, dst_d, dstres, row0, NT, key0, rope, nkt, last, have_hT=False, next_row0=None, hoist=None):
            nsub = NT // 128
            if not have_hT:
                make_hT(src_d, srcres, row0, NT)
            if rope:
                load_tables(row0, NT)
            seq0 = 0 if not rope else CTX
            seqn = CTX if not rope else SEQ
            lo = key0 - 15
            hi = key0 + NT + 15
            clo = max(lo, seq0)
            chi = min(hi, seq0 + seqn)
            if clo > lo:
                MSET("pool", gluw[:, :, 0:clo - lo], 0.0, ["gluw"])
            if chi < hi:
                MSET("pool", gluw[:, :, NT + 30 - (hi - chi):NT + 30], 0.0, ["gluw"])
            DMA("sp", gluw[:, :, clo - lo:chi - lo], glu_d[:, :, clo:chi].rearrange("c p n -> p c n"), r=["gluD"], w=["gluw"], key="gluw")
            cqb = [proj(B_CQ + c * 128, 128, NT, "WB") for c in range(2)]
            for c in range(2):
                ACT(sq[:, c, 0:NT], cqb[c][0][:, 0:NT], AF.Square, r=[cqb[c][1]], w=["sq"])

            def cq_tail():
                mb, mkey = nbank()
                MMG(mb[:, 0:NT], [(M256[:], sq[:, c, 0:NT]) for c in range(2)], r=["M256", "sq"], w=[mkey])
                rstd_from_mean(T3[:, 0:NT], mb[:, 0:NT], r=[mkey], w=["T3"])
                for c in range(2):
                    STT("dve", cqn[:, c, 0:NT], cqb[c][0][:, 0:NT], colv[:, c:c + 1], T3[:, 0:NT], ALU.mult, ALU.mult,
                        r=[cqb[c][1], "colv", "T3"], w=["cqn"])

            def qa_qr():
                for h in range(6):
                    qb, qkey = nbank()
                    MMG(qb[:, 0:NT], [(Wc[:, c, h, :], cqn[:, c, 0:NT]) for c in range(2)], r=["Wc", "cqn"], w=[qkey])
                    CP("act" if h % 2 else "dve", QA[:, h, 0:NT], qb[:, 0:NT], r=[qkey], w=["QA"])
                for g in range(2):
                    ob, okey = nbank()
                    MMG(ob[0:96, 0:NT], [(Wqr[:, c, g, :], cqn[:, c, 0:NT]) for c in range(2)], r=["Wqr", "cqn"], w=[okey])
                    if rope:
                        pb, pkey = nbank()
                        MMG(pb[0:96, 0:NT], [(Wqr[:, c, 2 + g, :], cqn[:, c, 0:NT]) for c in range(2)], r=["Wqr", "cqn"], w=[pkey])
                    else:
                        pb, pkey = None, None
                    rope96(ob, okey, pb, pkey, NT, rope, [QRp[32 * jj:32 * jj + 32, 3 * g + jj, 0:NT] for jj in range(3)], "QR")

            if rope:
                qslots = [(cbf[:, 0, 0:NT], "cbf"), (cbf[:, 1, 0:NT], "cbf"), (sq[:, 0, 0:NT], "sq")]

                def qg_proj(c):
                    ob, okey = proj(B_GQ + c * 128, 128, NT, "WB")
                    pb, pkey = proj(B_GQP + c * 128, 128, NT, "WB")
                    ACT(qslots[c][0], ob[:, 0:NT], AF.Square, r=[okey], w=[qslots[c][1]])
                    return (ob, okey, pb, pkey)

                def qg_tail(c, pr):
                    ob, okey, pb, pkey = pr
                    mb, mkey = nbank()
                    MM(mb[:, 0:NT], M64[:], qslots[c][0], True, True, r=["M64", qslots[c][1]], w=[mkey])
                    rstd_from_mean(T3[:, 0:NT], mb[:, 0:NT], r=[mkey], w=["T3"])
                    STT("dve", T1[:, 0:NT], ob[:, 0:NT], colv[:, 3:4], tabGs[:, 0, 0:NT], ALU.mult, ALU.mult, r=[okey, "colv", "tabGs"], w=["T1"])
                    STT("dve", T2[:, 0:NT], pb[:, 0:NT], colv[:, 4:5], tabGs[:, 1, 0:NT], ALU.mult, ALU.mult, r=[pkey, "colv", "tabGs"], w=["T2"])
                    TT("dve", T1[:, 0:NT], T1[:, 0:NT], T2[:, 0:NT], ALU.add, r=["T1", "T2"], w=["T1"])
                    TT("dve", QG[:, c, 0:NT], T1[:, 0:NT], T3[:, 0:NT], ALU.mult, r=["T1", "T3"], w=["QG"])

                pr0 = qg_proj(0)
                pr1 = qg_proj(1)
                cq_tail()
                qg_tail(0, pr0)
                qg_tail(1, pr1)
                qa_qr()
                pr2 = qg_proj(2)
                qg_tail(2, pr2)
            else:
                cq_tail()
                qa_qr()
                for c in range(3):
                    ob, okey = proj(B_GQ + c * 128, 128, NT, "WB")
                    head_norm_rope(ob, okey, None, None, 3, 4, NT, rope, QG[:, c, 0:NT], "QG")
            for c in range(8):
                gb, gkey = proj(B_GATE + c * 128, 128, NT, "WB")
                ACT(GT[:, c, 0:NT], gb[:, 0:NT], AF.Silu, r=[gkey], w=[("GT", c)])
            interleave = (NT == 512)
            cchunks = []

            def ch_conv_tap(c, k):
                if k == 0:
                    TS("dve", cacc[:, c, 0:NT], gluw[:, c, 0:NT], colv[:, 15 + c * 31: 16 + c * 31], colv[:, 7 + c:8 + c], ALU.mult, ALU.add,
                       r=["gluw", "colv"], w=[("cacc", c)])
                else:
                    STT("dve", cacc[:, c, 0:NT], gluw[:, c, k:k + NT], colv[:, 15 + c * 31 + k: 16 + c * 31 + k], cacc[:, c, 0:NT],
                        ALU.mult, ALU.add, r=["gluw", "colv", ("cacc", c)], w=[("cacc", c)])

            X1 = tabGs[:, 0, 0:NT]; X2 = tabGs[:, 1, 0:NT]; X3 = tabMs[:, 0, 0:NT]; X4 = tabMs[:, 1, 0:NT]

            def cbank():
                return (G1[:, :], "G1") if interleave else nbank()

            def ch_cs_cast(c):
                CP("dve", cbf[:, c, 0:NT], cacc[:, c, 0:NT], r=[("cacc", c)], w=["cbf"])
                TT("dve", sq[:, c, 0:NT], cacc[:, c, 0:NT], cacc[:, c, 0:NT], ALU.mult, r=[("cacc", c)], w=["sq"])

            def ch_cs_mean():
                bk, bkey = cbank()
                MMG(bk[:, 0:NT], [(M256[:], cbf[:, c, 0:NT]) for c in range(2)], r=["M256", "cbf"], w=[bkey])
                CP("dve", X1, bk[:, 0:NT], r=[bkey], w=["tabGs"])
                TT("dve", X2, X1, X1, ALU.mult, r=["tabGs"], w=["tabGs"])

            def ch_cs_var():
                bk, bkey = cbank()
                MMG(bk[:, 0:NT], [(M256[:], sq[:, c, 0:NT]) for c in range(2)], r=["M256", "sq"], w=[bkey])
                TT("dve", X2, bk[:, 0:NT], X2, ALU.subtract, r=[bkey, "tabGs"], w=["tabGs"])
                TS("dve", X2, X2, 0.0, None, ALU.max, None, r=["tabGs"], w=["tabGs"])

            def ch_cs_rstd1():
                ACT(X3, X2, AF.Ln, r=["tabGs"], w=["tabMs"], bias=EPS)

            def ch_cs_rstd2():
                ACT(X3, X3, AF.Exp, r=["tabMs"], w=["tabMs"], scale=-0.5)

            def ch_cs_norm(c):
                TT("dve", cacc[:, c, 0:NT], cacc[:, c, 0:NT], X1, ALU.subtract, r=[("cacc", c), "tabGs"], w=[("cacc", c)])
                TT("dve", cacc[:, c, 0:NT], cacc[:, c, 0:NT], X3, ALU.mult, r=[("cacc", c), "tabMs"], w=[("cacc", c)])
                TS("dve", cacc[:, c, 0:NT], cacc[:, c, 0:NT], colv[:, 9 + c:10 + c], colv[:, 11 + c:12 + c], ALU.mult, ALU.add,
                   r=[("cacc", c), "colv"], w=[("cacc", c)])

            def ch_cs_silu_a(c):
                ACT(X4, cacc[:, c, 0:NT], AF.Exp, r=[("cacc", c)], w=["tabMs1"], scale=-1.0)

            def ch_cs_silu_b(c):
                TS("dve", X4, X4, 1.0, None, ALU.add, None, r=["tabMs1"], w=["tabMs1"])
                RECIP(X4, X4, r=["tabMs1"], w=["tabMs1"])
                TT("dve", cbf[:, c, 0:NT], cacc[:, c, 0:NT], X4, ALU.mult, r=[("cacc", c), "tabMs1"], w=["cbf"])

            cs_chunks = [lambda: ch_cs_cast(0), lambda: ch_cs_cast(1), ch_cs_mean, ch_cs_var, ch_cs_rstd1, ch_cs_rstd2,
                         lambda: ch_cs_norm(0), lambda: ch_cs_norm(1), lambda: ch_cs_silu_a(0), lambda: ch_cs_silu_b(0),
                         lambda: ch_cs_silu_a(1), lambda: ch_cs_silu_b(1)]

            def ch_conv_pw(co):
                bk, bkey = (G1[:, :], "G1") if interleave else nbank()
                MMG(bk[:, 0:NT], [(Wpw[:, ci, co * 128:(co + 1) * 128], cbf[:, ci, 0:NT]) for ci in range(2)], r=["Wpw", "cbf"], w=[bkey])
                STT("dve", yT[:, 6 + co, 0:NT], bk[:, 0:NT], colv[:, 13 + co:14 + co], GT[:, 6 + co, 0:NT], ALU.add, ALU.mult,
                    r=[bkey, "colv", ("GT", 6 + co)], w=["hT"])

            for k in range(31):
                for c in range(2):
                    cchunks.append(lambda c=c, k=k: ch_conv_tap(c, k))
            n_tap_chunks = len(cchunks)
            cchunks.extend(cs_chunks)
            cchunks.append(lambda: ch_conv_pw(0))
            cchunks.append(lambda: ch_conv_pw(1))
            n_late = len(cchunks) - n_tap_chunks
            if not interleave:
                for f in cchunks:
                    f()
                cchunks = []
            Sbufs = [(S0, ["S0a", "S0b"]), (S1, ["S1a", "S1b"])]
            OB = {"O0": (O0, "O0"), "O1": (O1, "O1"), "G0": (G0, "G0")}
            segs = [("gqa", 0, ("O0", "O1")), ("mla", 0, ("G0",)), ("mla", 1, ("O0",)), ("gqa", 1, ("O1", "G0")),
                    ("mla", 2, ("O0",)), ("mla", 3, ("O1",)), ("gqa", 2, ("G0", "O0")), ("mla", 4, ("O1",)), ("mla", 5, ("G0",))]
            items = []
            for kind, idx, obs in segs:
                n = nkt // 2 if kind == "mla" else nkt
                for j in range(n):
                    items.append((kind, idx, obs, j, n))
            nit = len(items)

            def emit_QK(i):
                kind, idx, obs, j, n = items[i]
                Sb, Skeys = Sbufs[i % 2]
                if kind == "mla":
                    h = idx
                    for half in range(2):
                        kt = 2 * j + half
                        ks = slice(kt * 128, (kt + 1) * 128)
                        so = Sb[:, half * 512: half * 512 + NT]
                        MM(so, KC[:, ks], QA[:, h, 0:NT], True, False, r=[("KC", kt), "QA"], w=[Skeys[half]])
                        MM(so, KR[0:96, ks], QRp[0:96, h, 0:NT], False, True, r=[("KR", kt), "QR"], w=[Skeys[half]])
                else:
                    c = idx
                    kt = j
                    ks = slice(kt * 128, (kt + 1) * 128)
                    for g in range(2):
                        so = Sb[:, g * 512: g * 512 + NT]
                        MM(so, KG[64 * g:64 * g + 64, ks], QG[64 * g:64 * g + 64, c, 0:NT], True, True, r=[("KG", kt), "QG"], w=[Skeys[g]])

            def emit_EXP(i):
                kind = items[i][0]
                sc = 96.0 ** -0.5 if kind == "mla" else 0.125
                Sb, Skeys = Sbufs[i % 2]
                Pt = PT[i % 2]
                Pkey = ("pt", i % 2)
                if NT == 512:
                    ACT(Pt[:, :, :].rearrange("p a n -> p (a n)"), Sb[:, :], AF.Exp, r=Skeys, w=[Pkey], scale=sc)
                else:
                    ACT(Pt[:, :, 0:NT], Sb[:, :].rearrange("p (a n) -> p a n", a=2)[:, :, 0:NT], AF.Exp, r=Skeys, w=[Pkey], scale=sc)

            def emit_PV(i):
                kind, idx, obs, j, n = items[i]
                Pt = PT[i % 2]
                Pkey = ("pt", i % 2)
                if kind == "mla":
                    Ob, Okey = OB[obs[0]]
                    for half in range(2):
                        kt = 2 * j + half
                        MM(Ob[0:65, 0:NT], VM[:, kt, idx, :], Pt[:, half, 0:NT], j == 0 and half == 0, j == n - 1 and half == 1,
                           r=[("VM", kt), Pkey], w=[Okey])
                else:
                    kt = j
                    for g in range(2):
                        Ob, Okey = OB[obs[g]]
                        MM(Ob[0:65, 0:NT], VG[:, kt, g, :], Pt[:, g, 0:NT], j == 0, j == n - 1, r=[("VG", kt), Pkey], w=[Okey])

            post_ctr = [0]

            def post1(obname, chunk, R, use_act=False):
                Ob, Okey = OB[obname]
                pc = post_ctr[0] % 2
                post_ctr[0] += 1
                Tr, trk = (T4, "T4r") if pc == 0 else (T3, "T3")
                Tx, txk = (T2, "T2") if pc == 0 else (T1, "T1")
                if use_act:
                    ACT(Tr[64:65, 0:NT], Ob[64:65, 0:NT], AF.Ln, r=[Okey], w=[trk])
                    ACT(Tr[64:65, 0:NT], Tr[64:65, 0:NT], AF.Exp, r=[trk], w=[trk], scale=-1.0)
                else:
                    RECIP(Tr[64:65, 0:NT], Ob[64:65, 0:NT], r=[Okey], w=[trk])
                CP("dve", Tx[R:R + 64, 0:NT], Ob[0:64, 0:NT], r=[Okey], w=[txk])
                TT("dve", Tx[R:R + 64, 0:NT], Tx[R:R + 64, 0:NT], GT[R:R + 64, chunk, 0:NT], ALU.mult, r=[txk, ("GT", chunk)], w=[txk])
                return (Tr, trk, Tx, txk, chunk, R)

            def post2(st):
                Tr, trk, Tx, txk, chunk, R = st
                MM(G1[:, 0:NT], ones32[64:65, :], Tr[64:65, 0:NT], True, True, r=["ones32", trk], w=["G1"])
                TT("dve", yT[R:R + 64, chunk, 0:NT], Tx[R:R + 64, 0:NT], G1[R:R + 64, 0:NT], ALU.mult, r=[txk, "G1"], w=["hT"])

            pending = {}
            emit_QK(0)
            for i in range(nit):
                if hoist:
                    for hf_ in hoist.pop(i, []):
                        hf_()
                if cchunks and i >= 4:
                    k_late = n_late - len(cchunks)
                    if k_late < 0 or i >= 100 + 3 * k_late:
                        cchunks.pop(0)()
                if i + 1 < nit:
                    emit_QK(i + 1)
                emit_EXP(i)
                emit_PV(i)
                kind, idx, obs, j, n = items[i]
                if j == n - 1:
                    if kind == "mla":
                        heads = [(obs[0], idx // 2, 64 * (idx % 2))]
                    else:
                        heads = [(obs[0], 3 + idx // 2, 64 * (idx % 2)), (obs[1], 3 + (idx + 3) // 2, 64 * ((idx + 3) % 2))]
                    for k2, (obn, chunk, R) in enumerate(heads):
                        pending.setdefault(min(i + 8 + k2, nit - 1), []).append(post1(obn, chunk, R, use_act=(i == nit - 1 and NT == 512)))
                for stt in pending.pop(i, []):
                    post2(stt)
            while cchunks:
                cchunks.pop(0)()
            def post_sub(s, obanks):
                i = xt_ctr[0] % 2
                xt_ctr[0] += 1
                xt = XT[i]
                xr = ("xt", i)
                DMA("sp", xt[:], src_d[row0 + s * 128: row0 + (s + 1) * 128, :], r=[srcres], w=[xr], key="xt%d" % i)
                for f in range(2):
                    ob, okey = obanks[f] if obanks is not None else nbank()
                    Tt, tk = (T1, "T1") if f == 0 else (T2, "T2")
                    MMG(ob[:, :], [(yT[:, c, s * 128:(s + 1) * 128], Wout[:, c, f * 512:(f + 1) * 512]) for c in range(8)],
                        r=["hT", "Wout"], w=[okey])
                    TT("dve", Tt[:, :], ob[:, :], gate_bc[:, f * 512:(f + 1) * 512], ALU.mult, r=[okey, "gate_bc"], w=[tk])
                    TT("dve", xt[:, f * 512:(f + 1) * 512], Tt[:, :], xt[:, f * 512:(f + 1) * 512], ALU.add, r=[tk, xr], w=[xr])
                if last:
                    rs, rk = rms_stats(xt, xr)
                    STT("dve", xt[:], xt[:], rs, fnw_bc[:], ALU.mult, ALU.mult, r=[xr, rk, "fnw_bc"], w=[xr])
                DMA("pool", dst_d[row0 + s * 128: row0 + (s + 1) * 128, :], xt[:], r=[xr], w=[dstres], key="xst%d" % i)

            if next_row0 is None:
                for s in range(nsub):
                    post_sub(s, None)
            else:
                fbanks = [(S0[:, 0:512], "S0a"), (S0[:, 512:1024], "S0b"), (S1[:, 0:512], "S1a"), (S1[:, 512:1024], "S1b")]
                ob4 = [(O0[:, :], "O0"), (O1[:, :], "O1"), (G0[:, :], "G0"), (G1[:, :], "G1")]
                pend = []
                for s in range(nsub):
                    if last:
                        post_sub(s, (ob4[(2 * s) % 4], ob4[(2 * s + 1) % 4]))
                        pend.append(front_sub(src_d, srcres, next_row0, s, bank=fbanks[s]))
                        continue
                    hbx, hk = front_chain(src_d, srcres, next_row0, s)
                    post_sub(s, (ob4[(2 * s) % 4], ob4[(2 * s + 1) % 4]))
                    bk, bkey = fbanks[s]
                    bkb = bk.bitcast(BF16)
                    for c in range(8):
                        TR(bkb[:, c * 128:(c + 1) * 128], hbx[:, c * 128:(c + 1) * 128], r=[hk, "ident"], w=[bkey])
                    pend.append((bkb, bkey))
                for s in range(nsub):
                    evac_sub(hT, ["hT"], s, pend[s][0], pend[s][1])

        for l in range(n_layers):
            last = l == 1
            xs_d, xs_res = (x_d, "xD") if l == 0 else (x1_d, "x1D")
            xo_d, xo_res = (x1_d, "x1D") if l == 0 else (out_d, "outD")
            cs_d, cs_res = (ctx_d, "ctxD") if l == 0 else (ctx1_d, "ctx1D")
            DMA("sp", colv[:], colvec_d[l], r=[], w=["colv"], key="colv")
            DMA("pool", Wout[:], wout_d[l].rearrange("(c p) n -> p c n", p=128), r=[], w=["Wout"], key="Wout")
            DMA("pool", Wqr[:], wqr_d[l].rearrange("(c p) g n -> p c g n", p=128), r=[], w=["Wqr"], key="wsm0")
            DMA("pool", Wuv[:], wuv_d[l], r=[], w=["Wuv"], key="wsm1")
            DMA("pool", Wpw[:], wpw_d[l].rearrange("(c p) n -> p c n", p=128), r=[], w=["Wpw"], key="wsm2")
            for h in range(6):
                DMA("sp", T1[0:64, 0:256], wuqnT_d[l, h], r=[], w=["T1"], key="wcA")
                DMA("sp", T2[0:64, 0:128], wukT_d[l, h], r=[], w=["T2"], key="wcB")
                for c in range(2):
                    bk, bkey = nbank()
                    MM(bk[:, 0:128], T1[0:64, c * 128:(c + 1) * 128], T2[0:64, 0:128], True, True, r=["T1", "T2"], w=[bkey])
                    CP("dve", Wc[:, c, h, :], bk[:, 0:128], r=[bkey], w=["Wc"])
            if not (l == 1 and hoist_l1):
                DMA("pool", Wbuf[:, :, 0:NA], winA_d[l].rearrange("(c p) n -> p c n", p=128), r=[], w=WQ + ["WA", "WB"], key="WbufF")
            load_bc(l, 1)
            hres2 = ["hT2"] + [("GT", c) for c in range(8)]
            hbufs = [(hT, ["hT"]), (GT, hres2)]
            make_hT(cs_d, cs_res, 0, CTX, hbufs[0][0], hbufs[0][1])
            load_bc(l, 0)
            ntile = SEQ // 512
            pcs = phase_A(l, cs_d, cs_res, 0, CTX, 0, False, hbufs[0][0], hbufs[0][1])
            run_interleaved(pcs, (xs_d, xs_res, 0, hbufs[1][0], hbufs[1][1]))
            for j in range(ntile):
                cb_, cr_ = hbufs[(j + 1) % 2]
                pcs = phase_A(l, xs_d, xs_res, j * 512, 512, CTX + j * 512, True, cb_, cr_)
                if j + 1 < ntile:
                    nb_, nr_ = hbufs[j % 2]
                    run_interleaved(pcs, (xs_d, xs_res, (j + 1) * 512, nb_, nr_))
                else:
                    run_interleaved(pcs, None)
            DMA("pool", Wbuf[:, :, :], winB_d[l].rearrange("(c p) n -> p c n", p=128), r=[], w=WQ + ["WA", "WB"], key="WbufF")
            if not last:
                load_bc(l, 1)
                phase_B(l, cs_d, cs_res, ctx1_d, "ctx1D", 0, CTX, 0, False, CTX // 128, False)
                load_bc(l, 0)
            for j in range(nblk):
                hoist = None
                if l == 0 and hoist_l1 and j == nblk - 1:
                    hoist = {}
                    d_at = [2, 8, 30, 40, 74, 90]
                    c_at = [16, 28, 72, 86, 140, 154]
                    for jj in range(6):
                        hoist.setdefault(d_at[jj], []).append(lambda jj=jj: mod_chunk(1, jj, True, "dma"))
                        hoist.setdefault(c_at[jj], []).append(lambda jj=jj: mod_chunk(1, jj, True, "compute"))
                    hoist[165] = [lambda: DMA("pool", Wbuf[:, :, 0:NA], winA_d[1].rearrange("(c p) n -> p c n", p=128),
                                              r=[], w=WQ + ["WA", "WB"], key="WbufF")]
                phase_B(l, xs_d, xs_res, xo_d, xo_res, j * 512, 512, CTX + j * 512, True, NKT, last,
                        have_hT=(j > 0), next_row0=((j + 1) * 512 if j + 1 < nblk else None), hoist=hoist)
                assert not hoist
        if dump:
            dd = {}
            for nm, t, shp, dt in (("d_y", yT, [128, 8, 512], BF16), ("d_QA", QA, [128, 6, 512], BF16), ("d_QR", QRp, [128, 6, 512], BF16),
                                   ("d_QG", QG, [128, 3, 512], BF16), ("d_GT", GT, [128, 8, 512], BF16), ("d_KC", KC, [128, NKEY], BF16),
                                   ("d_KR", KR, [128, NKEY], BF16), ("d_KG", KG, [128, NKEY], BF16), ("d_VM", VM, [128, NKT, 6, 65], BF16),
                                   ("d_VG", VG, [128, NKT, 2, 65], BF16), ("d_Wc", Wc, [128, 2, 6, 128], BF16), ("d_cqn", cqn, [128, 2, 512], BF16)):
                dd[nm] = nc.dram_tensor(nm, shp, dt, kind="ExternalOutput").ap()
                allres = list(S.writers.keys())
                DMA("sp", dd[nm], t[:], r=allres, w=["outD"], key="dump_" + nm)
        S.op("sp", None, r=["outD", "x1D", "ctx1D"])
        S.emit()
    return nc


def _perm_blocks(n, blk):
    idx = np.arange(n).reshape(-1, 2, blk)
    return idx[:, ::-1, :].reshape(-1)


def _rope_tables():
    t = np.arange(SEQ)
    row = (t // 64).astype(np.float64)
    col = (t % 64).astype(np.float64)

    def tab(rdim):
        half = rdim // 2
        freqs = 10000.0 ** (-np.arange(half, dtype=np.float64) / half)
        cos = np.zeros((2 * rdim, SEQ)); sin = np.zeros((2 * rdim, SEQ))
        for a, pos in enumerate((row, col)):
            ang = pos[None, :] * freqs[:, None].astype(np.float32).astype(np.float64)
            ang = (pos.astype(np.float32)[None, :] * freqs.astype(np.float32)[:, None]).astype(np.float32)
            c = np.cos(ang); s = np.sin(ang)
            base = a * rdim
            cos[base:base + half] = c; cos[base + half:base + rdim] = c
            sin[base:base + half] = -s; sin[base + half:base + rdim] = s
        return cos.astype(np.float32), sin.astype(np.float32)

    cg, sg = tab(32)
    cm, sm = tab(16)
    tabG = np.stack([np.tile(cg, (2, 1)), np.tile(sg, (2, 1))]).astype(np.float32)
    tabM = np.stack([np.tile(cm, (3, 1)), np.tile(sm, (3, 1))]).astype(np.float32)
    return np.ascontiguousarray(tabG), np.ascontiguousarray(tabM)


def _prep_shared(norm_w, w_mod, b_mod, w_in, mla_q_norm, mla_w_uq, mla_kv_norm, mla_w_ukv, gqa_q_norm, gqa_k_norm,
                 conv_dw_w, conv_dw_b, conv_ln_w, conv_ln_b, conv_pw_w, conv_pw_b, w_out, final_norm_w):
    f = lambda a: np.ascontiguousarray(np.asarray(a, dtype=np.float32))
    w_in = f(w_in)
    p64 = _perm_blocks(64, 16)
    p32 = _perm_blocks(32, 8)
    kr = 384 + np.arange(32)
    krp = 384 + p32
    gk = 416 + np.arange(128)
    gkp = 416 + np.concatenate([p64, 64 + p64])
    colsA = np.concatenate([256 + np.arange(128), np.tile(kr, 3), np.tile(krp, 3), gk, gkp, 544 + np.arange(128), 1056 + np.arange(512)])
    assert colsA.size == NA
    gq = []
    gqp = []
    for c in range(3):
        for hq in (c, c + 3):
            gq.append(672 + hq * 64 + np.arange(64))
            gqp.append(672 + hq * 64 + p64)
    colsB = np.concatenate([np.arange(256)] + gq + gqp + [1568 + np.arange(1024)])
    assert colsB.size == NB
    w_inA = np.ascontiguousarray(w_in[:, :, colsA])
    w_inB = np.ascontiguousarray(w_in[:, :, colsB])
    wuq = f(mla_w_uq).reshape(2, 256, 6, 96)
    wuqnT = np.ascontiguousarray(wuq[:, :, :, 0:64].transpose(0, 2, 3, 1))
    wukv = f(mla_w_ukv).reshape(2, 128, 6, 128)
    wukT = np.ascontiguousarray(wukv[:, :, :, 0:64].transpose(0, 2, 3, 1))
    wuv = np.ascontiguousarray(wukv[:, :, :, 64:128].reshape(2, 128, 384))
    rope_o = wuq[:, :, :, 64:96]
    rope_p = rope_o[:, :, :, p32]
    wqr = np.stack([rope_o[:, :, 0:3].reshape(2, 256, 96), rope_o[:, :, 3:6].reshape(2, 256, 96),
                    rope_p[:, :, 0:3].reshape(2, 256, 96), rope_p[:, :, 3:6].reshape(2, 256, 96)], axis=2)
    colvec = np.zeros((2, 128, NCV), np.float32)
    qn = f(mla_q_norm)
    colvec[:, :, 0] = qn[:, 0:128]; colvec[:, :, 1] = qn[:, 128:256]
    colvec[:, :, 2] = f(mla_kv_norm)
    gqn = f(gqa_q_norm); gkn = f(gqa_k_norm)
    colvec[:, :, 3] = np.tile(gqn, (1, 2)); colvec[:, :, 4] = np.tile(gqn[:, p64], (1, 2))
    colvec[:, :, 5] = np.tile(gkn, (1, 2)); colvec[:, :, 6] = np.tile(gkn[:, p64], (1, 2))
    for c in range(2):
        sl = slice(c * 128, (c + 1) * 128)
        colvec[:, :, 7 + c] = f(conv_dw_b)[:, sl]
        colvec[:, :, 9 + c] = f(conv_ln_w)[:, sl]
        colvec[:, :, 11 + c] = f(conv_ln_b)[:, sl]
        colvec[:, :, 13 + c] = f(conv_pw_b)[:, sl]
        colvec[:, :, 15 + c * 31: 15 + (c + 1) * 31] = f(conv_dw_w)[:, :, sl].transpose(0, 2, 1)
    tabG, tabM = _rope_tables()
    return {
        "norm_w": f(norm_w), "fnw": f(final_norm_w), "w_mod": f(w_mod), "b_mod": f(b_mod),
        "w_inA": w_inA, "w_inB": w_inB, "w_out": f(w_out), "wuqnT": wuqnT, "wukT": wukT,
        "wqr": np.ascontiguousarray(wqr.astype(np.float32)), "wuv": wuv, "wpw": f(conv_pw_w), "colvec": colvec,
        "tabG": tabG, "tabM": tabM, "ident": np.eye(128, dtype=np.float32),
    }


_NC_CACHE = {}


def kernel(x, c, ctx, c_ctx, norm_w, w_mod, b_mod, w_in, mla_q_norm, mla_w_uq, mla_kv_norm, mla_w_ukv,
           gqa_q_norm, gqa_k_norm, conv_dw_w, conv_dw_b, conv_ln_w, conv_ln_b, conv_pw_w, conv_pw_b, w_out,
           final_norm_w, _debug=None):
    shared = _prep_shared(norm_w, w_mod, b_mod, w_in, mla_q_norm, mla_w_uq, mla_kv_norm, mla_w_ukv, gqa_q_norm,
                          gqa_k_norm, conv_dw_w, conv_dw_b, conv_ln_w, conv_ln_b, conv_pw_w, conv_pw_b, w_out, final_norm_w)
    x = np.asarray(x, dtype=np.float32)
    ctx = np.asarray(ctx, dtype=np.float32)
    c = np.asarray(c, dtype=np.float32)
    c_ctx = np.asarray(c_ctx, dtype=np.float32)
    nb = x.shape[0]
    batches = list(range(nb)) if _debug is None else _debug.get("batches", list(range(nb)))
    in_maps = []
    for b in batches:
        cv = np.stack([c[b], c_ctx], axis=0)
        cvT = np.ascontiguousarray(cv.reshape(2, 8, 128).transpose(2, 1, 0))
        m = dict(shared)
        m["x"] = np.ascontiguousarray(x[b])
        m["ctx"] = np.ascontiguousarray(ctx[b])
        m["cvT"] = cvT
        in_maps.append(m)
    if _debug is None:
        if "main" not in _NC_CACHE:
            _NC_CACHE["main"] = build_program()
        nc = _NC_CACHE["main"]
    else:
        nc = build_program(debug=True, **_debug.get("build", {}))
    res = run_bass_kernel_spmd(nc, in_maps, core_ids=list(range(len(batches))))
    if _debug is not None:
        return [r for r in res.results]
    return np.stack([np.asarray(r["out"]) for r in res.results], axis=0).astype(np.float32)
```

```python
import numpy as np
from contextlib import ExitStack
import concourse.bass as bass
import concourse.mybir as mybir
from concourse.bass_utils import run_bass_kernel_spmd

F32 = mybir.dt.float32
BF16 = mybir.dt.bfloat16
AF = mybir.ActivationFunctionType
ALU = mybir.AluOpType

D = 1024
SEQ = 4096
CTX = 256
NKEY = SEQ + CTX
NKT = NKEY // 128
EPS = 1e-6
NA = 1216
NB = 2048
NCV = 77
A_CKV, A_KR3, A_KR3P, A_GK, A_GKP, A_GV, A_CA, A_CG = 0, 128, 224, 320, 448, 576, 704, 960
B_CQ, B_GQ, B_GQP, B_GATE = 0, 256, 640, 1024


class _Op:
    __slots__ = ("eng", "fn", "deps", "sig", "val", "dkey", "is_dma", "sem")


class Sched:
    ENGS = ("pe", "act", "dve", "pool", "sp")

    def __init__(self, nc, stack):
        self.nc = nc
        self.stack = stack
        self.streams = {e: [] for e in self.ENGS}
        self.writers = {}
        self.readers = {}
        self.dma_cnt = {}
        self.dma_sems = {}
        self.eng_sems = {}
        self.psum_res = set()

    def op(self, eng, fn, r=(), w=(), dma=None):
        o = _Op()
        o.eng = eng; o.fn = fn; o.sig = False; o.val = None
        o.is_dma = dma is not None; o.dkey = dma; o.sem = None
        deps = []

        def add(d, kind):
            if d is o:
                return
            same = (d.eng == eng) and (not d.is_dma) and (not o.is_dma)
            if same:
                if eng == "pe" or kind == "RR":
                    return
            deps.append(d)

        for res in r:
            ws = self.writers.get(res)
            if ws:
                for d in ws.values():
                    add(d, "RAW")
            if res in self.psum_res:
                rs = self.readers.get(res)
                if rs:
                    for d in rs.values():
                        if d.eng != eng:
                            add(d, "RR")
        for res in w:
            ws = self.writers.get(res)
            if ws:
                for d in ws.values():
                    add(d, "WAW")
            rs = self.readers.get(res)
            if rs:
                for d in rs.values():
                    add(d, "WAR")
        k = ("dma", dma) if o.is_dma else eng
        for res in r:
            self.readers.setdefault(res, {})[k] = o
        for res in w:
            self.writers.setdefault(res, {})[k] = o
        if o.is_dma:
            n = self.dma_cnt.get(dma, 0) + 1
            self.dma_cnt[dma] = n
            o.val = 16 * n
            o.sig = True
        for d in deps:
            d.sig = True
        o.deps = deps
        self.streams[eng].append(o)
        return o

    def emit(self):
        nc = self.nc
        for e in self.ENGS:
            self.eng_sems[e] = self.stack.enter_context(nc.semaphore("es_" + e))
        for dk in self.dma_cnt:
            self.dma_sems[dk] = self.stack.enter_context(nc.semaphore("ds_%d" % len(self.dma_sems)))
        for e in self.ENGS:
            c = 0
            for o in self.streams[e]:
                if o.is_dma:
                    o.sem = self.dma_sems[o.dkey]
                else:
                    o.sem = self.eng_sems[e]
                    if o.sig:
                        c += 1
                        o.val = c
        block = self.stack.enter_context(nc.Block())

        def run(ename):
            def body(eng):
                waited = {}
                for o in self.streams[ename]:
                    need = {}
                    for d in o.deps:
                        if waited.get(d.sem, 0) < d.val and need.get(d.sem, 0) < d.val:
                            need[d.sem] = d.val
                    for sem, val in need.items():
                        eng.wait_ge(sem, val)
                        waited[sem] = val
                    if o.fn is None:
                        assert not o.sig
                        continue
                    ins = o.fn(eng)
                    if o.is_dma:
                        ins.then_inc(o.sem, 16)
                    elif o.sig:
                        ins.then_inc(o.sem, 1)
            return body

        block.tensor(run("pe"))
        block.scalar(run("act"))
        block.vector(run("dve"))
        block.gpsimd(run("pool"))
        block.sync(run("sp"))


def build_program(debug=False, n_layers=2, nblk=SEQ // 512, dump=False):
    nc = bass.Bass("TRN2", target_bir_lowering=False)

    def din(name, shape, dt=F32):
        return nc.dram_tensor(name, list(shape), dt, kind="ExternalInput").ap()

    x_d = din("x", [SEQ, D])
    ctx_d = din("ctx", [CTX, D])
    cvT_d = din("cvT", [128, 8, 2])
    normw_d = din("norm_w", [2, D])
    fnw_d = din("fnw", [D])
    wmod_d = din("w_mod", [2, D, 3 * D])
    bmod_d = din("b_mod", [2, 3 * D])
    winA_d = din("w_inA", [2, D, NA])
    winB_d = din("w_inB", [2, D, NB])
    wout_d = din("w_out", [2, D, D])
    wuqnT_d = din("wuqnT", [2, 6, 64, 256])
    wukT_d = din("wukT", [2, 6, 64, 128])
    wqr_d = din("wqr", [2, 256, 4, 96])
    wuv_d = din("wuv", [2, 128, 384])
    wpw_d = din("wpw", [2, 256, 256])
    colvec_d = din("colvec", [2, 128, NCV])
    tabG_d = din("tabG", [2, 128, SEQ])
    tabM_d = din("tabM", [2, 96, SEQ])
    ident_d = din("ident", [128, 128])
    out_d = nc.dram_tensor("out", [SEQ, D], F32, kind="ExternalOutput").ap()
    ikind = "ExternalOutput" if debug else "Internal"
    x1_d = nc.dram_tensor("x1", [SEQ, D], F32, kind=ikind).ap()
    ctx1_d = nc.dram_tensor("ctx1", [CTX, D], F32, kind=ikind).ap()
    glu_d = nc.dram_tensor("gluD", [2, 128, NKEY], BF16, kind="Internal").ap()
    mod_d = nc.dram_tensor("modD", [2, 2, 3 * D], F32, kind="Internal").ap()

    with ExitStack() as st:
        S = Sched(nc, st)

        def sb(name, shape, dt):
            return st.enter_context(nc.sbuf_tensor("sb_" + name, list(shape), dt))

        def ps(name, shape, dt=F32):
            return st.enter_context(nc.psum_tensor("ps_" + name, list(shape), dt))

        Wbuf = sb("Wbuf", [128, 8, NB], BF16)
        Wout = sb("Wout", [128, 8, D], BF16)
        Wc = sb("Wc", [128, 2, 6, 128], BF16)
        Wqr = sb("Wqr", [128, 2, 4, 96], BF16)
        Wuv = sb("Wuv", [128, 384], BF16)
        Wpw = sb("Wpw", [128, 2, 256], BF16)
        KC = sb("KC", [128, NKEY], BF16)
        KR = sb("KR", [128, NKEY], BF16)
        KG = sb("KG", [128, NKEY], BF16)
        VM = sb("VM", [128, NKT, 6, 65], BF16)
        VG = sb("VG", [128, NKT, 2, 65], BF16)
        gmod_bc = sb("gmod_bc", [128, D], F32)
        shift_bc = sb("shift_bc", [128, D], F32)
        gate_bc = sb("gate_bc", [128, D], F32)
        fnw_bc = sb("fnw_bc", [128, D], F32)
        XT = [sb("xt%d" % i, [128, D], F32) for i in range(2)]
        hb = sb("hb", [128, D], BF16)
        hT = sb("hT", [128, 8, 512], BF16)
        yT = hT
        QA = sb("QA", [128, 6, 512], BF16)
        QRp = sb("QRp", [128, 6, 512], BF16)
        QG = sb("QG", [128, 3, 512], BF16)
        GT = sb("GT", [128, 8, 512], BF16)
        cqn = sb("cqn", [128, 2, 512], BF16)
        PT = [sb("pt%d" % i, [128, 2, 512], BF16) for i in range(2)]
        gluw = sb("gluw", [128, 2, 542], BF16)
        cacc = sb("cacc", [128, 2, 512], F32)
        cbf = sb("cbf", [128, 2, 512], BF16)
        sq = sb("sq", [128, 2, 512], BF16)
        tabGs = sb("tabGs", [128, 2, 512], F32)
        tabMs = sb("tabMs", [128, 2, 512], F32)
        T1 = sb("T1", [128, 512], F32)
        T2 = sb("T2", [128, 512], F32)
        T3 = sb("T3", [128, 512], F32)
        T4 = sb("T4", [128, 512], F32)
        ident = sb("ident", [128, 128], BF16)
        M128 = sb("M128", [128, 128], BF16)
        M256 = sb("M256", [128, 128], BF16)
        M64 = sb("M64", [128, 128], BF16)
        ones32 = sb("ones32", [128, 128], F32)
        colv = sb("colv", [128, NCV], F32)
        ss = sb("ss", [128, 8], F32)
        cvs = sb("cvs", [128, 8, 2], F32)
        cv32 = sb("cv32", [128, 8, 2], F32)

        S0 = ps("S0", [128, 1024]); S1 = ps("S1", [128, 1024])
        O0 = ps("O0", [128, 512]); O1 = ps("O1", [128, 512])
        G0 = ps("G0", [128, 512]); G1 = ps("G1", [128, 512])
        banks = [(G0[:, :], "G0"), (G1[:, :], "G1"), (O0[:, :], "O0"), (O1[:, :], "O1"),
                 (S0[:, 0:512], "S0a"), (S0[:, 512:1024], "S0b"), (S1[:, 0:512], "S1a"), (S1[:, 512:1024], "S1b")]
        S.psum_res.update(k for _, k in banks)
        bank_ctr = [0]

        def nbank():
            b = banks[bank_ctr[0] % 8]
            bank_ctr[0] += 1
            return b

        def MM(out, lhsT, rhs, start, stop, r, w):
            S.op("pe", lambda e: e.matmul(out, lhsT=lhsT, rhs=rhs, start=start, stop=stop), r=r, w=w)

        def MMG(out, pairs, r, w):
            n = len(pairs)
            for i, (l, rr) in enumerate(pairs):
                MM(out, l, rr, i == 0, i == n - 1, r, w)

        def ACT(out, in_, func, r, w, bias=None, scale=None, accum=None):
            kw = {}
            if bias is not None:
                kw["bias"] = bias
            if scale is not None:
                kw["scale"] = scale
            if accum is not None:
                kw["accum_out"] = accum
            S.op("act", lambda e: e.activation(out=out, in_=in_, func=func, **kw), r=r, w=w)

        def TT(eng, out, in0, in1, op, r, w):
            S.op(eng, lambda e: e.tensor_tensor(out=out, in0=in0, in1=in1, op=op), r=r, w=w)

        def STT(eng, out, in0, scalar, in1, op0, op1, r, w):
            S.op(eng, lambda e: e.scalar_tensor_tensor(out=out, in0=in0, scalar=scalar, in1=in1, op0=op0, op1=op1), r=r, w=w)

        def TS(eng, out, in0, s1, s2, op0, op1, r, w):
            if s2 is None:
                S.op(eng, lambda e: e.tensor_scalar(out=out, in0=in0, scalar1=s1, scalar2=None, op0=op0), r=r, w=w)
            else:
                S.op(eng, lambda e: e.tensor_scalar(out=out, in0=in0, scalar1=s1, scalar2=s2, op0=op0, op1=op1), r=r, w=w)

        def CP(eng, out, in_, r, w):
            if eng == "act":
                S.op("act", lambda e: e.copy(out=out, in_=in_), r=r, w=w)
            else:
                S.op(eng, lambda e: e.tensor_copy(out=out, in_=in_), r=r, w=w)

        def TR(out, in_, r, w):
            S.op("pe", lambda e: e.transpose(out, in_, ident[:]), r=r, w=w)

        def RECIP(out, in_, r, w):
            S.op("dve", lambda e: e.reciprocal(out=out, in_=in_), r=r, w=w)

        def _l(x):
            return x if isinstance(x, list) else [x]

        def MSET(eng, ap, val, w):
            S.op(eng, lambda e: e.memset(ap, val), w=w)

        def DMA(q, out, in_, r, w, key):
            S.op(q, lambda e: e.dma_start(out=out, in_=in_), r=r, w=w, dma=key)

        def rstd_from_mean(out, in_, r, w, scale=1.0):
            ACT(out, in_, AF.Ln, r=r, w=w, bias=EPS, scale=scale)
            ACT(out, out, AF.Exp, r=w, w=w, scale=-0.5)

        DMA("pool", ident[:], ident_d, r=[], w=["ident"], key="cst")
        MSET("dve", M128[:], 1.0 / 128, ["M128"])
        MSET("dve", M256[:], 1.0 / 256, ["M256"])
        MSET("dve", M64[:], 0.0, ["M64"])
        MSET("dve", M64[0:64, 0:64], 1.0 / 64, ["M64"])
        MSET("dve", M64[64:128, 64:128], 1.0 / 64, ["M64"])
        MSET("dve", ones32[:], 1.0, ["ones32"])
        MSET("pool", QRp[:], 0.0, ["QR"])
        MSET("pool", VM[:], 1.0, [("VM", i) for i in range(NKT)])
        MSET("pool", VG[:], 1.0, [("VG", i) for i in range(NKT)])
        DMA("sp", fnw_bc[:], fnw_d.partition_broadcast(128), r=[], w=["fnw_bc"], key="cst2")
        DMA("sp", cv32[:], cvT_d, r=[], w=["cv32"], key="cst3")
        ACT(cvs[:], cv32[:], AF.Silu, r=["cv32"], w=["cvs"])

        WQ = ["Wq0", "Wq1", "Wq2", "Wq3"]
        Wf = Wbuf[:].bitcast(F32)

        hoist_l1 = (n_layers == 2 and nblk == SEQ // 512)

        def mod_chunk(l, j, hoisted, part="all"):
            wm = wmod_d[l].rearrange("(c p) n -> p c n", p=128)
            hf = j % 2
            cs = slice(j * 512, (j + 1) * 512)
            wkeys = [WQ[2 * hf], WQ[2 * hf + 1]]
            if hoisted:
                a1, k1 = XT[0][0:2, 0:512], ("xt", 0)
                a2, k2 = XT[0][0:2, 512:1024], ("xt", 0)
                a3, k3 = XT[1][0:2, 0:512], ("xt", 1)
                bk, bkey = G1[:, :], "G1"
                wdma = wkeys + ["WA", "WB"]
            else:
                a1, k1 = T1[0:2, :], "T1"
                a2, k2 = T2[0:2, :], "T2"
                a3, k3 = T3[0:2, :], "T3"
                bk, bkey = nbank()
                wdma = wkeys
            if part in ("all", "dma"):
                DMA("sp", Wf[:, :, hf * 512:(hf + 1) * 512], wm[:, :, cs], r=[], w=wdma, key="Wm%d" % hf)
                if part == "dma":
                    return
            DMA("sp", a2, bmod_d[l, cs].partition_broadcast(2), r=[], w=[k2], key="bm")
            MMG(bk[0:2, :], [(cvs[:, k, :], Wf[:, k, hf * 512:(hf + 1) * 512]) for k in range(8)],
                r=["cvs"] + wkeys, w=[bkey])
            TT("dve", a1, bk[0:2, :], a2, ALU.add, r=[bkey, k2], w=[k1])
            if j in (2, 3):
                DMA("sp", a3, normw_d[l, (j - 2) * 512:(j - 1) * 512].partition_broadcast(2), r=[], w=[k3], key="nwc")
                TS("dve", a1, a1, 1.0, None, ALU.add, None, r=[k1], w=[k1])
                TT("dve", a1, a1, a3, ALU.mult, r=[k1, k3], w=[k1])
            DMA("sp", mod_d[l, :, cs], a1, r=[k1], w=["modD"], key="modst")

        for l in range(2):
            if l == 1 and hoist_l1:
                continue
            for j in range(6):
                mod_chunk(l, j, False)

        def load_bc(l, v):
            DMA("sp", shift_bc[:], mod_d[l, v, 0:D].partition_broadcast(128), r=["modD"], w=["shift_bc"], key="bc0")
            DMA("sp", gmod_bc[:], mod_d[l, v, D:2 * D].partition_broadcast(128), r=["modD"], w=["gmod_bc"], key="bc1")
            DMA("sp", gate_bc[:], mod_d[l, v, 2 * D:3 * D].partition_broadcast(128), r=["modD"], w=["gate_bc"], key="bc2")

        xt_ctr = [0]

        HB = {"buf": None, "res": None}

        stat_ctr = [0]
        hb_ctr = [0]
        PT0f = PT[0][:, :, :].rearrange("p a n -> p (a n)")
        PT1f = PT[1][:, :, :].rearrange("p a n -> p (a n)")

        def rms_stats(xt, xr):
            k = stat_ctr[0] % 2
            stat_ctr[0] += 1
            c0 = 4 * k
            ACT(PT0f, xt[:], AF.Square, r=[xr], w=[("pt", 0), ("ss", k, 0)], accum=ss[:, c0:c0 + 1])
            ACT(ss[:, c0 + 1:c0 + 2], ss[:, c0:c0 + 1], AF.Ln, r=[("ss", k, 0)], w=[("ss", k, 1)], bias=EPS, scale=1.0 / D)
            ACT(ss[:, c0 + 2:c0 + 3], ss[:, c0 + 1:c0 + 2], AF.Exp, r=[("ss", k, 1)], w=[("ss", k, 2)], scale=-0.5)
            return ss[:, c0 + 2:c0 + 3], ("ss", k, 2)

        def front_sub(src_d, srcres, row0, s, bank=None):
            i = xt_ctr[0] % 2
            xt_ctr[0] += 1
            xt = XT[i]
            xr = ("xt", i)
            DMA("sp", xt[:], src_d[row0 + s * 128: row0 + (s + 1) * 128, :], r=[srcres], w=[xr], key="xt%d" % i)
            rs, rk = rms_stats(xt, xr)
            STT("dve", xt[:], xt[:], rs, gmod_bc[:], ALU.mult, ALU.mult, r=[xr, rk, "gmod_bc"], w=[xr])
            k = hb_ctr[0] % 2
            hb_ctr[0] += 1
            hbx, hk = (hb[:, :], "hb") if k == 0 else (PT1f, ("pt", 1))
            TT("dve", hbx, xt[:], shift_bc[:], ALU.add, r=[xr, "shift_bc"], w=[hk])
            bk, bkey = bank if bank is not None else nbank()
            bkb = bk.bitcast(BF16)
            for c in range(8):
                TR(bkb[:, c * 128:(c + 1) * 128], hbx[:, c * 128:(c + 1) * 128], r=[hk, "ident"], w=[bkey])
            return bkb, bkey

        def front_chain(src_d, srcres, row0, s):
            i = xt_ctr[0] % 2
            xt_ctr[0] += 1
            xt = XT[i]
            xr = ("xt", i)
            DMA("sp", xt[:], src_d[row0 + s * 128: row0 + (s + 1) * 128, :], r=[srcres], w=[xr], key="xt%d" % i)
            rs, rk = rms_stats(xt, xr)
            STT("dve", xt[:], xt[:], rs, gmod_bc[:], ALU.mult, ALU.mult, r=[xr, rk, "gmod_bc"], w=[xr])
            k = hb_ctr[0] % 2
            hb_ctr[0] += 1
            hbx, hk = (hb[:, :], "hb") if k == 0 else (PT1f, ("pt", 1))
            TT("dve", hbx, xt[:], shift_bc[:], ALU.add, r=[xr, "shift_bc"], w=[hk])
            return hbx, hk

        def front_tr_evac(hbx, hk, hbuf, hres, s):
            bk, bkey = nbank()
            bkb = bk.bitcast(BF16)
            for c in range(8):
                TR(bkb[:, c * 128:(c + 1) * 128], hbx[:, c * 128:(c + 1) * 128], r=[hk, "ident"], w=[bkey])
            evac_sub(hbuf, hres, s, bkb, bkey)

        def evac_sub(hbuf, hres, s, bkb, bkey):
            CP("dve", hbuf[:, :, s * 128:(s + 1) * 128], bkb[:, 0:1024].rearrange("p (c t) -> p c t", c=8), r=[bkey], w=hres)

        def make_hT(src_d, srcres, row0, NT, hbuf=None, hres=None):
            hbuf = hT if hbuf is None else hbuf
            hres = ["hT"] if hres is None else hres
            for s in range(NT // 128):
                bkb, bkey = front_sub(src_d, srcres, row0, s)
                evac_sub(hbuf, hres, s, bkb, bkey)

        def proj(cols, m, NT, wres, hbuf=None, hres=None):
            hbuf = hT if hbuf is None else hbuf
            hres = ["hT"] if hres is None else hres
            bk, bkey = nbank()
            MMG(bk[0:m, 0:NT], [(Wbuf[:, k, cols:cols + m], hbuf[:, k, 0:NT]) for k in range(8)], r=[wres] + hres, w=[bkey])
            return bk, bkey

        def load_tables(t0, NT):
            DMA("sp", tabGs[:, :, 0:NT], tabG_d[:, :, t0:t0 + NT].rearrange("a p n -> p a n"), r=[], w=["tabGs"], key="tabG")
            DMA("sp", tabMs[0:96, :, 0:NT], tabM_d[:, :, t0:t0 + NT].rearrange("a p n -> p a n"), r=[], w=["tabMs", "tabMs1"], key="tabM")

        def head_norm_rope(o_bk, o_key, p_bk, p_key, g_col, gp_col, NT, rope, dst, dst_res, pre_squared=False):
            if not pre_squared:
                ACT(sq[:, 0, 0:NT], o_bk[:, 0:NT], AF.Square, r=[o_key], w=["sq"])
            mb, mkey = nbank()
            MM(mb[:, 0:NT], M64[:], sq[:, 0, 0:NT], True, True, r=["M64", "sq"], w=[mkey])
            rstd_from_mean(T3[:, 0:NT], mb[:, 0:NT], r=[mkey], w=["T3"])
            if rope:
                STT("dve", T1[:, 0:NT], o_bk[:, 0:NT], colv[:, g_col:g_col + 1], tabGs[:, 0, 0:NT], ALU.mult, ALU.mult,
                    r=[o_key, "colv", "tabGs"], w=["T1"])
                STT("dve", T2[:, 0:NT], p_bk[:, 0:NT], colv[:, gp_col:gp_col + 1], tabGs[:, 1, 0:NT], ALU.mult, ALU.mult,
                    r=[p_key, "colv", "tabGs"], w=["T2"])
                TT("dve", T1[:, 0:NT], T1[:, 0:NT], T2[:, 0:NT], ALU.add, r=["T1", "T2"], w=["T1"])
                TT("dve", dst, T1[:, 0:NT], T3[:, 0:NT], ALU.mult, r=["T1", "T3"], w=_l(dst_res))
            else:
                STT("dve", dst, o_bk[:, 0:NT], colv[:, g_col:g_col + 1], T3[:, 0:NT], ALU.mult, ALU.mult,
                    r=[o_key, "colv", "T3"], w=_l(dst_res))

        def rope96(o_bk, o_key, p_bk, p_key, NT, rope, dst, dst_res):
            if rope:
                TT("dve", T1[0:96, 0:NT], o_bk[0:96, 0:NT], tabMs[0:96, 0, 0:NT], ALU.mult, r=[o_key, "tabMs"], w=["T1"])
                TT("dve", T2[0:96, 0:NT], p_bk[0:96, 0:NT], tabMs[0:96, 1, 0:NT], ALU.mult, r=[p_key, "tabMs", "tabMs1"], w=["T2"])
                if isinstance(dst, list):
                    for jj in range(3):
                        TT("dve", dst[jj], T1[32 * jj:32 * jj + 32, 0:NT], T2[32 * jj:32 * jj + 32, 0:NT], ALU.add, r=["T1", "T2"], w=_l(dst_res))
                else:
                    TT("dve", dst, T1[0:96, 0:NT], T2[0:96, 0:NT], ALU.add, r=["T1", "T2"], w=_l(dst_res))
            else:
                if isinstance(dst, list):
                    for jj in range(3):
                        CP("dve", dst[jj], o_bk[32 * jj:32 * jj + 32, 0:NT], r=[o_key], w=_l(dst_res))
                else:
                    CP("dve", dst, o_bk[0:96, 0:NT], r=[o_key], w=_l(dst_res))

        def phase_A(l, src_d, srcres, row0, NT, key0, rope, hbuf, hres):
            kt0 = key0 // 128
            nsub = NT // 128

            def pj(cols, m, NT, wres):
                return proj(cols, m, NT, wres, hbuf, hres)

            st = {}

            def pieceA():
                if rope:
                    load_tables(row0, NT)
                cb, ckey = pj(A_CKV, 128, NT, "WA")
                ACT(sq[:, 0, 0:NT], cb[:, 0:NT], AF.Square, r=[ckey], w=["sq"])
                st["ckv"] = (cb, ckey)

            def pieceB():
                cb, ckey = st["ckv"]
                mb, mkey = nbank()
                MM(mb[:, 0:NT], M128[:], sq[:, 0, 0:NT], True, True, r=["M128", "sq"], w=[mkey])
                rstd_from_mean(T3[:, 0:NT], mb[:, 0:NT], r=[mkey], w=["T3"])
                STT("dve", KC[:, key0:key0 + NT], cb[:, 0:NT], colv[:, 2:3], T3[:, 0:NT], ALU.mult, ALU.mult,
                    r=[ckey, "colv", "T3"], w=[("KC", kt0 + i) for i in range(nsub)])
                kb, kkey = pj(A_KR3, 96, NT, "WA")
                if rope:
                    kpb, kpkey = pj(A_KR3P, 96, NT, "WA")
                else:
                    kpb, kpkey = None, None
                rope96(kb, kkey, kpb, kpkey, NT, rope, KR[0:96, key0:key0 + NT], [("KR", kt0 + i) for i in range(nsub)])
                for s in range(nsub):
                    vb, vkey = nbank()
                    MM(vb[:, 0:384], KC[:, key0 + s * 128: key0 + (s + 1) * 128], Wuv[:], True, True, r=[("KC", kt0 + s), "Wuv"], w=[vkey])
                    CP("act", VM[:, kt0 + s, :, 0:64], vb[:, 0:384].rearrange("p (h d) -> p h d", h=6), r=[vkey], w=[("VM", kt0 + s)])

            def pieceC():
                gb, gkey = pj(A_GK, 128, NT, "WA")
                if rope:
                    gpb, gpkey = pj(A_GKP, 128, NT, "WA")
                else:
                    gpb, gpkey = None, None
                ACT(sq[:, 0, 0:NT], gb[:, 0:NT], AF.Square, r=[gkey], w=["sq"])
                st["gk"] = (gb, gkey, gpb, gpkey)

            def pieceD():
                gb, gkey, gpb, gpkey = st["gk"]
                head_norm_rope(gb, gkey, gpb, gpkey, 5, 6, NT, rope, KG[:, key0:key0 + NT], [("KG", kt0 + i) for i in range(nsub)],
                               pre_squared=True)
                for s in range(nsub):
                    vb, vkey = nbank()
                    MMG(vb[:, 0:128], [(hbuf[:, k, s * 128:(s + 1) * 128], Wbuf[:, k, A_GV:A_GV + 128]) for k in range(8)],
                        r=hres + ["WA"], w=[vkey])
                    CP("act", VG[:, kt0 + s, :, 0:64], vb[:, 0:128].rearrange("p (h d) -> p h d", h=2), r=[vkey], w=[("VG", kt0 + s)])
                for c in range(2):
                    ab, akey = pj(A_CA + c * 128, 128, NT, "WA")
                    gb2, gkey2 = pj(A_CG + c * 128, 128, NT, "WA")
                    ACT(T1[:, 0:NT], gb2[:, 0:NT], AF.Sigmoid, r=[gkey2], w=["T1"])
                    TT("dve", cbf[:, c, 0:NT], ab[:, 0:NT], T1[:, 0:NT], ALU.mult, r=[akey, "T1"], w=["cbf"])
                DMA("pool", glu_d[:, :, key0:key0 + NT].rearrange("c p n -> p c n"), cbf[:, :, 0:NT], r=["cbf"], w=["gluD"], key="glust")

            return [pieceA, pieceB, pieceC, pieceD]

        def run_interleaved(pieces, front):
            for s in range(4):
                h = None
                if front is not None:
                    h = front_chain(front[0], front[1], front[2], s)
                pieces[s]()
                if front is not None:
                    front_tr_evac(h[0], h[1], front[3], front[4], s)

        def phase_B(l, src_d, srcres, dst_d, dstres, row0, NT, key0, rope, nkt, last, have_hT=False, next_row0=None, hoist=None):
            nsub = NT // 128
            if not have_hT:
                make_hT(src_d, srcres, row0, NT)
            if rope:
                load_tables(row0, NT)
            seq0 = 0 if not rope else CTX
            seqn = CTX if not rope else SEQ
            lo = key0 - 15
            hi = key0 + NT + 15
            clo = max(lo, seq0)
            chi = min(hi, seq0 + seqn)
            if clo > lo:
                MSET("pool", gluw[:, :, 0:clo - lo], 0.0, ["gluw"])
            if chi < hi:
                MSET("pool", gluw[:, :, NT + 30 - (hi - chi):NT + 30], 0.0, ["gluw"])
            DMA("sp", gluw[:, :, clo - lo:chi - lo], glu_d[:, :, clo:chi].rearrange("c p n -> p c n"), r=["gluD"], w=["gluw"], key="gluw")
            cqb = [proj(B_CQ + c * 128, 128, NT, "WB") for c in range(2)]
            for c in range(2):
                ACT(sq[:, c, 0:NT], cqb[c][0][:, 0:NT], AF.Square, r=[cqb[c][1]], w=["sq"])
            mb, mkey = nbank()
            MMG(mb[:, 0:NT], [(M256[:], sq[:, c, 0:NT]) for c in range(2)], r=["M256", "sq"], w=[mkey])
            rstd_from_mean(T3[:, 0:NT], mb[:, 0:NT], r=[mkey], w=["T3"])
            for c in range(2):
                STT("dve", cqn[:, c, 0:NT], cqb[c][0][:, 0:NT], colv[:, c:c + 1], T3[:, 0:NT], ALU.mult, ALU.mult,
                    r=[cqb[c][1], "colv", "T3"], w=["cqn"])
            for h in range(6):
                qb, qkey = nbank()
                MMG(qb[:, 0:NT], [(Wc[:, c, h, :], cqn[:, c, 0:NT]) for c in range(2)], r=["Wc", "cqn"], w=[qkey])
                CP("act" if h % 2 else "dve", QA[:, h, 0:NT], qb[:, 0:NT], r=[qkey], w=["QA"])
            for g in range(2):
                ob, okey = nbank()
                MMG(ob[0:96, 0:NT], [(Wqr[:, c, g, :], cqn[:, c, 0:NT]) for c in range(2)], r=["Wqr", "cqn"], w=[okey])
                if rope:
                    pb, pkey = nbank()
                    MMG(pb[0:96, 0:NT], [(Wqr[:, c, 2 + g, :], cqn[:, c, 0:NT]) for c in range(2)], r=["Wqr", "cqn"], w=[pkey])
                else:
                    pb, pkey = None, None
                rope96(ob, okey, pb, pkey, NT, rope, [QRp[32 * jj:32 * jj + 32, 3 * g + jj, 0:NT] for jj in range(3)], "QR")
            if rope:
                slots = [(sq[:, 0, 0:NT], "sq"), (sq[:, 1, 0:NT], "sq"), (cbf[:, 0, 0:NT], "cbf")]
                pr = []
                for c in range(3):
                    ob, okey = proj(B_GQ + c * 128, 128, NT, "WB")
                    pb, pkey = proj(B_GQP + c * 128, 128, NT, "WB")
                    ACT(slots[c][0], ob[:, 0:NT], AF.Square, r=[okey], w=[slots[c][1]])
                    pr.append((ob, okey, pb, pkey))
                for c in range(3):
                    ob, okey, pb, pkey = pr[c]
                    mb, mkey = nbank()
                    MM(mb[:, 0:NT], M64[:], slots[c][0], True, True, r=["M64", slots[c][1]], w=[mkey])
                    rstd_from_mean(T3[:, 0:NT], mb[:, 0:NT], r=[mkey], w=["T3"])
                    STT("dve", T1[:, 0:NT], ob[:, 0:NT], colv[:, 3:4], tabGs[:, 0, 0:NT], ALU.mult, ALU.mult, r=[okey, "colv", "tabGs"], w=["T1"])
                    STT("dve", T2[:, 0:NT], pb[:, 0:NT], colv[:, 4:5], tabGs[:, 1, 0:NT], ALU.mult, ALU.mult, r=[pkey, "colv", "tabGs"], w=["T2"])
                    TT("dve", T1[:, 0:NT], T1[:, 0:NT], T2[:, 0:NT], ALU.add, r=["T1", "T2"], w=["T1"])
                    TT("dve", QG[:, c, 0:NT], T1[:, 0:NT], T3[:, 0:NT], ALU.mult, r=["T1", "T3"], w=["QG"])
            else:
                for c in range(3):
                    ob, okey = proj(B_GQ + c * 128, 128, NT, "WB")
                    head_norm_rope(ob, okey, None, None, 3, 4, NT, rope, QG[:, c, 0:NT], "QG")
            for c in range(8):
                gb, gkey = proj(B_GATE + c * 128, 128, NT, "WB")
                ACT(GT[:, c, 0:NT], gb[:, 0:NT], AF.Silu, r=[gkey], w=[("GT", c)])
            interleave = (NT == 512)
            cchunks = []

            def ch_conv_tap(c, k):
                if k == 0:
                    TS("dve", cacc[:, c, 0:NT], gluw[:, c, 0:NT], colv[:, 15 + c * 31: 16 + c * 31], colv[:, 7 + c:8 + c], ALU.mult, ALU.add,
                       r=["gluw", "colv"], w=[("cacc", c)])
                else:
                    STT("dve", cacc[:, c, 0:NT], gluw[:, c, k:k + NT], colv[:, 15 + c * 31 + k: 16 + c * 31 + k], cacc[:, c, 0:NT],
                        ALU.mult, ALU.add, r=["gluw", "colv", ("cacc", c)], w=[("cacc", c)])

            X1 = tabGs[:, 0, 0:NT]; X2 = tabGs[:, 1, 0:NT]; X3 = tabMs[:, 0, 0:NT]; X4 = tabMs[:, 1, 0:NT]

            def cbank():
                return (G1[:, :], "G1") if interleave else nbank()

            def ch_cs_cast(c):
                CP("dve", cbf[:, c, 0:NT], cacc[:, c, 0:NT], r=[("cacc", c)], w=["cbf"])
                TT("dve", sq[:, c, 0:NT], cacc[:, c, 0:NT], cacc[:, c, 0:NT], ALU.mult, r=[("cacc", c)], w=["sq"])

            def ch_cs_mean():
                bk, bkey = cbank()
                MMG(bk[:, 0:NT], [(M256[:], cbf[:, c, 0:NT]) for c in range(2)], r=["M256", "cbf"], w=[bkey])
                CP("dve", X1, bk[:, 0:NT], r=[bkey], w=["tabGs"])
                TT("dve", X2, X1, X1, ALU.mult, r=["tabGs"], w=["tabGs"])

            def ch_cs_var():
                bk, bkey = cbank()
                MMG(bk[:, 0:NT], [(M256[:], sq[:, c, 0:NT]) for c in range(2)], r=["M256", "sq"], w=[bkey])
                TT("dve", X2, bk[:, 0:NT], X2, ALU.subtract, r=[bkey, "tabGs"], w=["tabGs"])
                TS("dve", X2, X2, 0.0, None, ALU.max, None, r=["tabGs"], w=["tabGs"])

            def ch_cs_rstd1():
                ACT(X3, X2, AF.Ln, r=["tabGs"], w=["tabMs"], bias=EPS)

            def ch_cs_rstd2():
                ACT(X3, X3, AF.Exp, r=["tabMs"], w=["tabMs"], scale=-0.5)

            def ch_cs_norm(c):
                TT("dve", cacc[:, c, 0:NT], cacc[:, c, 0:NT], X1, ALU.subtract, r=[("cacc", c), "tabGs"], w=[("cacc", c)])
                TT("dve", cacc[:, c, 0:NT], cacc[:, c, 0:NT], X3, ALU.mult, r=[("cacc", c), "tabMs"], w=[("cacc", c)])
                TS("dve", cacc[:, c, 0:NT], cacc[:, c, 0:NT], colv[:, 9 + c:10 + c], colv[:, 11 + c:12 + c], ALU.mult, ALU.add,
                   r=[("cacc", c), "colv"], w=[("cacc", c)])

            def ch_cs_silu_a(c):
                ACT(X4, cacc[:, c, 0:NT], AF.Exp, r=[("cacc", c)], w=["tabMs1"], scale=-1.0)

            def ch_cs_silu_b(c):
                TS("dve", X4, X4, 1.0, None, ALU.add, None, r=["tabMs1"], w=["tabMs1"])
                RECIP(X4, X4, r=["tabMs1"], w=["tabMs1"])
                TT("dve", cbf[:, c, 0:NT], cacc[:, c, 0:NT], X4, ALU.mult, r=[("cacc", c), "tabMs1"], w=["cbf"])

            cs_chunks = [lambda: ch_cs_cast(0), lambda: ch_cs_cast(1), ch_cs_mean, ch_cs_var, ch_cs_rstd1, ch_cs_rstd2,
                         lambda: ch_cs_norm(0), lambda: ch_cs_norm(1), lambda: ch_cs_silu_a(0), lambda: ch_cs_silu_b(0),
                         lambda: ch_cs_silu_a(1), lambda: ch_cs_silu_b(1)]

            def ch_conv_pw(co):
                bk, bkey = (G1[:, :], "G1") if interleave else nbank()
                MMG(bk[:, 0:NT], [(Wpw[:, ci, co * 128:(co + 1) * 128], cbf[:, ci, 0:NT]) for ci in range(2)], r=["Wpw", "cbf"], w=[bkey])
                STT("dve", yT[:, 6 + co, 0:NT], bk[:, 0:NT], colv[:, 13 + co:14 + co], GT[:, 6 + co, 0:NT], ALU.add, ALU.mult,
                    r=[bkey, "colv", ("GT", 6 + co)], w=["hT"])

            for k in range(31):
                for c in range(2):
                    cchunks.append(lambda c=c, k=k: ch_conv_tap(c, k))
            n_tap_chunks = len(cchunks)
            cchunks.extend(cs_chunks)
            cchunks.append(lambda: ch_conv_pw(0))
            cchunks.append(lambda: ch_conv_pw(1))
            n_late = len(cchunks) - n_tap_chunks
            if not interleave:
                for f in cchunks:
                    f()
                cchunks = []
            Sbufs = [(S0, ["S0a", "S0b"]), (S1, ["S1a", "S1b"])]
            OB = {"O0": (O0, "O0"), "O1": (O1, "O1"), "G0": (G0, "G0")}
            segs = [("gqa", 0, ("O0", "O1")), ("mla", 0, ("G0",)), ("mla", 1, ("O0",)), ("gqa", 1, ("O1", "G0")),
                    ("mla", 2, ("O0",)), ("mla", 3, ("O1",)), ("gqa", 2, ("G0", "O0")), ("mla", 4, ("O1",)), ("mla", 5, ("G0",))]
            items = []
            for kind, idx, obs in segs:
                n = nkt // 2 if kind == "mla" else nkt
                for j in range(n):
                    items.append((kind, idx, obs, j, n))
            nit = len(items)

            def emit_QK(i):
                kind, idx, obs, j, n = items[i]
                Sb, Skeys = Sbufs[i % 2]
                if kind == "mla":
                    h = idx
                    for half in range(2):
                        kt = 2 * j + half
                        ks = slice(kt * 128, (kt + 1) * 128)
                        so = Sb[:, half * 512: half * 512 + NT]
                        MM(so, KC[:, ks], QA[:, h, 0:NT], True, False, r=[("KC", kt), "QA"], w=[Skeys[half]])
                        MM(so, KR[0:96, ks], QRp[0:96, h, 0:NT], False, True, r=[("KR", kt), "QR"], w=[Skeys[half]])
                else:
                    c = idx
                    kt = j
                    ks = slice(kt * 128, (kt + 1) * 128)
                    for g in range(2):
                        so = Sb[:, g * 512: g * 512 + NT]
                        MM(so, KG[64 * g:64 * g + 64, ks], QG[64 * g:64 * g + 64, c, 0:NT], True, True, r=[("KG", kt), "QG"], w=[Skeys[g]])

            def emit_EXP(i):
                kind = items[i][0]
                sc = 96.0 ** -0.5 if kind == "mla" else 0.125
                Sb, Skeys = Sbufs[i % 2]
                Pt = PT[i % 2]
                Pkey = ("pt", i % 2)
                if NT == 512:
                    ACT(Pt[:, :, :].rearrange("p a n -> p (a n)"), Sb[:, :], AF.Exp, r=Skeys, w=[Pkey], scale=sc)
                else:
                    ACT(Pt[:, :, 0:NT], Sb[:, :].rearrange("p (a n) -> p a n", a=2)[:, :, 0:NT], AF.Exp, r=Skeys, w=[Pkey], scale=sc)

            def emit_PV(i):
                kind, idx, obs, j, n = items[i]
                Pt = PT[i % 2]
                Pkey = ("pt", i % 2)
                if kind == "mla":
                    Ob, Okey = OB[obs[0]]
                    for half in range(2):
                        kt = 2 * j + half
                        MM(Ob[0:65, 0:NT], VM[:, kt, idx, :], Pt[:, half, 0:NT], j == 0 and half == 0, j == n - 1 and half == 1,
                           r=[("VM", kt), Pkey], w=[Okey])
                else:
                    kt = j
                    for g in range(2):
                        Ob, Okey = OB[obs[g]]
                        MM(Ob[0:65, 0:NT], VG[:, kt, g, :], Pt[:, g, 0:NT], j == 0, j == n - 1, r=[("VG", kt), Pkey], w=[Okey])

            post_ctr = [0]

            def post1(obname, chunk, R, use_act=False):
                Ob, Okey = OB[obname]
                pc = post_ctr[0] % 2
                post_ctr[0] += 1
                Tr, trk = (T4, "T4r") if pc == 0 else (T3, "T3")
                Tx, txk = (T2, "T2") if pc == 0 else (T1, "T1")
                if use_act:
                    ACT(Tr[64:65, 0:NT], Ob[64:65, 0:NT], AF.Ln, r=[Okey], w=[trk])
                    ACT(Tr[64:65, 0:NT], Tr[64:65, 0:NT], AF.Exp, r=[trk], w=[trk], scale=-1.0)
                else:
                    RECIP(Tr[64:65, 0:NT], Ob[64:65, 0:NT], r=[Okey], w=[trk])
                CP("dve", Tx[R:R + 64, 0:NT], Ob[0:64, 0:NT], r=[Okey], w=[txk])
                TT("dve", Tx[R:R + 64, 0:NT], Tx[R:R + 64, 0:NT], GT[R:R + 64, chunk, 0:NT], ALU.mult, r=[txk, ("GT", chunk)], w=[txk])
                return (Tr, trk, Tx, txk, chunk, R)

            def post2(st):
                Tr, trk, Tx, txk, chunk, R = st
                MM(G1[:, 0:NT], ones32[64:65, :], Tr[64:65, 0:NT], True, True, r=["ones32", trk], w=["G1"])
                TT("dve", yT[R:R + 64, chunk, 0:NT], Tx[R:R + 64, 0:NT], G1[R:R + 64, 0:NT], ALU.mult, r=[txk, "G1"], w=["hT"])

            pending = {}
            emit_QK(0)
            for i in range(nit):
                if hoist:
                    for hf_ in hoist.pop(i, []):
                        hf_()
                if cchunks and i >= 4:
                    k_late = n_late - len(cchunks)
                    if k_late < 0 or i >= 100 + 3 * k_late:
                        cchunks.pop(0)()
                if i + 1 < nit:
                    emit_QK(i + 1)
                emit_EXP(i)
                emit_PV(i)
                kind, idx, obs, j, n = items[i]
                if j == n - 1:
                    if kind == "mla":
                        heads = [(obs[0], idx // 2, 64 * (idx % 2))]
                    else:
                        heads = [(obs[0], 3 + idx // 2, 64 * (idx % 2)), (obs[1], 3 + (idx + 3) // 2, 64 * ((idx + 3) % 2))]
                    for k2, (obn, chunk, R) in enumerate(heads):
                        pending.setdefault(min(i + 8 + k2, nit - 1), []).append(post1(obn, chunk, R, use_act=(i == nit - 1 and NT == 512)))
                for stt in pending.pop(i, []):
                    post2(stt)
            while cchunks:
                cchunks.pop(0)()
            def post_sub(s, obanks):
                i = xt_ctr[0] % 2
                xt_ctr[0] += 1
                xt = XT[i]
                xr = ("xt", i)
                DMA("sp", xt[:], src_d[row0 + s * 128: row0 + (s + 1) * 128, :], r=[srcres], w=[xr], key="xt%d" % i)
                for f in range(2):
                    ob, okey = obanks[f] if obanks is not None else nbank()
                    Tt, tk = (T1, "T1") if f == 0 else (T2, "T2")
                    MMG(ob[:, :], [(yT[:, c, s * 128:(s + 1) * 128], Wout[:, c, f * 512:(f + 1) * 512]) for c in range(8)],
                        r=["hT", "Wout"], w=[okey])
                    TT("dve", Tt[:, :], ob[:, :], gate_bc[:, f * 512:(f + 1) * 512], ALU.mult, r=[okey, "gate_bc"], w=[tk])
                    TT("dve", xt[:, f * 512:(f + 1) * 512], Tt[:, :], xt[:, f * 512:(f + 1) * 512], ALU.add, r=[tk, xr], w=[xr])
                if last:
                    rs, rk = rms_stats(xt, xr)
                    STT("dve", xt[:], xt[:], rs, fnw_bc[:], ALU.mult, ALU.mult, r=[xr, rk, "fnw_bc"], w=[xr])
                DMA("pool", dst_d[row0 + s * 128: row0 + (s + 1) * 128, :], xt[:], r=[xr], w=[dstres], key="xst%d" % i)

            if next_row0 is None:
                for s in range(nsub):
                    post_sub(s, None)
            else:
                fbanks = [(S0[:, 0:512], "S0a"), (S0[:, 512:1024], "S0b"), (S1[:, 0:512], "S1a"), (S1[:, 512:1024], "S1b")]
                ob4 = [(O0[:, :], "O0"), (O1[:, :], "O1"), (G0[:, :], "G0"), (G1[:, :], "G1")]
                pend = []
                for s in range(nsub):
                    if last:
                        post_sub(s, (ob4[(2 * s) % 4], ob4[(2 * s + 1) % 4]))
                        pend.append(front_sub(src_d, srcres, next_row0, s, bank=fbanks[s]))
                        continue
                    hbx, hk = front_chain(src_d, srcres, next_row0, s)
                    post_sub(s, (ob4[(2 * s) % 4], ob4[(2 * s + 1) % 4]))
                    bk, bkey = fbanks[s]
                    bkb = bk.bitcast(BF16)
                    for c in range(8):
                        TR(bkb[:, c * 128:(c + 1) * 128], hbx[:, c * 128:(c + 1) * 128], r=[hk, "ident"], w=[bkey])
                    pend.append((bkb, bkey))
                for s in range(nsub):
                    evac_sub(hT, ["hT"], s, pend[s][0], pend[s][1])

        for l in range(n_layers):
            last = l == 1
            xs_d, xs_res = (x_d, "xD") if l == 0 else (x1_d, "x1D")
            xo_d, xo_res = (x1_d, "x1D") if l == 0 else (out_d, "outD")
            cs_d, cs_res = (ctx_d, "ctxD") if l == 0 else (ctx1_d, "ctx1D")
            DMA("sp", colv[:], colvec_d[l], r=[], w=["colv"], key="colv")
            DMA("pool", Wout[:], wout_d[l].rearrange("(c p) n -> p c n", p=128), r=[], w=["Wout"], key="Wout")
            DMA("pool", Wqr[:], wqr_d[l].rearrange("(c p) g n -> p c g n", p=128), r=[], w=["Wqr"], key="wsm0")
            DMA("pool", Wuv[:], wuv_d[l], r=[], w=["Wuv"], key="wsm1")
            DMA("pool", Wpw[:], wpw_d[l].rearrange("(c p) n -> p c n", p=128), r=[], w=["Wpw"], key="wsm2")
            for h in range(6):
                DMA("sp", T1[0:64, 0:256], wuqnT_d[l, h], r=[], w=["T1"], key="wcA")
                DMA("sp", T2[0:64, 0:128], wukT_d[l, h], r=[], w=["T2"], key="wcB")
                for c in range(2):
                    bk, bkey = nbank()
                    MM(bk[:, 0:128], T1[0:64, c * 128:(c + 1) * 128], T2[0:64, 0:128], True, True, r=["T1", "T2"], w=[bkey])
                    CP("dve", Wc[:, c, h, :], bk[:, 0:128], r=[bkey], w=["Wc"])
            if not (l == 1 and hoist_l1):
                DMA("pool", Wbuf[:, :, 0:NA], winA_d[l].rearrange("(c p) n -> p c n", p=128), r=[], w=WQ + ["WA", "WB"], key="WbufF")
            load_bc(l, 1)
            hres2 = ["hT2"] + [("GT", c) for c in range(8)]
            hbufs = [(hT, ["hT"]), (GT, hres2)]
            make_hT(cs_d, cs_res, 0, CTX, hbufs[0][0], hbufs[0][1])
            load_bc(l, 0)
            ntile = SEQ // 512
            pcs = phase_A(l, cs_d, cs_res, 0, CTX, 0, False, hbufs[0][0], hbufs[0][1])
            run_interleaved(pcs, (xs_d, xs_res, 0, hbufs[1][0], hbufs[1][1]))
            for j in range(ntile):
                cb_, cr_ = hbufs[(j + 1) % 2]
                pcs = phase_A(l, xs_d, xs_res, j * 512, 512, CTX + j * 512, True, cb_, cr_)
                if j + 1 < ntile:
                    nb_, nr_ = hbufs[j % 2]
                    run_interleaved(pcs, (xs_d, xs_res, (j + 1) * 512, nb_, nr_))
                else:
                    run_interleaved(pcs, None)
            DMA("pool", Wbuf[:, :, :], winB_d[l].rearrange("(c p) n -> p c n", p=128), r=[], w=WQ + ["WA", "WB"], key="WbufF")
            if not last:
                load_bc(l, 1)
                phase_B(l, cs_d, cs_res, ctx1_d, "ctx1D", 0, CTX, 0, False, CTX // 128, False)
                load_bc(l, 0)
            for j in range(nblk):
                hoist = None
                if l == 0 and hoist_l1 and j == nblk - 1:
                    hoist = {}
                    d_at = [2, 8, 30, 40, 74, 90]
                    c_at = [16, 28, 72, 86, 140, 154]
                    for jj in range(6):
                        hoist.setdefault(d_at[jj], []).append(lambda jj=jj: mod_chunk(1, jj, True, "dma"))
                        hoist.setdefault(c_at[jj], []).append(lambda jj=jj: mod_chunk(1, jj, True, "compute"))
                    hoist[165] = [lambda: DMA("pool", Wbuf[:, :, 0:NA], winA_d[1].rearrange("(c p) n -> p c n", p=128),
                                              r=[], w=WQ + ["WA", "WB"], key="WbufF")]
                phase_B(l, xs_d, xs_res, xo_d, xo_res, j * 512, 512, CTX + j * 512, True, NKT, last,
                        have_hT=(j > 0), next_row0=((j + 1) * 512 if j + 1 < nblk else None), hoist=hoist)
                assert not hoist
        if dump:
            dd = {}
            for nm, t, shp, dt in (("d_y", yT, [128, 8, 512], BF16), ("d_QA", QA, [128, 6, 512], BF16), ("d_QR", QRp, [128, 6, 512], BF16),
                                   ("d_QG", QG, [128, 3, 512], BF16), ("d_GT", GT, [128, 8, 512], BF16), ("d_KC", KC, [128, NKEY], BF16),
                                   ("d_KR", KR, [128, NKEY], BF16), ("d_KG", KG, [128, NKEY], BF16), ("d_VM", VM, [128, NKT, 6, 65], BF16),
                                   ("d_VG", VG, [128, NKT, 2, 65], BF16), ("d_Wc", Wc, [128, 2, 6, 128], BF16), ("d_cqn", cqn, [128, 2, 512], BF16)):
                dd[nm] = nc.dram_tensor(nm, shp, dt, kind="ExternalOutput").ap()
                allres = list(S.writers.keys())
                DMA("sp", dd[nm], t[:], r=allres, w=["outD"], key="dump_" + nm)
        S.op("sp", None, r=["outD", "x1D", "ctx1D"])
        S.emit()
    return nc


def _perm_blocks(n, blk):
    idx = np.arange(n).reshape(-1, 2, blk)
    return idx[:, ::-1, :].reshape(-1)


def _rope_tables():
    t = np.arange(SEQ)
    row = (t // 64).astype(np.float64)
    col = (t % 64).astype(np.float64)

    def tab(rdim):
        half = rdim // 2
        freqs = 10000.0 ** (-np.arange(half, dtype=np.float64) / half)
        cos = np.zeros((2 * rdim, SEQ)); sin = np.zeros((2 * rdim, SEQ))
        for a, pos in enumerate((row, col)):
            ang = pos[None, :] * freqs[:, None].astype(np.float32).astype(np.float64)
            ang = (pos.astype(np.float32)[None, :] * freqs.astype(np.float32)[:, None]).astype(np.float32)
            c = np.cos(ang); s = np.sin(ang)
            base = a * rdim
            cos[base:base + half] = c; cos[base + half:base + rdim] = c
            sin[base:base + half] = -s; sin[base + half:base + rdim] = s
        return cos.astype(np.float32), sin.astype(np.float32)

    cg, sg = tab(32)
    cm, sm = tab(16)
    tabG = np.stack([np.tile(cg, (2, 1)), np.tile(sg, (2, 1))]).astype(np.float32)
    tabM = np.stack([np.tile(cm, (3, 1)), np.tile(sm, (3, 1))]).astype(np.float32)
    return np.ascontiguousarray(tabG), np.ascontiguousarray(tabM)


def _prep_shared(norm_w, w_mod, b_mod, w_in, mla_q_norm, mla_w_uq, mla_kv_norm, mla_w_ukv, gqa_q_norm, gqa_k_norm,
                 conv_dw_w, conv_dw_b, conv_ln_w, conv_ln_b, conv_pw_w, conv_pw_b, w_out, final_norm_w):
    f = lambda a: np.ascontiguousarray(np.asarray(a, dtype=np.float32))
    w_in = f(w_in)
    p64 = _perm_blocks(64, 16)
    p32 = _perm_blocks(32, 8)
    kr = 384 + np.arange(32)
    krp = 384 + p32
    gk = 416 + np.arange(128)
    gkp = 416 + np.concatenate([p64, 64 + p64])
    colsA = np.concatenate([256 + np.arange(128), np.tile(kr, 3), np.tile(krp, 3), gk, gkp, 544 + np.arange(128), 1056 + np.arange(512)])
    assert colsA.size == NA
    gq = []
    gqp = []
    for c in range(3):
        for hq in (c, c + 3):
            gq.append(672 + hq * 64 + np.arange(64))
            gqp.append(672 + hq * 64 + p64)
    colsB = np.concatenate([np.arange(256)] + gq + gqp + [1568 + np.arange(1024)])
    assert colsB.size == NB
    w_inA = np.ascontiguousarray(w_in[:, :, colsA])
    w_inB = np.ascontiguousarray(w_in[:, :, colsB])
    wuq = f(mla_w_uq).reshape(2, 256, 6, 96)
    wuqnT = np.ascontiguousarray(wuq[:, :, :, 0:64].transpose(0, 2, 3, 1))
    wukv = f(mla_w_ukv).reshape(2, 128, 6, 128)
    wukT = np.ascontiguousarray(wukv[:, :, :, 0:64].transpose(0, 2, 3, 1))
    wuv = np.ascontiguousarray(wukv[:, :, :, 64:128].reshape(2, 128, 384))
    rope_o = wuq[:, :, :, 64:96]
    rope_p = rope_o[:, :, :, p32]
    wqr = np.stack([rope_o[:, :, 0:3].reshape(2, 256, 96), rope_o[:, :, 3:6].reshape(2, 256, 96),
                    rope_p[:, :, 0:3].reshape(2, 256, 96), rope_p[:, :, 3:6].reshape(2, 256, 96)], axis=2)
    colvec = np.zeros((2, 128, NCV), np.float32)
    qn = f(mla_q_norm)
    colvec[:, :, 0] = qn[:, 0:128]; colvec[:, :, 1] = qn[:, 128:256]
    colvec[:, :, 2] = f(mla_kv_norm)
    gqn = f(gqa_q_norm); gkn = f(gqa_k_norm)
    colvec[:, :, 3] = np.tile(gqn, (1, 2)); colvec[:, :, 4] = np.tile(gqn[:, p64], (1, 2))
    colvec[:, :, 5] = np.tile(gkn, (1, 2)); colvec[:, :, 6] = np.tile(gkn[:, p64], (1, 2))
    for c in range(2):
        sl = slice(c * 128, (c + 1) * 128)
        colvec[:, :, 7 + c] = f(conv_dw_b)[:, sl]
        colvec[:, :, 9 + c] = f(conv_ln_w)[:, sl]
        colvec[:, :, 11 + c] = f(conv_ln_b)[:, sl]
        colvec[:, :, 13 + c] = f(conv_pw_b)[:, sl]
        colvec[:, :, 15 + c * 31: 15 + (c + 1) * 31] = f(conv_dw_w)[:, :, sl].transpose(0, 2, 1)
    tabG, tabM = _rope_tables()
    return {
        "norm_w": f(norm_w), "fnw": f(final_norm_w), "w_mod": f(w_mod), "b_mod": f(b_mod),
        "w_inA": w_inA, "w_inB": w_inB, "w_out": f(w_out), "wuqnT": wuqnT, "wukT": wukT,
        "wqr": np.ascontiguousarray(wqr.astype(np.float32)), "wuv": wuv, "wpw": f(conv_pw_w), "colvec": colvec,
        "tabG": tabG, "tabM": tabM, "ident": np.eye(128, dtype=np.float32),
    }


_NC_CACHE = {}


def kernel(x, c, ctx, c_ctx, norm_w, w_mod, b_mod, w_in, mla_q_norm, mla_w_uq, mla_kv_norm, mla_w_ukv,
           gqa_q_norm, gqa_k_norm, conv_dw_w, conv_dw_b, conv_ln_w, conv_ln_b, conv_pw_w, conv_pw_b, w_out,
           final_norm_w, _debug=None):
    shared = _prep_shared(norm_w, w_mod, b_mod, w_in, mla_q_norm, mla_w_uq, mla_kv_norm, mla_w_ukv, gqa_q_norm,
                          gqa_k_norm, conv_dw_w, conv_dw_b, conv_ln_w, conv_ln_b, conv_pw_w, conv_pw_b, w_out, final_norm_w)
    x = np.asarray(x, dtype=np.float32)
    ctx = np.asarray(ctx, dtype=np.float32)
    c = np.asarray(c, dtype=np.float32)
    c_ctx = np.asarray(c_ctx, dtype=np.float32)
    nb = x.shape[0]
    batches = list(range(nb)) if _debug is None else _debug.get("batches", list(range(nb)))
    in_maps = []
    for b in batches:
        cv = np.stack([c[b], c_ctx], axis=0)
        cvT = np.ascontiguousarray(cv.reshape(2, 8, 128).transpose(2, 1, 0))
        m = dict(shared)
        m["x"] = np.ascontiguousarray(x[b])
        m["ctx"] = np.ascontiguousarray(ctx[b])
        m["cvT"] = cvT
        in_maps.append(m)
    if _debug is None:
        if "main" not in _NC_CACHE:
            _NC_CACHE["main"] = build_program()
        nc = _NC_CACHE["main"]
    else:
        nc = build_program(debug=True, **_debug.get("build", {}))
    res = run_bass_kernel_spmd(nc, in_maps, core_ids=list(range(len(batches))))
    if _debug is not None:
        return [r for r in res.results]
    return np.stack([np.asarray(r["out"]) for r in res.results], axis=0).astype(np.float32)
```

```python
import numpy as np
from contextlib import ExitStack
import concourse.bass as bass
import concourse.mybir as mybir
from concourse.bass_utils import run_bass_kernel_spmd

F32 = mybir.dt.float32
BF16 = mybir.dt.bfloat16
AF = mybir.ActivationFunctionType
ALU = mybir.AluOpType

D = 1024
SEQ = 4096
CTX = 256
NKEY = SEQ + CTX
NKT = NKEY // 128
EPS = 1e-6
NA = 1216
NB = 2048
NCV = 77
A_CKV, A_KR3, A_KR3P, A_GK, A_GKP, A_GV, A_CA, A_CG = 0, 128, 224, 320, 448, 576, 704, 960
B_CQ, B_GQ, B_GQP, B_GATE = 0, 256, 640, 1024


class _Op:
    __slots__ = ("eng", "fn", "deps", "sig", "val", "dkey", "is_dma", "sem")


class Sched:
    ENGS = ("pe", "act", "dve", "pool", "sp")

    def __init__(self, nc, stack):
        self.nc = nc
        self.stack = stack
        self.streams = {e: [] for e in self.ENGS}
        self.writers = {}
        self.readers = {}
        self.dma_cnt = {}
        self.dma_sems = {}
        self.eng_sems = {}
        self.psum_res = set()

    def op(self, eng, fn, r=(), w=(), dma=None):
        o = _Op()
        o.eng = eng; o.fn = fn; o.sig = False; o.val = None
        o.is_dma = dma is not None; o.dkey = dma; o.sem = None
        deps = []

        def add(d, kind):
            if d is o:
                return
            same = (d.eng == eng) and (not d.is_dma) and (not o.is_dma)
            if same:
                if eng == "pe" or kind != "RAW":
                    return
            deps.append(d)

        for res in r:
            ws = self.writers.get(res)
            if ws:
                for d in ws.values():
                    add(d, "RAW")
            if res in self.psum_res:
                rs = self.readers.get(res)
                if rs:
                    for d in rs.values():
                        if d.eng != eng:
                            add(d, "RR")
        for res in w:
            ws = self.writers.get(res)
            if ws:
                for d in ws.values():
                    add(d, "WAW")
            rs = self.readers.get(res)
            if rs:
                for d in rs.values():
                    add(d, "WAR")
        k = ("dma", dma) if o.is_dma else eng
        for res in r:
            self.readers.setdefault(res, {})[k] = o
        for res in w:
            self.writers.setdefault(res, {})[k] = o
        if o.is_dma:
            n = self.dma_cnt.get(dma, 0) + 1
            self.dma_cnt[dma] = n
            o.val = 16 * n
            o.sig = True
        for d in deps:
            d.sig = True
        o.deps = deps
        self.streams[eng].append(o)
        return o

    def emit(self):
        nc = self.nc
        for e in self.ENGS:
            self.eng_sems[e] = self.stack.enter_context(nc.semaphore("es_" + e))
        for dk in self.dma_cnt:
            self.dma_sems[dk] = self.stack.enter_context(nc.semaphore("ds_%d" % len(self.dma_sems)))
        for e in self.ENGS:
            c = 0
            for o in self.streams[e]:
                if o.is_dma:
                    o.sem = self.dma_sems[o.dkey]
                else:
                    o.sem = self.eng_sems[e]
                    if o.sig:
                        c += 1
                        o.val = c
        block = self.stack.enter_context(nc.Block())

        def run(ename):
            def body(eng):
                waited = {}
                for o in self.streams[ename]:
                    need = {}
                    for d in o.deps:
                        if waited.get(d.sem, 0) < d.val and need.get(d.sem, 0) < d.val:
                            need[d.sem] = d.val
                    for sem, val in need.items():
                        eng.wait_ge(sem, val)
                        waited[sem] = val
                    if o.fn is None:
                        assert not o.sig
                        continue
                    ins = o.fn(eng)
                    if o.is_dma:
                        ins.then_inc(o.sem, 16)
                    elif o.sig:
                        ins.then_inc(o.sem, 1)
            return body

        block.tensor(run("pe"))
        block.scalar(run("act"))
        block.vector(run("dve"))
        block.gpsimd(run("pool"))
        block.sync(run("sp"))


def build_program(debug=False, n_layers=2, nblk=SEQ // 512, dump=False):
    nc = bass.Bass("TRN2", target_bir_lowering=False)

    def din(name, shape, dt=F32):
        return nc.dram_tensor(name, list(shape), dt, kind="ExternalInput").ap()

    x_d = din("x", [SEQ, D])
    ctx_d = din("ctx", [CTX, D])
    cvT_d = din("cvT", [128, 8, 2])
    normw_d = din("norm_w", [2, D])
    fnw_d = din("fnw", [D])
    wmod_d = din("w_mod", [2, D, 3 * D])
    bmod_d = din("b_mod", [2, 3 * D])
    winA_d = din("w_inA", [2, D, NA])
    winB_d = din("w_inB", [2, D, NB])
    wout_d = din("w_out", [2, D, D])
    wuqnT_d = din("wuqnT", [2, 6, 64, 256])
    wukT_d = din("wukT", [2, 6, 64, 128])
    wqr_d = din("wqr", [2, 256, 4, 96])
    wuv_d = din("wuv", [2, 128, 384])
    wpw_d = din("wpw", [2, 256, 256])
    colvec_d = din("colvec", [2, 128, NCV])
    tabG_d = din("tabG", [2, 128, SEQ])
    tabM_d = din("tabM", [2, 96, SEQ])
    ident_d = din("ident", [128, 128])
    out_d = nc.dram_tensor("out", [SEQ, D], F32, kind="ExternalOutput").ap()
    ikind = "ExternalOutput" if debug else "Internal"
    x1_d = nc.dram_tensor("x1", [SEQ, D], F32, kind=ikind).ap()
    ctx1_d = nc.dram_tensor("ctx1", [CTX, D], F32, kind=ikind).ap()
    glu_d = nc.dram_tensor("gluD", [2, 128, NKEY], BF16, kind="Internal").ap()
    mod_d = nc.dram_tensor("modD", [2, 2, 3 * D], F32, kind="Internal").ap()

    with ExitStack() as st:
        S = Sched(nc, st)

        def sb(name, shape, dt):
            return st.enter_context(nc.sbuf_tensor("sb_" + name, list(shape), dt))

        def ps(name, shape, dt=F32):
            return st.enter_context(nc.psum_tensor("ps_" + name, list(shape), dt))

        Wbuf = sb("Wbuf", [128, 8, NB], BF16)
        Wout = sb("Wout", [128, 8, D], BF16)
        Wc = sb("Wc", [128, 2, 6, 128], BF16)
        Wqr = sb("Wqr", [128, 2, 4, 96], BF16)
        Wuv = sb("Wuv", [128, 384], BF16)
        Wpw = sb("Wpw", [128, 2, 256], BF16)
        KC = sb("KC", [128, NKEY], BF16)
        KR = sb("KR", [128, NKEY], BF16)
        KG = sb("KG", [128, NKEY], BF16)
        VM = sb("VM", [128, NKT, 6, 65], BF16)
        VG = sb("VG", [128, NKT, 2, 65], BF16)
        gmod_bc = sb("gmod_bc", [128, D], F32)
        shift_bc = sb("shift_bc", [128, D], F32)
        gate_bc = sb("gate_bc", [128, D], F32)
        fnw_bc = sb("fnw_bc", [128, D], F32)
        XT = [sb("xt%d" % i, [128, D], F32) for i in range(2)]
        hb = sb("hb", [128, D], BF16)
        hT = sb("hT", [128, 8, 512], BF16)
        yT = hT
        QA = sb("QA", [128, 6, 512], BF16)
        QRp = sb("QRp", [128, 6, 512], BF16)
        QG = sb("QG", [128, 3, 512], BF16)
        GT = sb("GT", [128, 8, 512], BF16)
        cqn = sb("cqn", [128, 2, 512], BF16)
        PT = [sb("pt%d" % i, [128, 2, 512], BF16) for i in range(2)]
        gluw = sb("gluw", [128, 2, 542], BF16)
        cacc = sb("cacc", [128, 2, 512], F32)
        cbf = sb("cbf", [128, 2, 512], BF16)
        sq = sb("sq", [128, 2, 512], BF16)
        tabGs = sb("tabGs", [128, 2, 512], F32)
        tabMs = sb("tabMs", [128, 2, 512], F32)
        T1 = sb("T1", [128, 512], F32)
        T2 = sb("T2", [128, 512], F32)
        T3 = sb("T3", [128, 512], F32)
        T4 = sb("T4", [128, 512], F32)
        ident = sb("ident", [128, 128], BF16)
        M128 = sb("M128", [128, 128], BF16)
        M256 = sb("M256", [128, 128], BF16)
        M64 = sb("M64", [128, 128], BF16)
        ones32 = sb("ones32", [128, 128], F32)
        colv = sb("colv", [128, NCV], F32)
        ss = sb("ss", [128, 8], F32)
        cvs = sb("cvs", [128, 8, 2], F32)
        cv32 = sb("cv32", [128, 8, 2], F32)

        S0 = ps("S0", [128, 1024]); S1 = ps("S1", [128, 1024])
        O0 = ps("O0", [128, 512]); O1 = ps("O1", [128, 512])
        G0 = ps("G0", [128, 512]); G1 = ps("G1", [128, 512])
        banks = [(G0[:, :], "G0"), (G1[:, :], "G1"), (O0[:, :], "O0"), (O1[:, :], "O1"),
                 (S0[:, 0:512], "S0a"), (S0[:, 512:1024], "S0b"), (S1[:, 0:512], "S1a"), (S1[:, 512:1024], "S1b")]
        S.psum_res.update(k for _, k in banks)
        bank_ctr = [0]

        def nbank():
            b = banks[bank_ctr[0] % 8]
            bank_ctr[0] += 1
            return b

        def MM(out, lhsT, rhs, start, stop, r, w):
            S.op("pe", lambda e: e.matmul(out, lhsT=lhsT, rhs=rhs, start=start, stop=stop), r=r, w=w)

        def MMG(out, pairs, r, w):
            n = len(pairs)
            for i, (l, rr) in enumerate(pairs):
                MM(out, l, rr, i == 0, i == n - 1, r, w)

        def ACT(out, in_, func, r, w, bias=None, scale=None, accum=None):
            kw = {}
            if bias is not None:
                kw["bias"] = bias
            if scale is not None:
                kw["scale"] = scale
            if accum is not None:
                kw["accum_out"] = accum
            S.op("act", lambda e: e.activation(out=out, in_=in_, func=func, **kw), r=r, w=w)

        def TT(eng, out, in0, in1, op, r, w):
            S.op(eng, lambda e: e.tensor_tensor(out=out, in0=in0, in1=in1, op=op), r=r, w=w)

        def STT(eng, out, in0, scalar, in1, op0, op1, r, w):
            S.op(eng, lambda e: e.scalar_tensor_tensor(out=out, in0=in0, scalar=scalar, in1=in1, op0=op0, op1=op1), r=r, w=w)

        def TS(eng, out, in0, s1, s2, op0, op1, r, w):
            if s2 is None:
                S.op(eng, lambda e: e.tensor_scalar(out=out, in0=in0, scalar1=s1, scalar2=None, op0=op0), r=r, w=w)
            else:
                S.op(eng, lambda e: e.tensor_scalar(out=out, in0=in0, scalar1=s1, scalar2=s2, op0=op0, op1=op1), r=r, w=w)

        def CP(eng, out, in_, r, w):
            if eng == "act":
                S.op("act", lambda e: e.copy(out=out, in_=in_), r=r, w=w)
            else:
                S.op(eng, lambda e: e.tensor_copy(out=out, in_=in_), r=r, w=w)

        def TR(out, in_, r, w):
            S.op("pe", lambda e: e.transpose(out, in_, ident[:]), r=r, w=w)

        def RECIP(out, in_, r, w):
            S.op("dve", lambda e: e.reciprocal(out=out, in_=in_), r=r, w=w)

        def _l(x):
            return x if isinstance(x, list) else [x]

        def MSET(eng, ap, val, w):
            S.op(eng, lambda e: e.memset(ap, val), w=w)

        def DMA(q, out, in_, r, w, key):
            S.op(q, lambda e: e.dma_start(out=out, in_=in_), r=r, w=w, dma=key)

        def rstd_from_mean(out, in_, r, w, scale=1.0):
            ACT(out, in_, AF.Ln, r=r, w=w, bias=EPS, scale=scale)
            ACT(out, out, AF.Exp, r=w, w=w, scale=-0.5)

        DMA("pool", ident[:], ident_d, r=[], w=["ident"], key="cst")
        MSET("dve", M128[:], 1.0 / 128, ["M128"])
        MSET("dve", M256[:], 1.0 / 256, ["M256"])
        MSET("dve", M64[:], 0.0, ["M64"])
        MSET("dve", M64[0:64, 0:64], 1.0 / 64, ["M64"])
        MSET("dve", M64[64:128, 64:128], 1.0 / 64, ["M64"])
        MSET("dve", ones32[:], 1.0, ["ones32"])
        MSET("pool", QRp[:], 0.0, ["QR"])
        MSET("pool", VM[:], 1.0, [("VM", i) for i in range(NKT)])
        MSET("pool", VG[:], 1.0, [("VG", i) for i in range(NKT)])
        DMA("sp", fnw_bc[:], fnw_d.partition_broadcast(128), r=[], w=["fnw_bc"], key="cst2")
        DMA("sp", cv32[:], cvT_d, r=[], w=["cv32"], key="cst3")
        ACT(cvs[:], cv32[:], AF.Silu, r=["cv32"], w=["cvs"])

        WQ = ["Wq0", "Wq1", "Wq2", "Wq3"]
        Wf = Wbuf[:].bitcast(F32)

        hoist_l1 = (n_layers == 2 and nblk == SEQ // 512)

        def mod_chunk(l, j, hoisted, part="all"):
            wm = wmod_d[l].rearrange("(c p) n -> p c n", p=128)
            hf = j % 2
            cs = slice(j * 512, (j + 1) * 512)
            wkeys = [WQ[2 * hf], WQ[2 * hf + 1]]
            if hoisted:
                a1, k1 = XT[0][0:2, 0:512], ("xt", 0)
                a2, k2 = XT[0][0:2, 512:1024], ("xt", 0)
                a3, k3 = XT[1][0:2, 0:512], ("xt", 1)
                bk, bkey = G1[:, :], "G1"
                wdma = wkeys + ["WA", "WB"]
            else:
                a1, k1 = T1[0:2, :], "T1"
                a2, k2 = T2[0:2, :], "T2"
                a3, k3 = T3[0:2, :], "T3"
                bk, bkey = nbank()
                wdma = wkeys
            if part in ("all", "dma"):
                DMA("sp", Wf[:, :, hf * 512:(hf + 1) * 512], wm[:, :, cs], r=[], w=wdma, key="Wm%d" % hf)
                if part == "dma":
                    return
            DMA("sp", a2, bmod_d[l, cs].partition_broadcast(2), r=[], w=[k2], key="bm")
            MMG(bk[0:2, :], [(cvs[:, k, :], Wf[:, k, hf * 512:(hf + 1) * 512]) for k in range(8)],
                r=["cvs"] + wkeys, w=[bkey])
            TT("dve", a1, bk[0:2, :], a2, ALU.add, r=[bkey, k2], w=[k1])
            if j in (2, 3):
                DMA("sp", a3, normw_d[l, (j - 2) * 512:(j - 1) * 512].partition_broadcast(2), r=[], w=[k3], key="nwc")
                TS("dve", a1, a1, 1.0, None, ALU.add, None, r=[k1], w=[k1])
                TT("dve", a1, a1, a3, ALU.mult, r=[k1, k3], w=[k1])
            DMA("sp", mod_d[l, :, cs], a1, r=[k1], w=["modD"], key="modst")

        for l in range(2):
            if l == 1 and hoist_l1:
                continue
            for j in range(6):
                mod_chunk(l, j, False)

        def load_bc(l, v):
            DMA("sp", shift_bc[:], mod_d[l, v, 0:D].partition_broadcast(128), r=["modD"], w=["shift_bc"], key="bc0")
            DMA("sp", gmod_bc[:], mod_d[l, v, D:2 * D].partition_broadcast(128), r=["modD"], w=["gmod_bc"], key="bc1")
            DMA("sp", gate_bc[:], mod_d[l, v, 2 * D:3 * D].partition_broadcast(128), r=["modD"], w=["gate_bc"], key="bc2")

        xt_ctr = [0]

        HB = {"buf": None, "res": None}

        stat_ctr = [0]
        hb_ctr = [0]
        PT0f = PT[0][:, :, :].rearrange("p a n -> p (a n)")
        PT1f = PT[1][:, :, :].rearrange("p a n -> p (a n)")

        def rms_stats(xt, xr):
            k = stat_ctr[0] % 2
            stat_ctr[0] += 1
            c0 = 4 * k
            ACT(PT0f, xt[:], AF.Square, r=[xr], w=[("pt", 0), ("ss", k, 0)], accum=ss[:, c0:c0 + 1])
            ACT(ss[:, c0 + 1:c0 + 2], ss[:, c0:c0 + 1], AF.Ln, r=[("ss", k, 0)], w=[("ss", k, 1)], bias=EPS, scale=1.0 / D)
            ACT(ss[:, c0 + 2:c0 + 3], ss[:, c0 + 1:c0 + 2], AF.Exp, r=[("ss", k, 1)], w=[("ss", k, 2)], scale=-0.5)
            return ss[:, c0 + 2:c0 + 3], ("ss", k, 2)

        def front_sub(src_d, srcres, row0, s, bank=None):
            i = xt_ctr[0] % 2
            xt_ctr[0] += 1
            xt = XT[i]
            xr = ("xt", i)
            DMA("sp", xt[:], src_d[row0 + s * 128: row0 + (s + 1) * 128, :], r=[srcres], w=[xr], key="xt%d" % i)
            rs, rk = rms_stats(xt, xr)
            STT("dve", xt[:], xt[:], rs, gmod_bc[:], ALU.mult, ALU.mult, r=[xr, rk, "gmod_bc"], w=[xr])
            k = hb_ctr[0] % 2
            hb_ctr[0] += 1
            hbx, hk = (hb[:, :], "hb") if k == 0 else (PT1f, ("pt", 1))
            TT("dve", hbx, xt[:], shift_bc[:], ALU.add, r=[xr, "shift_bc"], w=[hk])
            bk, bkey = bank if bank is not None else nbank()
            bkb = bk.bitcast(BF16)
            for c in range(8):
                TR(bkb[:, c * 128:(c + 1) * 128], hbx[:, c * 128:(c + 1) * 128], r=[hk, "ident"], w=[bkey])
            return bkb, bkey

        def front_chain(src_d, srcres, row0, s):
            i = xt_ctr[0] % 2
            xt_ctr[0] += 1
            xt = XT[i]
            xr = ("xt", i)
            DMA("sp", xt[:], src_d[row0 + s * 128: row0 + (s + 1) * 128, :], r=[srcres], w=[xr], key="xt%d" % i)
            rs, rk = rms_stats(xt, xr)
            STT("dve", xt[:], xt[:], rs, gmod_bc[:], ALU.mult, ALU.mult, r=[xr, rk, "gmod_bc"], w=[xr])
            k = hb_ctr[0] % 2
            hb_ctr[0] += 1
            hbx, hk = (hb[:, :], "hb") if k == 0 else (PT1f, ("pt", 1))
            TT("dve", hbx, xt[:], shift_bc[:], ALU.add, r=[xr, "shift_bc"], w=[hk])
            return hbx, hk

        def front_tr_evac(hbx, hk, hbuf, hres, s):
            bk, bkey = nbank()
            bkb = bk.bitcast(BF16)
            for c in range(8):
                TR(bkb[:, c * 128:(c + 1) * 128], hbx[:, c * 128:(c + 1) * 128], r=[hk, "ident"], w=[bkey])
            evac_sub(hbuf, hres, s, bkb, bkey)

        def evac_sub(hbuf, hres, s, bkb, bkey):
            CP("dve", hbuf[:, :, s * 128:(s + 1) * 128], bkb[:, 0:1024].rearrange("p (c t) -> p c t", c=8), r=[bkey], w=hres)

        def make_hT(src_d, srcres, row0, NT, hbuf=None, hres=None):
            hbuf = hT if hbuf is None else hbuf
            hres = ["hT"] if hres is None else hres
            for s in range(NT // 128):
                bkb, bkey = front_sub(src_d, srcres, row0, s)
                evac_sub(hbuf, hres, s, bkb, bkey)

        def proj(cols, m, NT, wres, hbuf=None, hres=None):
            hbuf = hT if hbuf is None else hbuf
            hres = ["hT"] if hres is None else hres
            bk, bkey = nbank()
            MMG(bk[0:m, 0:NT], [(Wbuf[:, k, cols:cols + m], hbuf[:, k, 0:NT]) for k in range(8)], r=[wres] + hres, w=[bkey])
            return bk, bkey

        def load_tables(t0, NT):
            DMA("sp", tabGs[:, :, 0:NT], tabG_d[:, :, t0:t0 + NT].rearrange("a p n -> p a n"), r=[], w=["tabGs"], key="tabG")
            DMA("sp", tabMs[0:96, :, 0:NT], tabM_d[:, :, t0:t0 + NT].rearrange("a p n -> p a n"), r=[], w=["tabMs", "tabMs1"], key="tabM")

        def head_norm_rope(o_bk, o_key, p_bk, p_key, g_col, gp_col, NT, rope, dst, dst_res, pre_squared=False):
            if not pre_squared:
                ACT(sq[:, 0, 0:NT], o_bk[:, 0:NT], AF.Square, r=[o_key], w=["sq"])
            mb, mkey = nbank()
            MM(mb[:, 0:NT], M64[:], sq[:, 0, 0:NT], True, True, r=["M64", "sq"], w=[mkey])
            rstd_from_mean(T3[:, 0:NT], mb[:, 0:NT], r=[mkey], w=["T3"])
            if rope:
                STT("dve", T1[:, 0:NT], o_bk[:, 0:NT], colv[:, g_col:g_col + 1], tabGs[:, 0, 0:NT], ALU.mult, ALU.mult,
                    r=[o_key, "colv", "tabGs"], w=["T1"])
                STT("dve", T2[:, 0:NT], p_bk[:, 0:NT], colv[:, gp_col:gp_col + 1], tabGs[:, 1, 0:NT], ALU.mult, ALU.mult,
                    r=[p_key, "colv", "tabGs"], w=["T2"])
                TT("dve", T1[:, 0:NT], T1[:, 0:NT], T2[:, 0:NT], ALU.add, r=["T1", "T2"], w=["T1"])
                TT("dve", dst, T1[:, 0:NT], T3[:, 0:NT], ALU.mult, r=["T1", "T3"], w=_l(dst_res))
            else:
                STT("dve", dst, o_bk[:, 0:NT], colv[:, g_col:g_col + 1], T3[:, 0:NT], ALU.mult, ALU.mult,
                    r=[o_key, "colv", "T3"], w=_l(dst_res))

        def rope96(o_bk, o_key, p_bk, p_key, NT, rope, dst, dst_res):
            if rope:
                TT("dve", T1[0:96, 0:NT], o_bk[0:96, 0:NT], tabMs[0:96, 0, 0:NT], ALU.mult, r=[o_key, "tabMs"], w=["T1"])
                TT("dve", T2[0:96, 0:NT], p_bk[0:96, 0:NT], tabMs[0:96, 1, 0:NT], ALU.mult, r=[p_key, "tabMs", "tabMs1"], w=["T2"])
                if isinstance(dst, list):
                    for jj in range(3):
                        TT("dve", dst[jj], T1[32 * jj:32 * jj + 32, 0:NT], T2[32 * jj:32 * jj + 32, 0:NT], ALU.add, r=["T1", "T2"], w=_l(dst_res))
                else:
                    TT("dve", dst, T1[0:96, 0:NT], T2[0:96, 0:NT], ALU.add, r=["T1", "T2"], w=_l(dst_res))
            else:
                if isinstance(dst, list):
                    for jj in range(3):
                        CP("dve", dst[jj], o_bk[32 * jj:32 * jj + 32, 0:NT], r=[o_key], w=_l(dst_res))
                else:
                    CP("dve", dst, o_bk[0:96, 0:NT], r=[o_key], w=_l(dst_res))

        def phase_A(l, src_d, srcres, row0, NT, key0, rope, hbuf, hres):
            kt0 = key0 // 128
            nsub = NT // 128

            def pj(cols, m, NT, wres):
                return proj(cols, m, NT, wres, hbuf, hres)

            st = {}

            def pieceA():
                if rope:
                    load_tables(row0, NT)
                cb, ckey = pj(A_CKV, 128, NT, "WA")
                ACT(sq[:, 0, 0:NT], cb[:, 0:NT], AF.Square, r=[ckey], w=["sq"])
                st["ckv"] = (cb, ckey)

            def pieceB():
                cb, ckey = st["ckv"]
                mb, mkey = nbank()
                MM(mb[:, 0:NT], M128[:], sq[:, 0, 0:NT], True, True, r=["M128", "sq"], w=[mkey])
                rstd_from_mean(T3[:, 0:NT], mb[:, 0:NT], r=[mkey], w=["T3"])
                STT("dve", KC[:, key0:key0 + NT], cb[:, 0:NT], colv[:, 2:3], T3[:, 0:NT], ALU.mult, ALU.mult,
                    r=[ckey, "colv", "T3"], w=[("KC", kt0 + i) for i in range(nsub)])
                kb, kkey = pj(A_KR3, 96, NT, "WA")
                if rope:
                    kpb, kpkey = pj(A_KR3P, 96, NT, "WA")
                else:
                    kpb, kpkey = None, None
                rope96(kb, kkey, kpb, kpkey, NT, rope, KR[0:96, key0:key0 + NT], [("KR", kt0 + i) for i in range(nsub)])
                for s in range(nsub):
                    vb, vkey = nbank()
                    MM(vb[:, 0:384], KC[:, key0 + s * 128: key0 + (s + 1) * 128], Wuv[:], True, True, r=[("KC", kt0 + s), "Wuv"], w=[vkey])
                    CP("act", VM[:, kt0 + s, :, 0:64], vb[:, 0:384].rearrange("p (h d) -> p h d", h=6), r=[vkey], w=[("VM", kt0 + s)])

            def pieceC():
                gb, gkey = pj(A_GK, 128, NT, "WA")
                if rope:
                    gpb, gpkey = pj(A_GKP, 128, NT, "WA")
                else:
                    gpb, gpkey = None, None
                ACT(sq[:, 0, 0:NT], gb[:, 0:NT], AF.Square, r=[gkey], w=["sq"])
                st["gk"] = (gb, gkey, gpb, gpkey)

            def pieceD():
                gb, gkey, gpb, gpkey = st["gk"]
                head_norm_rope(gb, gkey, gpb, gpkey, 5, 6, NT, rope, KG[:, key0:key0 + NT], [("KG", kt0 + i) for i in range(nsub)],
                               pre_squared=True)
                for s in range(nsub):
                    vb, vkey = nbank()
                    MMG(vb[:, 0:128], [(hbuf[:, k, s * 128:(s + 1) * 128], Wbuf[:, k, A_GV:A_GV + 128]) for k in range(8)],
                        r=hres + ["WA"], w=[vkey])
                    CP("act", VG[:, kt0 + s, :, 0:64], vb[:, 0:128].rearrange("p (h d) -> p h d", h=2), r=[vkey], w=[("VG", kt0 + s)])
                for c in range(2):
                    ab, akey = pj(A_CA + c * 128, 128, NT, "WA")
                    gb2, gkey2 = pj(A_CG + c * 128, 128, NT, "WA")
                    ACT(T1[:, 0:NT], gb2[:, 0:NT], AF.Sigmoid, r=[gkey2], w=["T1"])
                    TT("dve", cbf[:, c, 0:NT], ab[:, 0:NT], T1[:, 0:NT], ALU.mult, r=[akey, "T1"], w=["cbf"])
                DMA("pool", glu_d[:, :, key0:key0 + NT].rearrange("c p n -> p c n"), cbf[:, :, 0:NT], r=["cbf"], w=["gluD"], key="glust")

            return [pieceA, pieceB, pieceC, pieceD]

        def run_interleaved(pieces, front):
            for s in range(4):
                h = None
                if front is not None:
                    h = front_chain(front[0], front[1], front[2], s)
                pieces[s]()
                if front is not None:
                    front_tr_evac(h[0], h[1], front[3], front[4], s)

        def phase_B(l, src_d, srcres, dst_d, dstres, row0, NT, key0, rope, nkt, last, have_hT=False, next_row0=None, hoist=None):
            nsub = NT // 128
            if not have_hT:
                make_hT(src_d, srcres, row0, NT)
            if rope:
                load_tables(row0, NT)
            seq0 = 0 if not rope else CTX
            seqn = CTX if not rope else SEQ
            lo = key0 - 15
            hi = key0 + NT + 15
            clo = max(lo, seq0)
            chi = min(hi, seq0 + seqn)
            if clo > lo:
                MSET("pool", gluw[:, :, 0:clo - lo], 0.0, ["gluw"])
            if chi < hi:
                MSET("pool", gluw[:, :, NT + 30 - (hi - chi):NT + 30], 0.0, ["gluw"])
            DMA("sp", gluw[:, :, clo - lo:chi - lo], glu_d[:, :, clo:chi].rearrange("c p n -> p c n"), r=["gluD"], w=["gluw"], key="gluw")
            cqb = [proj(B_CQ + c * 128, 128, NT, "WB") for c in range(2)]
            for c in range(2):
                ACT(sq[:, c, 0:NT], cqb[c][0][:, 0:NT], AF.Square, r=[cqb[c][1]], w=["sq"])
            mb, mkey = nbank()
            MMG(mb[:, 0:NT], [(M256[:], sq[:, c, 0:NT]) for c in range(2)], r=["M256", "sq"], w=[mkey])
            rstd_from_mean(T3[:, 0:NT], mb[:, 0:NT], r=[mkey], w=["T3"])
            for c in range(2):
                STT("dve", cqn[:, c, 0:NT], cqb[c][0][:, 0:NT], colv[:, c:c + 1], T3[:, 0:NT], ALU.mult, ALU.mult,
                    r=[cqb[c][1], "colv", "T3"], w=["cqn"])
            for h in range(6):
                qb, qkey = nbank()
                MMG(qb[:, 0:NT], [(Wc[:, c, h, :], cqn[:, c, 0:NT]) for c in range(2)], r=["Wc", "cqn"], w=[qkey])
                CP("act" if h % 2 else "dve", QA[:, h, 0:NT], qb[:, 0:NT], r=[qkey], w=["QA"])
            for g in range(2):
                ob, okey = nbank()
                MMG(ob[0:96, 0:NT], [(Wqr[:, c, g, :], cqn[:, c, 0:NT]) for c in range(2)], r=["Wqr", "cqn"], w=[okey])
                if rope:
                    pb, pkey = nbank()
                    MMG(pb[0:96, 0:NT], [(Wqr[:, c, 2 + g, :], cqn[:, c, 0:NT]) for c in range(2)], r=["Wqr", "cqn"], w=[pkey])
                else:
                    pb, pkey = None, None
                rope96(ob, okey, pb, pkey, NT, rope, [QRp[32 * jj:32 * jj + 32, 3 * g + jj, 0:NT] for jj in range(3)], "QR")
            if rope:
                slots = [(sq[:, 0, 0:NT], "sq"), (sq[:, 1, 0:NT], "sq"), (cbf[:, 0, 0:NT], "cbf")]
                pr = []
                for c in range(3):
                    ob, okey = proj(B_GQ + c * 128, 128, NT, "WB")
                    pb, pkey = proj(B_GQP + c * 128, 128, NT, "WB")
                    ACT(slots[c][0], ob[:, 0:NT], AF.Square, r=[okey], w=[slots[c][1]])
                    pr.append((ob, okey, pb, pkey))
                for c in range(3):
                    ob, okey, pb, pkey = pr[c]
                    mb, mkey = nbank()
                    MM(mb[:, 0:NT], M64[:], slots[c][0], True, True, r=["M64", slots[c][1]], w=[mkey])
                    rstd_from_mean(T3[:, 0:NT], mb[:, 0:NT], r=[mkey], w=["T3"])
                    STT("dve", T1[:, 0:NT], ob[:, 0:NT], colv[:, 3:4], tabGs[:, 0, 0:NT], ALU.mult, ALU.mult, r=[okey, "colv", "tabGs"], w=["T1"])
                    STT("dve", T2[:, 0:NT], pb[:, 0:NT], colv[:, 4:5], tabGs[:, 1, 0:NT], ALU.mult, ALU.mult, r=[pkey, "colv", "tabGs"], w=["T2"])
                    TT("dve", T1[:, 0:NT], T1[:, 0:NT], T2[:, 0:NT], ALU.add, r=["T1", "T2"], w=["T1"])
                    TT("dve", QG[:, c, 0:NT], T1[:, 0:NT], T3[:, 0:NT], ALU.mult, r=["T1", "T3"], w=["QG"])
            else:
                for c in range(3):
                    ob, okey = proj(B_GQ + c * 128, 128, NT, "WB")
                    head_norm_rope(ob, okey, None, None, 3, 4, NT, rope, QG[:, c, 0:NT], "QG")
            for c in range(8):
                gb, gkey = proj(B_GATE + c * 128, 128, NT, "WB")
                ACT(GT[:, c, 0:NT], gb[:, 0:NT], AF.Silu, r=[gkey], w=[("GT", c)])
            interleave = (NT == 512)
            cchunks = []

            def ch_conv_tap(c, k):
                if k == 0:
                    TS("dve", cacc[:, c, 0:NT], gluw[:, c, 0:NT], colv[:, 15 + c * 31: 16 + c * 31], colv[:, 7 + c:8 + c], ALU.mult, ALU.add,
                       r=["gluw", "colv"], w=[("cacc", c)])
                else:
                    STT("dve", cacc[:, c, 0:NT], gluw[:, c, k:k + NT], colv[:, 15 + c * 31 + k: 16 + c * 31 + k], cacc[:, c, 0:NT],
                        ALU.mult, ALU.add, r=["gluw", "colv", ("cacc", c)], w=[("cacc", c)])

            X1 = tabGs[:, 0, 0:NT]; X2 = tabGs[:, 1, 0:NT]; X3 = tabMs[:, 0, 0:NT]; X4 = tabMs[:, 1, 0:NT]

            def cbank():
                return (G1[:, :], "G1") if interleave else nbank()

            def ch_cs_cast(c):
                CP("dve", cbf[:, c, 0:NT], cacc[:, c, 0:NT], r=[("cacc", c)], w=["cbf"])
                TT("dve", sq[:, c, 0:NT], cacc[:, c, 0:NT], cacc[:, c, 0:NT], ALU.mult, r=[("cacc", c)], w=["sq"])

            def ch_cs_mean():
                bk, bkey = cbank()
                MMG(bk[:, 0:NT], [(M256[:], cbf[:, c, 0:NT]) for c in range(2)], r=["M256", "cbf"], w=[bkey])
                CP("dve", X1, bk[:, 0:NT], r=[bkey], w=["tabGs"])
                TT("dve", X2, X1, X1, ALU.mult, r=["tabGs"], w=["tabGs"])

            def ch_cs_var():
                bk, bkey = cbank()
                MMG(bk[:, 0:NT], [(M256[:], sq[:, c, 0:NT]) for c in range(2)], r=["M256", "sq"], w=[bkey])
                TT("dve", X2, bk[:, 0:NT], X2, ALU.subtract, r=[bkey, "tabGs"], w=["tabGs"])
                TS("dve", X2, X2, 0.0, None, ALU.max, None, r=["tabGs"], w=["tabGs"])

            def ch_cs_rstd1():
                ACT(X3, X2, AF.Ln, r=["tabGs"], w=["tabMs"], bias=EPS)

            def ch_cs_rstd2():
                ACT(X3, X3, AF.Exp, r=["tabMs"], w=["tabMs"], scale=-0.5)

            def ch_cs_norm(c):
                TT("dve", cacc[:, c, 0:NT], cacc[:, c, 0:NT], X1, ALU.subtract, r=[("cacc", c), "tabGs"], w=[("cacc", c)])
                TT("dve", cacc[:, c, 0:NT], cacc[:, c, 0:NT], X3, ALU.mult, r=[("cacc", c), "tabMs"], w=[("cacc", c)])
                TS("dve", cacc[:, c, 0:NT], cacc[:, c, 0:NT], colv[:, 9 + c:10 + c], colv[:, 11 + c:12 + c], ALU.mult, ALU.add,
                   r=[("cacc", c), "colv"], w=[("cacc", c)])

            def ch_cs_silu_a(c):
                ACT(X4, cacc[:, c, 0:NT], AF.Exp, r=[("cacc", c)], w=["tabMs1"], scale=-1.0)

            def ch_cs_silu_b(c):
                TS("dve", X4, X4, 1.0, None, ALU.add, None, r=["tabMs1"], w=["tabMs1"])
                RECIP(X4, X4, r=["tabMs1"], w=["tabMs1"])
                TT("dve", cbf[:, c, 0:NT], cacc[:, c, 0:NT], X4, ALU.mult, r=[("cacc", c), "tabMs1"], w=["cbf"])

            cs_chunks = [lambda: ch_cs_cast(0), lambda: ch_cs_cast(1), ch_cs_mean, ch_cs_var, ch_cs_rstd1, ch_cs_rstd2,
                         lambda: ch_cs_norm(0), lambda: ch_cs_norm(1), lambda: ch_cs_silu_a(0), lambda: ch_cs_silu_b(0),
                         lambda: ch_cs_silu_a(1), lambda: ch_cs_silu_b(1)]

            def ch_conv_pw(co):
                bk, bkey = (G1[:, :], "G1") if interleave else nbank()
                MMG(bk[:, 0:NT], [(Wpw[:, ci, co * 128:(co + 1) * 128], cbf[:, ci, 0:NT]) for ci in range(2)], r=["Wpw", "cbf"], w=[bkey])
                STT("dve", yT[:, 6 + co, 0:NT], bk[:, 0:NT], colv[:, 13 + co:14 + co], GT[:, 6 + co, 0:NT], ALU.add, ALU.mult,
                    r=[bkey, "colv", ("GT", 6 + co)], w=["hT"])

            for k in range(31):
                for c in range(2):
                    cchunks.append(lambda c=c, k=k: ch_conv_tap(c, k))
            n_tap_chunks = len(cchunks)
            cchunks.extend(cs_chunks)
            cchunks.append(lambda: ch_conv_pw(0))
            cchunks.append(lambda: ch_conv_pw(1))
            n_late = len(cchunks) - n_tap_chunks
            if not interleave:
                for f in cchunks:
                    f()
                cchunks = []
            Sbufs = [(S0, ["S0a", "S0b"]), (S1, ["S1a", "S1b"])]
            OB = {"O0": (O0, "O0"), "O1": (O1, "O1"), "G0": (G0, "G0")}
            segs = [("gqa", 0, ("O0", "O1")), ("mla", 0, ("G0",)), ("mla", 1, ("O0",)), ("gqa", 1, ("O1", "G0")),
                    ("mla", 2, ("O0",)), ("mla", 3, ("O1",)), ("gqa", 2, ("G0", "O0")), ("mla", 4, ("O1",)), ("mla", 5, ("G0",))]
            items = []
            for kind, idx, obs in segs:
                n = nkt // 2 if kind == "mla" else nkt
                for j in range(n):
                    items.append((kind, idx, obs, j, n))
            nit = len(items)

            def emit_QK(i):
                kind, idx, obs, j, n = items[i]
                Sb, Skeys = Sbufs[i % 2]
                if kind == "mla":
                    h = idx
                    for half in range(2):
                        kt = 2 * j + half
                        ks = slice(kt * 128, (kt + 1) * 128)
                        so = Sb[:, half * 512: half * 512 + NT]
                        MM(so, KC[:, ks], QA[:, h, 0:NT], True, False, r=[("KC", kt), "QA"], w=[Skeys[half]])
                        MM(so, KR[0:96, ks], QRp[0:96, h, 0:NT], False, True, r=[("KR", kt), "QR"], w=[Skeys[half]])
                else:
                    c = idx
                    kt = j
                    ks = slice(kt * 128, (kt + 1) * 128)
                    for g in range(2):
                        so = Sb[:, g * 512: g * 512 + NT]
                        MM(so, KG[64 * g:64 * g + 64, ks], QG[64 * g:64 * g + 64, c, 0:NT], True, True, r=[("KG", kt), "QG"], w=[Skeys[g]])

            def emit_EXP(i):
                kind = items[i][0]
                sc = 96.0 ** -0.5 if kind == "mla" else 0.125
                Sb, Skeys = Sbufs[i % 2]
                Pt = PT[i % 2]
                Pkey = ("pt", i % 2)
                if NT == 512:
                    ACT(Pt[:, :, :].rearrange("p a n -> p (a n)"), Sb[:, :], AF.Exp, r=Skeys, w=[Pkey], scale=sc)
                else:
                    ACT(Pt[:, :, 0:NT], Sb[:, :].rearrange("p (a n) -> p a n", a=2)[:, :, 0:NT], AF.Exp, r=Skeys, w=[Pkey], scale=sc)

            def emit_PV(i):
                kind, idx, obs, j, n = items[i]
                Pt = PT[i % 2]
                Pkey = ("pt", i % 2)
                if kind == "mla":
                    Ob, Okey = OB[obs[0]]
                    for half in range(2):
                        kt = 2 * j + half
                        MM(Ob[0:65, 0:NT], VM[:, kt, idx, :], Pt[:, half, 0:NT], j == 0 and half == 0, j == n - 1 and half == 1,
                           r=[("VM", kt), Pkey], w=[Okey])
                else:
                    kt = j
                    for g in range(2):
                        Ob, Okey = OB[obs[g]]
                        MM(Ob[0:65, 0:NT], VG[:, kt, g, :], Pt[:, g, 0:NT], j == 0, j == n - 1, r=[("VG", kt), Pkey], w=[Okey])

            post_ctr = [0]

            def post1(obname, chunk, R):
                Ob, Okey = OB[obname]
                pc = post_ctr[0] % 2
                post_ctr[0] += 1
                Tr, trk = (T4, "T4r") if pc == 0 else (T3, "T3")
                Tx, txk = (T2, "T2") if pc == 0 else (T1, "T1")
                RECIP(Tr[64:65, 0:NT], Ob[64:65, 0:NT], r=[Okey], w=[trk])
                CP("dve", Tx[R:R + 64, 0:NT], Ob[0:64, 0:NT], r=[Okey], w=[txk])
                TT("dve", Tx[R:R + 64, 0:NT], Tx[R:R + 64, 0:NT], GT[R:R + 64, chunk, 0:NT], ALU.mult, r=[txk, ("GT", chunk)], w=[txk])
                return (Tr, trk, Tx, txk, chunk, R)

            def post2(st):
                Tr, trk, Tx, txk, chunk, R = st
                MM(G1[:, 0:NT], ones32[64:65, :], Tr[64:65, 0:NT], True, True, r=["ones32", trk], w=["G1"])
                TT("dve", yT[R:R + 64, chunk, 0:NT], Tx[R:R + 64, 0:NT], G1[R:R + 64, 0:NT], ALU.mult, r=[txk, "G1"], w=["hT"])

            pending = {}
            emit_QK(0)
            for i in range(nit):
                if hoist:
                    for hf_ in hoist.pop(i, []):
                        hf_()
                if cchunks and i >= 4:
                    k_late = n_late - len(cchunks)
                    if k_late < 0 or i >= 100 + 3 * k_late:
                        cchunks.pop(0)()
                if i + 1 < nit:
                    emit_QK(i + 1)
                emit_EXP(i)
                emit_PV(i)
                kind, idx, obs, j, n = items[i]
                if j == n - 1:
                    if kind == "mla":
                        heads = [(obs[0], idx // 2, 64 * (idx % 2))]
                    else:
                        heads = [(obs[0], 3 + idx // 2, 64 * (idx % 2)), (obs[1], 3 + (idx + 3) // 2, 64 * ((idx + 3) % 2))]
                    for k2, (obn, chunk, R) in enumerate(heads):
                        pending.setdefault(min(i + 8 + k2, nit - 1), []).append(post1(obn, chunk, R))
                for stt in pending.pop(i, []):
                    post2(stt)
            while cchunks:
                cchunks.pop(0)()
            def post_sub(s, obanks):
                i = xt_ctr[0] % 2
                xt_ctr[0] += 1
                xt = XT[i]
                xr = ("xt", i)
                DMA("sp", xt[:], src_d[row0 + s * 128: row0 + (s + 1) * 128, :], r=[srcres], w=[xr], key="xt%d" % i)
                for f in range(2):
                    ob, okey = obanks[f] if obanks is not None else nbank()
                    Tt, tk = (T1, "T1") if f == 0 else (T2, "T2")
                    MMG(ob[:, :], [(yT[:, c, s * 128:(s + 1) * 128], Wout[:, c, f * 512:(f + 1) * 512]) for c in range(8)],
                        r=["hT", "Wout"], w=[okey])
                    TT("dve", Tt[:, :], ob[:, :], gate_bc[:, f * 512:(f + 1) * 512], ALU.mult, r=[okey, "gate_bc"], w=[tk])
                    TT("dve", xt[:, f * 512:(f + 1) * 512], Tt[:, :], xt[:, f * 512:(f + 1) * 512], ALU.add, r=[tk, xr], w=[xr])
                if last:
                    rs, rk = rms_stats(xt, xr)
                    STT("dve", xt[:], xt[:], rs, fnw_bc[:], ALU.mult, ALU.mult, r=[xr, rk, "fnw_bc"], w=[xr])
                DMA("pool", dst_d[row0 + s * 128: row0 + (s + 1) * 128, :], xt[:], r=[xr], w=[dstres], key="xst%d" % i)

            if next_row0 is None:
                for s in range(nsub):
                    post_sub(s, None)
            else:
                fbanks = [(S0[:, 0:512], "S0a"), (S0[:, 512:1024], "S0b"), (S1[:, 0:512], "S1a"), (S1[:, 512:1024], "S1b")]
                ob4 = [(O0[:, :], "O0"), (O1[:, :], "O1"), (G0[:, :], "G0"), (G1[:, :], "G1")]
                pend = []
                for s in range(nsub):
                    if last:
                        post_sub(s, (ob4[(2 * s) % 4], ob4[(2 * s + 1) % 4]))
                        pend.append(front_sub(src_d, srcres, next_row0, s, bank=fbanks[s]))
                        continue
                    hbx, hk = front_chain(src_d, srcres, next_row0, s)
                    post_sub(s, (ob4[(2 * s) % 4], ob4[(2 * s + 1) % 4]))
                    bk, bkey = fbanks[s]
                    bkb = bk.bitcast(BF16)
                    for c in range(8):
                        TR(bkb[:, c * 128:(c + 1) * 128], hbx[:, c * 128:(c + 1) * 128], r=[hk, "ident"], w=[bkey])
                    pend.append((bkb, bkey))
                for s in range(nsub):
                    evac_sub(hT, ["hT"], s, pend[s][0], pend[s][1])

        for l in range(n_layers):
            last = l == 1
            xs_d, xs_res = (x_d, "xD") if l == 0 else (x1_d, "x1D")
            xo_d, xo_res = (x1_d, "x1D") if l == 0 else (out_d, "outD")
            cs_d, cs_res = (ctx_d, "ctxD") if l == 0 else (ctx1_d, "ctx1D")
            DMA("sp", colv[:], colvec_d[l], r=[], w=["colv"], key="colv")
            DMA("pool", Wout[:], wout_d[l].rearrange("(c p) n -> p c n", p=128), r=[], w=["Wout"], key="Wout")
            DMA("pool", Wqr[:], wqr_d[l].rearrange("(c p) g n -> p c g n", p=128), r=[], w=["Wqr"], key="wsm0")
            DMA("pool", Wuv[:], wuv_d[l], r=[], w=["Wuv"], key="wsm1")
            DMA("pool", Wpw[:], wpw_d[l].rearrange("(c p) n -> p c n", p=128), r=[], w=["Wpw"], key="wsm2")
            for h in range(6):
                DMA("sp", T1[0:64, 0:256], wuqnT_d[l, h], r=[], w=["T1"], key="wcA")
                DMA("sp", T2[0:64, 0:128], wukT_d[l, h], r=[], w=["T2"], key="wcB")
                for c in range(2):
                    bk, bkey = nbank()
                    MM(bk[:, 0:128], T1[0:64, c * 128:(c + 1) * 128], T2[0:64, 0:128], True, True, r=["T1", "T2"], w=[bkey])
                    CP("dve", Wc[:, c, h, :], bk[:, 0:128], r=[bkey], w=["Wc"])
            if not (l == 1 and hoist_l1):
                DMA("pool", Wbuf[:, :, 0:NA], winA_d[l].rearrange("(c p) n -> p c n", p=128), r=[], w=WQ + ["WA", "WB"], key="WbufF")
            load_bc(l, 1)
            hres2 = ["hT2"] + [("GT", c) for c in range(8)]
            hbufs = [(hT, ["hT"]), (GT, hres2)]
            make_hT(cs_d, cs_res, 0, CTX, hbufs[0][0], hbufs[0][1])
            load_bc(l, 0)
            ntile = SEQ // 512
            pcs = phase_A(l, cs_d, cs_res, 0, CTX, 0, False, hbufs[0][0], hbufs[0][1])
            run_interleaved(pcs, (xs_d, xs_res, 0, hbufs[1][0], hbufs[1][1]))
            for j in range(ntile):
                cb_, cr_ = hbufs[(j + 1) % 2]
                pcs = phase_A(l, xs_d, xs_res, j * 512, 512, CTX + j * 512, True, cb_, cr_)
                if j + 1 < ntile:
                    nb_, nr_ = hbufs[j % 2]
                    run_interleaved(pcs, (xs_d, xs_res, (j + 1) * 512, nb_, nr_))
                else:
                    run_interleaved(pcs, None)
            DMA("pool", Wbuf[:, :, :], winB_d[l].rearrange("(c p) n -> p c n", p=128), r=[], w=WQ + ["WA", "WB"], key="WbufF")
            if not last:
                load_bc(l, 1)
                phase_B(l, cs_d, cs_res, ctx1_d, "ctx1D", 0, CTX, 0, False, CTX // 128, False)
                load_bc(l, 0)
            for j in range(nblk):
                hoist = None
                if l == 0 and hoist_l1 and j == nblk - 1:
                    hoist = {}
                    d_at = [2, 8, 30, 40, 74, 90]
                    c_at = [16, 28, 72, 86, 140, 154]
                    for jj in range(6):
                        hoist.setdefault(d_at[jj], []).append(lambda jj=jj: mod_chunk(1, jj, True, "dma"))
                        hoist.setdefault(c_at[jj], []).append(lambda jj=jj: mod_chunk(1, jj, True, "compute"))
                    hoist[165] = [lambda: DMA("pool", Wbuf[:, :, 0:NA], winA_d[1].rearrange("(c p) n -> p c n", p=128),
                                              r=[], w=WQ + ["WA", "WB"], key="WbufF")]
                phase_B(l, xs_d, xs_res, xo_d, xo_res, j * 512, 512, CTX + j * 512, True, NKT, last,
                        have_hT=(j > 0), next_row0=((j + 1) * 512 if j + 1 < nblk else None), hoist=hoist)
                assert not hoist
        if dump:
            dd = {}
            for nm, t, shp, dt in (("d_y", yT, [128, 8, 512], BF16), ("d_QA", QA, [128, 6, 512], BF16), ("d_QR", QRp, [128, 6, 512], BF16),
                                   ("d_QG", QG, [128, 3, 512], BF16), ("d_GT", GT, [128, 8, 512], BF16), ("d_KC", KC, [128, NKEY], BF16),
                                   ("d_KR", KR, [128, NKEY], BF16), ("d_KG", KG, [128, NKEY], BF16), ("d_VM", VM, [128, NKT, 6, 65], BF16),
                                   ("d_VG", VG, [128, NKT, 2, 65], BF16), ("d_Wc", Wc, [128, 2, 6, 128], BF16), ("d_cqn", cqn, [128, 2, 512], BF16)):
                dd[nm] = nc.dram_tensor(nm, shp, dt, kind="ExternalOutput").ap()
                allres = list(S.writers.keys())
                DMA("sp", dd[nm], t[:], r=allres, w=["outD"], key="dump_" + nm)
        S.op("sp", None, r=["outD", "x1D", "ctx1D"])
        S.emit()
    return nc


def _perm_blocks(n, blk):
    idx = np.arange(n).reshape(-1, 2, blk)
    return idx[:, ::-1, :].reshape(-1)


def _rope_tables():
    t = np.arange(SEQ)
    row = (t // 64).astype(np.float64)
    col = (t % 64).astype(np.float64)

    def tab(rdim):
        half = rdim // 2
        freqs = 10000.0 ** (-np.arange(half, dtype=np.float64) / half)
        cos = np.zeros((2 * rdim, SEQ)); sin = np.zeros((2 * rdim, SEQ))
        for a, pos in enumerate((row, col)):
            ang = pos[None, :] * freqs[:, None].astype(np.float32).astype(np.float64)
            ang = (pos.astype(np.float32)[None, :] * freqs.astype(np.float32)[:, None]).astype(np.float32)
            c = np.cos(ang); s = np.sin(ang)
            base = a * rdim
            cos[base:base + half] = c; cos[base + half:base + rdim] = c
            sin[base:base + half] = -s; sin[base + half:base + rdim] = s
        return cos.astype(np.float32), sin.astype(np.float32)

    cg, sg = tab(32)
    cm, sm = tab(16)
    tabG = np.stack([np.tile(cg, (2, 1)), np.tile(sg, (2, 1))]).astype(np.float32)
    tabM = np.stack([np.tile(cm, (3, 1)), np.tile(sm, (3, 1))]).astype(np.float32)
    return np.ascontiguousarray(tabG), np.ascontiguousarray(tabM)


def _prep_shared(norm_w, w_mod, b_mod, w_in, mla_q_norm, mla_w_uq, mla_kv_norm, mla_w_ukv, gqa_q_norm, gqa_k_norm,
                 conv_dw_w, conv_dw_b, conv_ln_w, conv_ln_b, conv_pw_w, conv_pw_b, w_out, final_norm_w):
    f = lambda a: np.ascontiguousarray(np.asarray(a, dtype=np.float32))
    w_in = f(w_in)
    p64 = _perm_blocks(64, 16)
    p32 = _perm_blocks(32, 8)
    kr = 384 + np.arange(32)
    krp = 384 + p32
    gk = 416 + np.arange(128)
    gkp = 416 + np.concatenate([p64, 64 + p64])
    colsA = np.concatenate([256 + np.arange(128), np.tile(kr, 3), np.tile(krp, 3), gk, gkp, 544 + np.arange(128), 1056 + np.arange(512)])
    assert colsA.size == NA
    gq = []
    gqp = []
    for c in range(3):
        for hq in (c, c + 3):
            gq.append(672 + hq * 64 + np.arange(64))
            gqp.append(672 + hq * 64 + p64)
    colsB = np.concatenate([np.arange(256)] + gq + gqp + [1568 + np.arange(1024)])
    assert colsB.size == NB
    w_inA = np.ascontiguousarray(w_in[:, :, colsA])
    w_inB = np.ascontiguousarray(w_in[:, :, colsB])
    wuq = f(mla_w_uq).reshape(2, 256, 6, 96)
    wuqnT = np.ascontiguousarray(wuq[:, :, :, 0:64].transpose(0, 2, 3, 1))
    wukv = f(mla_w_ukv).reshape(2, 128, 6, 128)
    wukT = np.ascontiguousarray(wukv[:, :, :, 0:64].transpose(0, 2, 3, 1))
    wuv = np.ascontiguousarray(wukv[:, :, :, 64:128].reshape(2, 128, 384))
    rope_o = wuq[:, :, :, 64:96]
    rope_p = rope_o[:, :, :, p32]
    wqr = np.stack([rope_o[:, :, 0:3].reshape(2, 256, 96), rope_o[:, :, 3:6].reshape(2, 256, 96),
                    rope_p[:, :, 0:3].reshape(2, 256, 96), rope_p[:, :, 3:6].reshape(2, 256, 96)], axis=2)
    colvec = np.zeros((2, 128, NCV), np.float32)
    qn = f(mla_q_norm)
    colvec[:, :, 0] = qn[:, 0:128]; colvec[:, :, 1] = qn[:, 128:256]
    colvec[:, :, 2] = f(mla_kv_norm)
    gqn = f(gqa_q_norm); gkn = f(gqa_k_norm)
    colvec[:, :, 3] = np.tile(gqn, (1, 2)); colvec[:, :, 4] = np.tile(gqn[:, p64], (1, 2))
    colvec[:, :, 5] = np.tile(gkn, (1, 2)); colvec[:, :, 6] = np.tile(gkn[:, p64], (1, 2))
    for c in range(2):
        sl = slice(c * 128, (c + 1) * 128)
        colvec[:, :, 7 + c] = f(conv_dw_b)[:, sl]
        colvec[:, :, 9 + c] = f(conv_ln_w)[:, sl]
        colvec[:, :, 11 + c] = f(conv_ln_b)[:, sl]
        colvec[:, :, 13 + c] = f(conv_pw_b)[:, sl]
        colvec[:, :, 15 + c * 31: 15 + (c + 1) * 31] = f(conv_dw_w)[:, :, sl].transpose(0, 2, 1)
    tabG, tabM = _rope_tables()
    return {
        "norm_w": f(norm_w), "fnw": f(final_norm_w), "w_mod": f(w_mod), "b_mod": f(b_mod),
        "w_inA": w_inA, "w_inB": w_inB, "w_out": f(w_out), "wuqnT": wuqnT, "wukT": wukT,
        "wqr": np.ascontiguousarray(wqr.astype(np.float32)), "wuv": wuv, "wpw": f(conv_pw_w), "colvec": colvec,
        "tabG": tabG, "tabM": tabM, "ident": np.eye(128, dtype=np.float32),
    }


_NC_CACHE = {}


def kernel(x, c, ctx, c_ctx, norm_w, w_mod, b_mod, w_in, mla_q_norm, mla_w_uq, mla_kv_norm, mla_w_ukv,
           gqa_q_norm, gqa_k_norm, conv_dw_w, conv_dw_b, conv_ln_w, conv_ln_b, conv_pw_w, conv_pw_b, w_out,
           final_norm_w, _debug=None):
    shared = _prep_shared(norm_w, w_mod, b_mod, w_in, mla_q_norm, mla_w_uq, mla_kv_norm, mla_w_ukv, gqa_q_norm,
                          gqa_k_norm, conv_dw_w, conv_dw_b, conv_ln_w, conv_ln_b, conv_pw_w, conv_pw_b, w_out, final_norm_w)
    x = np.asarray(x, dtype=np.float32)
    ctx = np.asarray(ctx, dtype=np.float32)
    c = np.asarray(c, dtype=np.float32)
    c_ctx = np.asarray(c_ctx, dtype=np.float32)
    nb = x.shape[0]
    batches = list(range(nb)) if _debug is None else _debug.get("batches", list(range(nb)))
    in_maps = []
    for b in batches:
        cv = np.stack([c[b], c_ctx], axis=0)
        cvT = np.ascontiguousarray(cv.reshape(2, 8, 128).transpose(2, 1, 0))
        m = dict(shared)
        m["x"] = np.ascontiguousarray(x[b])
        m["ctx"] = np.ascontiguousarray(ctx[b])
        m["cvT"] = cvT
        in_maps.append(m)
    if _debug is None:
        if "main" not in _NC_CACHE:
            _NC_CACHE["main"] = build_program()
        nc = _NC_CACHE["main"]
    else:
        nc = build_program(debug=True, **_debug.get("build", {}))
    res = run_bass_kernel_spmd(nc, in_maps, core_ids=list(range(len(batches))))
    if _debug is not None:
        return [r for r in res.results]
    return np.stack([np.asarray(r["out"]) for r in res.results], axis=0).astype(np.float32)
```
